# Optimizing a Trainium2 kernel written in Bass

```python
import jax, jax.numpy as jnp
from jax import lax
import numpy as np

D_MODEL = 2048
BATCH = 4
SEQ = 8192
DEPTH = 1

HEAD_DIM = 64
N_FOX_HEADS = 16
N_DIL_HEADS = 16
FOX_WIDTH = N_FOX_HEADS * HEAD_DIM
DIL_WIDTH = N_DIL_HEADS * HEAD_DIM
MIX_WIDTH = FOX_WIDTH + DIL_WIDTH
ROPE_THETA = 500000.0
ROPE_DIM = HEAD_DIM // 4
DILATED_PAIRS = ((128, 1), (512, 4), (2048, 16))
Q_BLOCK = 128
PEER_HEADS = 8
PEER_N_KEYS = 128
PEER_N_EXPERTS = PEER_N_KEYS * PEER_N_KEYS
PEER_KEY_DIM = 256
PEER_HALF = PEER_KEY_DIM // 2
PEER_TOPK = 16
PEER_TOKEN_BLOCK = 128
RMS_EPS = 1e-6
IN_COLS = 3 * FOX_WIDTH + N_FOX_HEADS + 3 * DIL_WIDTH

kernel_name = 'hybrid_fox_dilated_peer_block'


def rms_norm(x, g):
    xf = x.astype(jnp.float32)
    y = xf * lax.rsqrt(jnp.mean(xf * xf, axis=-1, keepdims=True) + RMS_EPS)
    return y.astype(x.dtype) * g


def partial_rope(x, pos):
    inv = ROPE_THETA ** (-jnp.arange(0, ROPE_DIM, 2, dtype=jnp.float32) / ROPE_DIM)
    ang = pos.astype(jnp.float32)[:, None] * inv[None, :]
    cos = jnp.cos(ang)[None, :, None, :].astype(x.dtype)
    sin = jnp.sin(ang)[None, :, None, :].astype(x.dtype)
    xr, xp = x[..., :ROPE_DIM], x[..., ROPE_DIM:]
    x1, x2 = xr[..., :ROPE_DIM // 2], xr[..., ROPE_DIM // 2:]
    rot = jnp.concatenate([x1 * cos - x2 * sin, x2 * cos + x1 * sin], axis=-1)
    return jnp.concatenate([rot, xp], axis=-1)


def forgetting_attention(q, k, v, f_logit):
    B, S, H, hd = q.shape
    nb = S // Q_BLOCK
    log_f = jax.nn.log_sigmoid(f_logit.astype(jnp.float32))
    c = jnp.cumsum(log_f, axis=1).transpose(0, 2, 1)
    qh = q.transpose(0, 2, 1, 3) * (hd ** -0.5)
    kh = k.transpose(0, 2, 1, 3)
    vh = v.transpose(0, 2, 1, 3)
    q_blocks = qh.reshape(B, H, nb, Q_BLOCK, hd).transpose(2, 0, 1, 3, 4)
    c_blocks = c.reshape(B, H, nb, Q_BLOCK).transpose(2, 0, 1, 3)
    starts = jnp.arange(nb, dtype=jnp.int32) * Q_BLOCK
    k_pos = jnp.arange(S, dtype=jnp.int32)

    def block(args):
        qb, cb, s0 = args
        logits = jnp.einsum('bhqd,bhkd->bhqk', qb, kh).astype(jnp.float32)
        logits = logits + (cb[..., :, None] - c[..., None, :])
        q_pos = s0 + jnp.arange(Q_BLOCK, dtype=jnp.int32)
        mask = k_pos[None, :] <= q_pos[:, None]
        logits = jnp.where(mask, logits, -jnp.inf)
        p = jax.nn.softmax(logits, axis=-1).astype(vh.dtype)
        return jnp.einsum('bhqk,bhkd->bhqd', p, vh)

    o = lax.map(block, (q_blocks, c_blocks, starts))
    return o.transpose(1, 0, 3, 2, 4).reshape(B, S, H, hd)


def dilated_pattern(q, k, v, window, dilation):
    B, S, H, hd = q.shape
    span = window // dilation
    unit = dilation * Q_BLOCK
    s_pad = -(-S // unit) * unit
    L = s_pad // dilation
    nb = L // Q_BLOCK
    pad = ((0, 0), (0, s_pad - S), (0, 0), (0, 0))

    def to_sub(t):
        t = jnp.pad(t, pad).reshape(B, L, dilation, H, hd)
        return t.transpose(0, 2, 3, 1, 4).reshape(B, dilation, H, nb, Q_BLOCK, hd)

    def with_prev(t):
        prev = jnp.pad(t, ((0, 0), (0, 0), (0, 0), (1, 0), (0, 0), (0, 0)))[:, :, :, :-1]
        return jnp.concatenate([prev, t], axis=-2)

    qs = to_sub(q)
    kk = with_prev(to_sub(k))
    vv = with_prev(to_sub(v))
    logits = jnp.einsum('brhnqd,brhnkd->brhnqk', qs, kk).astype(jnp.float32) * (hd ** -0.5)
    qi = jnp.arange(Q_BLOCK)[:, None]
    kj = jnp.arange(2 * Q_BLOCK)[None, :]
    dist = qi + Q_BLOCK - kj
    band = (dist >= 0) & (dist <= span)
    has_prev = (jnp.arange(nb)[:, None, None] > 0) | (kj[None] >= Q_BLOCK)
    mask = band[None] & has_prev
    logits = jnp.where(mask, logits, -jnp.inf)
    m = jnp.max(logits, axis=-1, keepdims=True)
    e = jnp.exp(logits - m)
    den = jnp.sum(e, axis=-1, keepdims=True)
    o = jnp.einsum('brhnqk,brhnkd->brhnqd', (e / den).astype(v.dtype), vv)
    lse = (m + jnp.log(den))[..., 0]
    o = o.reshape(B, dilation, H, L, hd).transpose(0, 3, 1, 2, 4).reshape(B, s_pad, H, hd)[:, :S]
    lse = lse.reshape(B, dilation, H, L).transpose(0, 3, 1, 2).reshape(B, s_pad, H)[:, :S]
    return o, lse


def dilated_attention(q, k, v):
    outs = []
    lses = []
    for window, dilation in DILATED_PAIRS:
        o, lse = dilated_pattern(q, k, v, window, dilation)
        outs.append(o)
        lses.append(lse)
    wts = jax.nn.softmax(jnp.stack(lses, axis=0), axis=0)
    return jnp.sum(wts[..., None].astype(q.dtype) * jnp.stack(outs, axis=0), axis=0)


def peer(x, w_query, sub_keys, expert_down, expert_up):
    B, S, D = x.shape
    T = B * S
    xt = x.reshape(T, D)
    q = (xt @ w_query).reshape(T, PEER_HEADS, 2, PEER_HALF)
    scores = jnp.einsum('thpc,hpnc->thpn', q, sub_keys).astype(jnp.float32)
    vals, idx = lax.top_k(scores, PEER_TOPK)
    cand = (vals[..., 0, :, None] + vals[..., 1, None, :]).reshape(T, PEER_HEADS, PEER_TOPK * PEER_TOPK)
    top_vals, top_c = lax.top_k(cand, PEER_TOPK)
    i1 = jnp.take_along_axis(idx[..., 0, :], top_c // PEER_TOPK, axis=-1)
    i2 = jnp.take_along_axis(idx[..., 1, :], top_c % PEER_TOPK, axis=-1)
    experts = (i1 * PEER_N_KEYS + i2).reshape(T, PEER_HEADS * PEER_TOPK)
    gates = jax.nn.softmax(top_vals, axis=-1).reshape(T, PEER_HEADS * PEER_TOPK).astype(x.dtype)
    nc = T // PEER_TOKEN_BLOCK

    def block(args):
        xc, ec, gc = args
        u = jnp.take(expert_down, ec, axis=0)
        hdn = jax.nn.gelu(jnp.einsum('ted,td->te', u, xc))
        vsel = jnp.take(expert_up, ec, axis=0)
        return jnp.einsum('te,ted->td', gc * hdn, vsel)

    y = lax.map(block, (xt.reshape(nc, PEER_TOKEN_BLOCK, D),
                        experts.reshape(nc, PEER_TOKEN_BLOCK, PEER_HEADS * PEER_TOPK),
                        gates.reshape(nc, PEER_TOKEN_BLOCK, PEER_HEADS * PEER_TOPK)))
    return y.reshape(B, S, D)


def setup_inputs(seed: int = 0) -> dict:
    key = jax.random.key(seed)
    ks = jax.random.split(key, 16)
    f32 = jnp.float32
    x = jax.random.normal(ks[0], (BATCH, SEQ, D_MODEL), f32)
    attn_norm_gain = 1.0 + 0.02 * jax.random.normal(ks[1], (DEPTH, D_MODEL), f32)
    w_in = jax.random.normal(ks[2], (DEPTH, D_MODEL, IN_COLS), f32) * D_MODEL ** -0.5
    forget_bias = jax.random.uniform(ks[3], (DEPTH, N_FOX_HEADS), f32, 1.0, 6.0)
    fox_out_gain = 1.0 + 0.02 * jax.random.normal(ks[4], (DEPTH, FOX_WIDTH), f32)
    dil_out_gain = 1.0 + 0.02 * jax.random.normal(ks[5], (DEPTH, DIL_WIDTH), f32)
    w_out = jax.random.normal(ks[6], (DEPTH, MIX_WIDTH, D_MODEL), f32) * MIX_WIDTH ** -0.5
    ffn_norm_gain = 1.0 + 0.02 * jax.random.normal(ks[7], (DEPTH, D_MODEL), f32)
    peer_query = jax.random.normal(ks[8], (DEPTH, D_MODEL, PEER_HEADS * PEER_KEY_DIM), f32) * D_MODEL ** -0.5
    peer_sub_keys = jax.random.normal(ks[9], (DEPTH, PEER_HEADS, 2, PEER_N_KEYS, PEER_HALF), f32) * PEER_HALF ** -0.5
    peer_down = jax.random.normal(ks[10], (DEPTH, PEER_N_EXPERTS, D_MODEL), f32) * D_MODEL ** -0.5
    peer_up = jax.random.normal(ks[11], (DEPTH, PEER_N_EXPERTS, D_MODEL), f32) * (PEER_HEADS * PEER_TOPK) ** -0.5
    final_norm_gain = 1.0 + 0.02 * jax.random.normal(ks[12], (D_MODEL,), f32)
    return {'x': x, 'attn_norm_gain': attn_norm_gain, 'w_in': w_in, 'forget_bias': forget_bias,
            'fox_out_gain': fox_out_gain, 'dil_out_gain': dil_out_gain, 'w_out': w_out,
            'ffn_norm_gain': ffn_norm_gain, 'peer_query': peer_query, 'peer_sub_keys': peer_sub_keys,
            'peer_down': peer_down, 'peer_up': peer_up, 'final_norm_gain': final_norm_gain}


def reference(x, attn_norm_gain, w_in, forget_bias, fox_out_gain, dil_out_gain, w_out,
              ffn_norm_gain, peer_query, peer_sub_keys, peer_down, peer_up, final_norm_gain):
    B, S, _ = x.shape
    pos = jnp.arange(S, dtype=jnp.int32)
    splits = [FOX_WIDTH, 2 * FOX_WIDTH, 3 * FOX_WIDTH, 3 * FOX_WIDTH + N_FOX_HEADS,
              3 * FOX_WIDTH + N_FOX_HEADS + DIL_WIDTH, 3 * FOX_WIDTH + N_FOX_HEADS + 2 * DIL_WIDTH]
    h = x
    for layer in range(DEPTH):
        xn = rms_norm(h, attn_norm_gain[layer])
        proj = xn @ w_in[layer]
        fq, fk, fv, fg, dq, dk, dv = jnp.split(proj, splits, axis=-1)
        heads_f = (B, S, N_FOX_HEADS, HEAD_DIM)
        heads_d = (B, S, N_DIL_HEADS, HEAD_DIM)
        fox_o = forgetting_attention(fq.reshape(heads_f), fk.reshape(heads_f), fv.reshape(heads_f),
                                     fg + forget_bias[layer])
        dq = partial_rope(dq.reshape(heads_d), pos)
        dk = partial_rope(dk.reshape(heads_d), pos)
        dil_o = dilated_attention(dq, dk, dv.reshape(heads_d))
        mixed = jnp.concatenate([rms_norm(fox_o.reshape(B, S, FOX_WIDTH), fox_out_gain[layer]),
                                 rms_norm(dil_o.reshape(B, S, DIL_WIDTH), dil_out_gain[layer])], axis=-1)
        h = h + mixed @ w_out[layer]
        hn = rms_norm(h, ffn_norm_gain[layer])
        h = h + peer(hn, peer_query[layer], peer_sub_keys[layer], peer_down[layer], peer_up[layer])
    return rms_norm(h, final_norm_gain)
```

```python
import numpy as np
from contextlib import ExitStack

import concourse.bass as bass
import concourse.mybir as mybir
from concourse.bass_utils import run_bass_kernel_spmd

F32 = mybir.dt.float32
BF16 = mybir.dt.bfloat16
I32 = mybir.dt.int32
U32 = mybir.dt.uint32
AF = mybir.ActivationFunctionType
ALU = mybir.AluOpType
AX = mybir.AxisListType

HD = 64
NH = 16
RMS_EPS = 1e-6
NEXP = 16384


class Cfg:
    def __init__(self, npb=32, nob=32, D=2048):
        self.NPB = npb
        self.NOB = nob
        self.NCB = npb + nob
        self.D = D
        self.NCH = D // 128
        self.TC = self.NCB * 128
        self.TO = self.NOB * 128
        self.IN_COLS = 3 * 1024 + 16 + 3 * 1024


class Prog:
    ENGS = ("pe", "act", "dve", "pool", "sp")

    def __init__(self, nc, stack):
        self.nc = nc
        self.stack = stack
        self.all_sems = []
        self.dpool = []
        self.nphase = 0
        self.streams = {e: [] for e in self.ENGS}
        self.new_phase()

    def _alloc(self, name):
        h = self.stack.enter_context(self.nc.semaphore(name))
        self.all_sems.append(h)
        return h

    def new_phase(self):
        self.nphase += 1
        self.sem = {e: self._alloc("s%d_%s" % (self.nphase, e)) for e in self.ENGS}
        self.cnt = {e: 0 for e in self.ENGS}
        self.dsem = {}
        self.dnext = 0
        self.known = {e: {} for e in self.ENGS}
        self.last_w = {}
        self.readers = {}

    def finish(self):
        sems = list(self.all_sems)
        with self.nc.Block() as block:
            @block.gpsimd
            def _(e):
                for h in sems:
                    e.sem_clear(h)

    def _semh(self, key):
        if isinstance(key, str):
            return self.sem[key]
        return self.dsem[key[1]][0]

    def capture(self):
        self._cap = []

    def end_capture(self):
        c, self._cap = self._cap, None
        return c

    def replay(self, items):
        for it in items:
            self.op(*it)

    def op(self, eng, fn, reads=(), writes=(), dma=None):
        if getattr(self, "_cap", None) is not None:
            self._cap.append((eng, fn, tuple(reads), tuple(writes), dma))
            return
        deps = {}

        def add(ev):
            if ev is None:
                return
            k, v = ev
            if deps.get(k, 0) < v:
                deps[k] = v

        writes = list(writes)
        if dma is not None:
            writes.append(("dmakey", dma))
        for r in reads:
            add(self.last_w.get(r))
        for w in writes:
            add(self.last_w.get(w))
            for k, v in self.readers.get(w, {}).items():
                add((k, v))
        is_dma = dma is not None
        if is_dma:
            if dma not in self.dsem:
                if self.dnext >= len(self.dpool):
                    self.dpool.append([self._alloc("dsem%d" % len(self.dpool)), 0])
                self.dsem[dma] = self.dpool[self.dnext]
                self.dnext += 1
            self.dsem[dma][1] += 1
            ev = (("dma", dma), 16 * self.dsem[dma][1])
            inc = (self.dsem[dma][0], 16)
        else:
            self.cnt[eng] += 1
            ev = (eng, self.cnt[eng])
            inc = (self.sem[eng], 1)
        waits = []
        kn = self.known[eng]
        for k, v in deps.items():
            if k == eng and eng == "pe" and not is_dma:
                continue
            if kn.get(k, 0) >= v:
                continue
            kn[k] = v
            waits.append((self._semh(k), v))
        for w in writes:
            self.last_w[w] = ev
            self.readers[w] = {}
        for r in reads:
            d = self.readers.setdefault(r, {})
            if d.get(ev[0], 0) < ev[1]:
                d[ev[0]] = ev[1]
        self.streams[eng].append((waits, fn, inc))

    def barrier(self):
        allev = [(e, self.cnt[e]) for e in self.ENGS if self.cnt[e] > 0]
        allev += [(("dma", k), 16 * v[1]) for k, v in self.dsem.items()]
        for eng in self.ENGS:
            waits = []
            kn = self.known[eng]
            for k, v in allev:
                if kn.get(k, 0) >= v:
                    continue
                kn[k] = v
                waits.append((self._semh(k), v))
            if waits:
                self.streams[eng].append((waits, None, None))

    def emit(self):
        nc = self.nc
        streams = self.streams
        with nc.Block() as block:
            def run(name, e):
                for waits, fn, inc in streams[name]:
                    for s, v in waits:
                        e.wait_ge(s, v)
                    if fn is not None:
                        fn(e).then_inc(inc[0], inc[1])

            @block.tensor
            def _(e):
                run("pe", e)

            @block.scalar
            def _(e):
                run("act", e)

            @block.vector
            def _(e):
                run("dve", e)

            @block.gpsimd
            def _(e):
                run("pool", e)

            @block.sync
            def _(e):
                run("sp", e)
        self.streams = {e: [] for e in self.ENGS}


def AP(t, off, pat):
    return bass.AP(t, off, [list(x) for x in pat])


class Ring:
    def __init__(self, nc, stack, name, n, shape, dtype, psum=False):
        mk = nc.psum_tensor if psum else nc.sbuf_tensor
        self.t = [stack.enter_context(mk("%s%d" % (name, i), shape, dtype)) for i in range(n)]
        self.name = name
        self.n = n
        self.i = -1

    def next(self):
        self.i += 1
        s = self.i % self.n
        return self.t[s], (self.name, s)


def build(cfg, upto=99, dbg_out=()):
    nc = bass.Bass("TRN2", target_bir_lowering=False)
    D, NCH, NCB, NOB, NPB, TC, TO = cfg.D, cfg.NCH, cfg.NCB, cfg.NOB, cfg.NPB, cfg.TC, cfg.TO
    NTG = NCB // 4
    OTG0 = NPB // 4

    def din(name, shape, dt=F32):
        return nc.dram_tensor(name, list(shape), dt, kind="ExternalInput")

    def dscr(name, shape, dt):
        return nc.dram_tensor(name, list(shape), dt, kind="ExternalOutput" if name in dbg_out else "Internal")

    x_ctx = din("x_ctx", [TC, D])
    kvalid = din("kvalid", [128, NCB])
    rope_cos = din("rope_cos", [128, NCB * 8])
    rope_sin = din("rope_sin", [128, NCB * 8])
    wcat_in = din("wcat", [128, 20 * 512])
    tri_in = din("tri", [128, 128])
    ident_in = din("ident", [128, 128])
    w_in = din("w_in", [D, cfg.IN_COLS])
    g_attn = din("g_attn", [128, NCH])
    fbias = din("fbias", [16, 1])
    g_mix = din("g_mix", [128, NCH])
    w_out = din("w_out", [D, D])
    g_ffn = din("g_ffn", [128, D])
    w_pq = din("w_pq", [D, 2048])
    skT_in = din("skT", [128, 16 * 128])
    p_down = din("p_down", [NEXP, D])
    p_up = din("p_up", [NEXP, D])
    g_fin = din("g_fin", [128, D])
    out = nc.dram_tensor("out", [TO, D], F32, kind="ExternalOutput")

    xnT = dscr("xnT", [NCH, 128, TC], BF16)
    qfT = dscr("qfT", [NH, 65, TO], BF16)
    kfT = dscr("kfT", [NH, 65, TC], BF16)
    vf = dscr("vf", [128, NCB, NH, 65], BF16)
    qdT = dscr("qdT", [NH * 64, TO], BF16)
    kdT = dscr("kdT", [NH * 64, TC], BF16)
    vd = dscr("vd", [128, NCB, NH, 65], BF16)
    attn_o = dscr("attn_o", [TO, D], F32)
    hbuf = dscr("hbuf", [TO, D], F32)
    dn_b = dscr("dn_b", [NEXP, D], BF16)
    up_b = dscr("up_b", [NEXP, D], BF16)
    dbg = {}

    with ExitStack() as gstack:
        P = Prog(nc, gstack)
        sb = lambda st, name, shape, dt: st.enter_context(nc.sbuf_tensor(name, list(shape), dt))

        ident_f = sb(gstack, "ident_f", [128, 128], F32)
        ident_b = sb(gstack, "ident_b", [128, 128], BF16)
        negc_tm = sb(gstack, "negc_tm", [128, NCB, 16], F32)
        P.op("sp", lambda e: e.dma_start(out=ident_f[:], in_=ident_in.ap()), writes=["ident_f"], dma="c0")
        P.op("dve", lambda e: e.tensor_copy(out=ident_b[:], in_=ident_f[:]), reads=["ident_f"], writes=["ident_b"])

        with ExitStack() as st:
            xr = Ring(nc, st, "p0x", 2, [128, D], F32)
            jr = Ring(nc, st, "p0j", 1, [128, D], BF16)
            xbr = Ring(nc, st, "p0xb", 2, [128, D], BF16)
            tpr = Ring(nc, st, "p0tp", 4, [128, 1024], BF16, psum=True)
            xtr = Ring(nc, st, "p0xt", 2, [128, NCH, 512], BF16)
            ss = sb(st, "p0ss", [128, NCB], F32)
            rs = sb(st, "p0rs", [128, NCB], F32)
            rstd = sb(st, "p0rstd", [128, NCB], F32)
            for i in range(NCB):
                xt, xres = xr.next()
                P.op("sp", lambda e, xt=xt, i=i: e.dma_start(out=xt[:], in_=x_ctx[i * 128:(i + 1) * 128, :]),
                     writes=[xres], dma=("p0x", xres[1]))
                jt, jres = jr.next()
                P.op("act", lambda e, xt=xt, jt=jt, i=i: e.activation(out=jt[:], in_=xt[:], func=AF.Square,
                                                                    accum_out=ss[:, i:i + 1]),
                     reads=[xres], writes=[jres, ("ss", i)])
                P.op("dve", lambda e, i=i: e.tensor_scalar(out=rs[:, i:i + 1], in0=ss[:, i:i + 1], scalar1=1.0 / D,
                                                           scalar2=RMS_EPS, op0=ALU.mult, op1=ALU.add),
                     reads=[("ss", i)], writes=[("rs", i)])
                P.op("act", lambda e, i=i: e.activation(out=rs[:, i:i + 1], in_=rs[:, i:i + 1], func=AF.Sqrt),
                     reads=[("rs", i)], writes=[("rs", i)])
                P.op("dve", lambda e, i=i: e.reciprocal(out=rstd[:, i:i + 1], in_=rs[:, i:i + 1]),
                     reads=[("rs", i)], writes=[("rstd", i)])
                xb, xbres = xbr.next()
                P.op("dve", lambda e, xt=xt, xb=xb, i=i: e.tensor_scalar(out=xb[:], in0=xt[:], scalar1=rstd[:, i:i + 1],
                                                                        scalar2=None, op0=ALU.mult),
                     reads=[xres, ("rstd", i)], writes=[xbres])
                if i % 4 == 0:
                    xtt, xtres = xtr.next()
                for half in range(NCH // 8):
                    tp, tpres = tpr.next()
                    for j in range(8):
                        c = half * 8 + j
                        P.op("pe", lambda e, tp=tp, xb=xb, j=j, c=c: e.transpose(
                            out=tp[:, j * 128:(j + 1) * 128], in_=xb[:, c * 128:(c + 1) * 128], identity=ident_b[:]),
                            reads=[xbres, "ident_b"], writes=[tpres])
                    eng = "act" if half % 2 == 0 else "dve"
                    o_ap = xtt[:, half * 8:(half + 1) * 8, (i % 4) * 128:(i % 4 + 1) * 128]
                    i_ap = tp[:].rearrange("p (c t) -> p c t", c=8)
                    if eng == "act":
                        P.op("act", lambda e, o_ap=o_ap, i_ap=i_ap: e.activation(out=o_ap, in_=i_ap, func=AF.Copy),
                             reads=[tpres], writes=[xtres])
                    else:
                        P.op("dve", lambda e, o_ap=o_ap, i_ap=i_ap: e.tensor_copy(out=o_ap, in_=i_ap),
                             reads=[tpres], writes=[xtres])
                if i % 4 == 3:
                    g = i // 4
                    dst = AP(xnT, g * 512, [[TC, 128], [128 * TC, NCH], [1, 512]])
                    P.op("sp", lambda e, dst=dst, xtt=xtt: e.dma_start(out=dst, in_=xtt[:]),
                         reads=[xtres], writes=[("xnT", g)], dma=("p0st", xtres[1]))
            P.barrier()
            P.emit()
            P.new_phase()

        if upto <= 0:
            return nc

        with ExitStack() as st:
            wst = Ring(nc, st, "p1ws", 3, [128, 1024], F32)
            wgr = Ring(nc, st, "p1wg", 1, [128, NCH, 1024], BF16)
            xgr = Ring(nc, st, "p1xg", 2, [128, NCH, 512], BF16)
            psr = Ring(nc, st, "p1ps", 4, [128, 512], F32, psum=True)
            tpr = Ring(nc, st, "p1tp", 2, [128, 1024], BF16, psum=True)
            obr = Ring(nc, st, "p1ob", 3, [128, 512], BF16)
            vtr = Ring(nc, st, "p1vt", 2, [128, NH, 65], BF16)
            tmr = Ring(nc, st, "p1tm", 2, [128, 1024], F32)
            rbr = Ring(nc, st, "p1rb", 2, [128, 1024], BF16)
            dtr = Ring(nc, st, "p1dt", 2, [128, 8, 512], BF16)
            gat = sb(st, "p1gat", [128, NCH], F32)
            kv_sb = sb(st, "p1kv", [128, NCB], F32)
            cos_sb = sb(st, "p1cos", [128, NCB * 8], F32)
            sin_sb = sb(st, "p1sin", [128, NCB * 8], F32)
            ra = sb(st, "p1ra", [128, NH, 8], F32)
            rb_ = sb(st, "p1rb_", [128, NH, 8], F32)
            fb_sb = sb(st, "p1fb", [16, 1], F32)
            nfb = sb(st, "p1nfb", [16, 1], F32)
            negc_fm = sb(st, "p1negc", [16, TC], F32)
            ones16 = sb(st, "p1ones", [16, 512], F32)
            onesb = sb(st, "p1onesb", [16, 512], BF16)
            spt = sb(st, "p1spt", [16, 512], F32)
            rqr = Ring(nc, st, "p1rq", 2, [16, 512], BF16)
            for (t_, src, key) in ((gat, g_attn, "gat"), (kv_sb, kvalid, "kv"), (cos_sb, rope_cos, "cos"),
                                   (sin_sb, rope_sin, "sin"), (fb_sb, fbias, "fb")):
                P.op("sp", lambda e, t_=t_, src=src: e.dma_start(out=t_[:], in_=src.ap()), writes=[key], dma="c0")
            P.op("dve", lambda e: e.tensor_scalar(out=nfb[:], in0=fb_sb[:], scalar1=-1.0, scalar2=None, op0=ALU.mult),
                 reads=["fb"], writes=["nfb"])
            P.op("pool", lambda e: e.memset(ones16[:], 1.0), writes=["ones16"])
            P.op("pool", lambda e: e.memset(onesb[:], 1.0), writes=["onesb"])

            groups = [("fq", 0, 1024, "fm", OTG0), ("fk", 1024, 1024, "fm", 0), ("fv", 2048, 1024, "tmv", 0),
                      ("fg", 3072, 16, "fg", 0), ("dq", 3088, 1024, "tmr", OTG0), ("dk", 4112, 1024, "tmr", 0),
                      ("dv", 5136, 1024, "tmv", 0)]
            for (gname, col0, ncols, mode, tg0) in groups:
                wg, wgres = wgr.next()
                for c in range(NCH):
                    ws, wsres = wst.next()
                    P.op("sp" if c % 2 == 0 else "act",
                         lambda e, ws=ws, c=c, col0=col0, ncols=ncols: e.dma_start(
                             out=ws[:, 0:ncols], in_=w_in[c * 128:(c + 1) * 128, col0:col0 + ncols]),
                         writes=[wsres], dma=("p1w", wsres[1]))
                    if c % 2 == 0:
                        P.op("act", lambda e, ws=ws, wg=wg, c=c, ncols=ncols: e.activation(
                            out=wg[:, c, 0:ncols], in_=ws[:, 0:ncols], func=AF.Copy, scale=gat[:, c:c + 1]),
                            reads=[wsres, "gat"], writes=[wgres])
                    else:
                        P.op("dve", lambda e, ws=ws, wg=wg, c=c, ncols=ncols: e.tensor_scalar(
                            out=wg[:, c, 0:ncols], in0=ws[:, 0:ncols], scalar1=gat[:, c:c + 1], scalar2=None,
                            op0=ALU.mult),
                            reads=[wsres, "gat"], writes=[wgres])
                for tg in range(tg0, NTG):
                    xg, xgres = xgr.next()
                    src = AP(xnT, tg * 512, [[TC, 128], [128 * TC, NCH], [1, 512]])
                    P.op("sp", lambda e, xg=xg, src=src: e.dma_start(out=xg[:], in_=src),
                         reads=[("xnT", tg)], writes=[xgres], dma=("p1x", xgres[1]))
                    otg = tg - OTG0
                    if gname == "fk":
                        P.op("act", lambda e, tg=tg: e.dma_start(
                            out=AP(kfT, 64 * TC + tg * 512, [[65 * TC, 16], [1, 512]]), in_=onesb[:]),
                            reads=["onesb"], writes=[("kaug", tg)], dma="kaug")
                    if mode == "fm":
                        dstT, TT, toff, scl = (qfT, TO, otg * 512, 0.125) if gname == "fq" else (kfT, TC, tg * 512, 1.0)
                        for sub in range(8):
                            ps, psres = psr.next()
                            for c in range(NCH):
                                P.op("pe", lambda e, ps=ps, wg=wg, xg=xg, c=c, sub=sub: e.matmul(
                                    ps[:], lhsT=wg[:, c, sub * 128:(sub + 1) * 128], rhs=xg[:, c, :],
                                    start=(c == 0), stop=(c == NCH - 1)),
                                    reads=[wgres, xgres], writes=[psres])
                            ob, obres = obr.next()
                            P.op("act", lambda e, ob=ob, ps=ps, scl=scl: e.activation(out=ob[:], in_=ps[:], func=AF.Copy,
                                                                                      scale=scl),
                                 reads=[psres], writes=[obres])
                            for hh in range(2):
                                h = 2 * sub + hh
                                dst = AP(dstT, h * 65 * TT + toff, [[TT, 64], [1, 512]])
                                P.op("sp" if hh == 0 else "act", lambda e, dst=dst, ob=ob, hh=hh: e.dma_start(
                                    out=dst, in_=ob[hh * 64:(hh + 1) * 64, :]),
                                    reads=[obres], writes=[(gname, h, tg)], dma=("p1o", obres[1], hh))
                    elif mode == "fg":
                        ps, psres = psr.next()
                        for c in range(NCH):
                            P.op("pe", lambda e, ps=ps, wg=wg, xg=xg, c=c: e.matmul(
                                ps[0:16, :], lhsT=wg[:, c, 0:16], rhs=xg[:, c, :], start=(c == 0), stop=(c == NCH - 1)),
                                reads=[wgres, xgres], writes=[psres])
                        P.op("act", lambda e, ps=ps: e.activation(out=spt[:], in_=ps[0:16, :], func=AF.Exp, scale=-1.0,
                                                                  bias=nfb[:]),
                             reads=[psres, "nfb"], writes=["spt"])
                        P.op("act", lambda e: e.activation(out=spt[:], in_=spt[:], func=AF.Ln, bias=1.0),
                             reads=["spt"], writes=["spt"])
                        init = 0.0 if tg == 0 else negc_fm[:, tg * 512 - 1:tg * 512]
                        P.op("dve", lambda e, tg=tg, init=init: e.tensor_tensor_scan(
                            out=negc_fm[:, tg * 512:(tg + 1) * 512], data0=ones16[:], data1=spt[:], initial=init,
                            op0=ALU.mult, op1=ALU.add),
                            reads=["spt", "ones16", "negc_fm"], writes=["negc_fm"])
                        if tg >= OTG0:
                            rq, rqres = rqr.next()
                            P.op("dve", lambda e, rq=rq, tg=tg: e.tensor_scalar(
                                out=rq[:], in0=negc_fm[:, tg * 512:(tg + 1) * 512], scalar1=-1.0, scalar2=None,
                                op0=ALU.mult), reads=["negc_fm"], writes=[rqres])
                            P.op("act", lambda e, rq=rq, otg=otg: e.dma_start(
                                out=AP(qfT, 64 * TO + otg * 512, [[65 * TO, 16], [1, 512]]), in_=rq[:]),
                                reads=[rqres], writes=[("qaug", otg)], dma=("qaug", rqres[1]))
                    else:
                        for ti in range(4):
                            blk = tg * 4 + ti
                            if mode == "tmv":
                                vt, vtres = vtr.next()
                            else:
                                tm, tmres = tmr.next()
                            for half in range(2):
                                ps, psres = psr.next()
                                for c in range(NCH):
                                    P.op("pe", lambda e, ps=ps, wg=wg, xg=xg, c=c, ti=ti, half=half: e.matmul(
                                        ps[:], lhsT=xg[:, c, ti * 128:(ti + 1) * 128],
                                        rhs=wg[:, c, half * 512:(half + 1) * 512], start=(c == 0), stop=(c == NCH - 1)),
                                        reads=[wgres, xgres], writes=[psres])
                                if mode == "tmv":
                                    P.op("act", lambda e, vt=vt, ps=ps, half=half: e.activation(
                                        out=vt[:, half * 8:(half + 1) * 8, 0:64],
                                        in_=ps[:].rearrange("p (h d) -> p h d", h=8), func=AF.Copy),
                                        reads=[psres], writes=[vtres])
                                else:
                                    scl = 0.125 if gname == "dq" else 1.0
                                    P.op("act", lambda e, tm=tm, ps=ps, half=half, scl=scl: e.activation(
                                        out=tm[:, half * 512:(half + 1) * 512], in_=ps[:], func=AF.Copy, scale=scl),
                                        reads=[psres], writes=[tmres])
                            if mode == "tmv":
                                P.op("pool", lambda e, vt=vt, blk=blk: e.tensor_copy(
                                    out=vt[:, :, 64:65], in_=AP(kv_sb, blk, [[NCB, 128], [0, NH], [1, 1]])),
                                    reads=["kv"], writes=[vtres])
                                dstv = vf if gname == "fv" else vd
                                dst = AP(dstv, blk * NH * 65, [[NCB * NH * 65, 128], [1, NH * 65]])
                                P.op("sp", lambda e, dst=dst, vt=vt: e.dma_start(
                                    out=dst, in_=vt[:].rearrange("p h d -> p (h d)")),
                                    reads=[vtres], writes=[(gname, blk)], dma=("p1v", vtres[1]))
                            else:
                                rbt, rbres = rbr.next()
                                P.op("act", lambda e, rbt=rbt, tm=tm: e.activation(out=rbt[:], in_=tm[:], func=AF.Copy),
                                     reads=[tmres], writes=[rbres])
                                tm3 = tm[:].rearrange("p (h d) -> p h d", h=NH)
                                rb3 = rbt[:].rearrange("p (h d) -> p h d", h=NH)
                                x1, x2 = tm3[:, :, 0:8], tm3[:, :, 8:16]
                                cosb = AP(cos_sb, blk * 8, [[NCB * 8, 128], [0, NH], [1, 8]])
                                sinb = AP(sin_sb, blk * 8, [[NCB * 8, 128], [0, NH], [1, 8]])
                                for (o_, a_, ca, b_, cb, opx) in ((rb3[:, :, 0:8], x1, cosb, x2, sinb, ALU.subtract),
                                                                 (rb3[:, :, 8:16], x2, cosb, x1, sinb, ALU.add)):
                                    P.op("dve", lambda e, a_=a_, ca=ca: e.tensor_tensor(out=ra[:], in0=a_, in1=ca, op=ALU.mult),
                                         reads=[tmres, "cos"], writes=["ra"])
                                    P.op("dve", lambda e, b_=b_, cb=cb: e.tensor_tensor(out=rb_[:], in0=b_, in1=cb, op=ALU.mult),
                                         reads=[tmres, "sin"], writes=["rb_"])
                                    P.op("dve", lambda e, o_=o_, opx=opx: e.tensor_tensor(out=o_, in0=ra[:], in1=rb_[:], op=opx),
                                         reads=["ra", "rb_"], writes=[rbres])
                                if ti == 0:
                                    dt_, dtres = dtr.next()
                                tp, tpres = tpr.next()
                                for j in range(8):
                                    P.op("pe", lambda e, tp=tp, rbt=rbt, j=j: e.transpose(
                                        out=tp[:, j * 128:(j + 1) * 128], in_=rbt[:, j * 128:(j + 1) * 128],
                                        identity=ident_b[:]),
                                        reads=[rbres, "ident_b"], writes=[tpres])
                                P.op("dve", lambda e, dt_=dt_, tp=tp, ti=ti: e.tensor_copy(
                                    out=dt_[:, :, ti * 128:(ti + 1) * 128], in_=tp[:].rearrange("p (j t) -> p j t", j=8)),
                                    reads=[tpres], writes=[dtres])
                                if ti == 3:
                                    dstT, TT, toff = (qdT, TO, otg * 512) if gname == "dq" else (kdT, TC, tg * 512)
                                    dst = AP(dstT, toff, [[TT, 128], [128 * TT, 8], [1, 512]])
                                    P.op("sp", lambda e, dst=dst, dt_=dt_: e.dma_start(out=dst, in_=dt_[:]),
                                         reads=[dtres], writes=[(gname, tg)], dma=("p1d", dtres[1]))
                if mode == "fg":
                    for b0 in range(0, NCB, 32):
                        ps, psres = psr.next()
                        nb = min(32, NCB - b0)
                        for bb in range(nb):
                            P.op("pe", lambda e, ps=ps, b0=b0, bb=bb: e.transpose(
                                out=ps[:, bb * 16:(bb + 1) * 16], in_=negc_fm[:, (b0 + bb) * 128:(b0 + bb + 1) * 128],
                                identity=ident_f[0:16, 0:16]),
                                reads=["negc_fm", "ident_f"], writes=[psres])
                        P.op("dve", lambda e, ps=ps, b0=b0, nb=nb: e.tensor_copy(
                            out=negc_tm[:, b0:b0 + nb, :], in_=ps[:, 0:nb * 16].rearrange("p (b h) -> p b h", h=16)),
                            reads=[psres], writes=["negc_tm"])
            P.barrier()
            P.emit()
            P.new_phase()
        if upto <= 1:
            return nc

        for kind in ("fox", "dil"):
            with ExitStack() as st:
                KR = 65 if kind == "fox" else 64
                qr = Ring(nc, st, kind + "q", 2, [65, TO], BF16)
                kr = Ring(nc, st, kind + "k", 2, [65, TC], BF16)
                if kind == "dil":
                    for r_ in range(2):
                        P.op("pool", lambda e, r_=r_: e.memset(qr.t[r_][64:65, :], 0.0), writes=[(qr.name, r_)])
                        P.op("pool", lambda e, r_=r_: e.memset(kr.t[r_][64:65, :], 0.0), writes=[(kr.name, r_)])
                vr = Ring(nc, st, kind + "v", 2, [128, NCB, 4 * 65], BF16)
                sr = Ring(nc, st, kind + "s", 4, [128, 512], F32, psum=True)
                otr = Ring(nc, st, kind + "ot", 2, [128, 512], F32, psum=True)
                tor = Ring(nc, st, kind + "to", 2, [128, 4, 128], F32, psum=True)
                ptr_ = Ring(nc, st, kind + "pt", 4, [128, 512], BF16)
                osr = Ring(nc, st, kind + "os", 2, [65, 512], F32)
                rcr = Ring(nc, st, kind + "rc", 2, [128, 4], F32)
                ofr = Ring(nc, st, kind + "of", 2, [128, 4, 64], F32)
                cst = sb(st, kind + "cst", [128, 5 * 512], F32)
                if kind == "fox":
                    tri_b = sb(st, "tri_b", [128, 128], BF16)
                    P.op("sp", lambda e: e.dma_start(out=cst[:, 0:128], in_=tri_in.ap()), writes=["cst"], dma="c0")
                    P.op("dve", lambda e: e.tensor_copy(out=tri_b[:], in_=cst[:, 0:128]), reads=["cst"], writes=["tri_b"])
                else:
                    wcat_b = sb(st, "wcat_b", [128, 20, 512], BF16)
                    for pc in range(4):
                        P.op("sp", lambda e, pc=pc: e.dma_start(out=cst[:], in_=wcat_in[:, pc * 2560:(pc + 1) * 2560]),
                             writes=["cst"], dma="c0")
                        P.op("dve", lambda e, pc=pc: e.tensor_copy(
                            out=wcat_b[:, pc * 5:(pc + 1) * 5, :], in_=cst[:].rearrange("p (a b) -> p a b", a=5)),
                            reads=["cst"], writes=["wcat_b"])
                qsrc, ksrc, vsrc = (qfT, kfT, vf) if kind == "fox" else (qdT, kdT, vd)
                coff = 0 if kind == "fox" else 1024
                LA = 3
                units = []
                loads = {}
                head_starts = []
                for hg in range(4):
                    for hh in range(4):
                        h = hg * 4 + hh
                        head_starts.append((len(units), hg, hh))
                        for g in range(NOB // 4):
                            qb0 = NPB + 4 * g
                            kb_lo = 0 if kind == "fox" else max(0, qb0 - 16)
                            kbs = list(range(kb_lo, qb0 + 4))
                            for kb in kbs:
                                j = kb - qb0
                                q0 = (max(0, j) * 128) if kind == "fox" else 0
                                units.append((h, hh, g, kb, q0, kb == kbs[0], kb == kbs[-1], kb - (qb0 - 16), j))
                for n_, (u0, hg_, hh_) in enumerate(head_starts):
                    loads.setdefault(0 if n_ == 0 else head_starts[n_ - 1][0], []).append((hg_, hh_))
                cur = {}
                hstate = {}
                sinfo = {}
                deferred = []
                state = {"vg": None}

                def issue_loads(hg, hh):
                    h = hg * 4 + hh
                    if hh == 0:
                        vg, vgres = vr.next()
                        nvp = max(1, NCB // 16)
                        for vp in range(nvp):
                            nb_ = NCB // nvp
                            P.op("act", lambda e, vg=vg, hg=hg, vp=vp, nb_=nb_: e.dma_start(
                                out=vg[:, vp * nb_:(vp + 1) * nb_, :],
                                in_=AP(vsrc, hg * 4 * 65 + vp * nb_ * NH * 65,
                                       [[NCB * NH * 65, 128], [NH * 65, nb_], [1, 4 * 65]])),
                                writes=[vgres], dma=(kind + "v", vgres[1], vp))
                        state["vg"] = (vg, vgres)
                    qT, qres = qr.next()
                    kT, kres = kr.next()
                    P.op("sp", lambda e, qT=qT, h=h: e.dma_start(
                        out=qT[0:KR, :], in_=AP(qsrc, h * KR * TO, [[TO, KR], [1, TO]])),
                        writes=[qres], dma=(kind + "q", qres[1]))
                    P.op("sp", lambda e, kT=kT, h=h: e.dma_start(
                        out=kT[0:KR, :], in_=AP(ksrc, h * KR * TC, [[TC, KR], [1, TC]])),
                        writes=[kres], dma=(kind + "k", kres[1]))
                    hstate[h] = (qT, qres, kT, kres, state["vg"][0], state["vg"][1])

                def emit_S(ui):
                    (h, hh, g, kb, q0, first, last, ko, j) = units[ui]
                    for (hg_, hh_) in loads.get(ui, []):
                        issue_loads(hg_, hh_)
                    qT, qres, kT, kres, vg, vgres = hstate[h]
                    S, sres = sr.next()
                    sinfo[ui] = (S, sres)
                    P.op("pe", lambda e, S=S, kT=kT, qT=qT, kb=kb, g=g, q0=q0: e.matmul(
                        S[:, q0:512], lhsT=kT[:, kb * 128:(kb + 1) * 128],
                        rhs=qT[:, g * 512 + q0:(g + 1) * 512], start=True, stop=True),
                        reads=[kres, qres], writes=[sres])

                def finalize2(osb, osres, g, h):
                    to, tores = tor.next()
                    for jj in range(4):
                        P.op("pe", lambda e, to=to, osb=osb, jj=jj: e.transpose(
                            out=to[:, jj, 0:65], in_=osb[:, jj * 128:(jj + 1) * 128], identity=ident_f[0:65, 0:65]),
                            reads=[osres, "ident_f"], writes=[tores])
                    rc, rcres = rcr.next()
                    of, ofres = ofr.next()
                    P.op("dve", lambda e, rc=rc, to=to: e.reciprocal(out=rc[:], in_=to[:, :, 64]),
                         reads=[tores], writes=[rcres])
                    P.op("dve", lambda e, rc=rc, to=to, of=of: e.tensor_tensor(
                        out=of[:], in0=to[:, :, 0:64], in1=AP(rc, 0, [[4, 128], [1, 4], [0, 64]]), op=ALU.mult),
                        reads=[tores, rcres], writes=[ofres])
                    P.op("sp", lambda e, of=of, g=g, h=h: e.dma_start(
                        out=AP(attn_o, g * 512 * D + coff + h * 64, [[D, 128], [128 * D, 4], [1, 64]]), in_=of[:]),
                        reads=[ofres], writes=[("attn_o", g)], dma=(kind + "o", ofres[1]))

                def emit_rest(ui):
                    (h, hh, g, kb, q0, first, last, ko, j) = units[ui]
                    qT, qres, kT, kres, vg, vgres = hstate[h]
                    S, sres = sinfo.pop(ui)
                    if first:
                        cur["ot"] = otr.next()
                    ot, otres = cur["ot"]
                    Pt, ptres = ptr_.next()
                    if kind == "fox":
                        P.op("act", lambda e, Pt=Pt, S=S, q0=q0, kb=kb, h=h: e.activation(
                            out=Pt[:, q0:512], in_=S[:, q0:512], func=AF.Exp, bias=negc_tm[:, kb, h:h + 1]),
                            reads=[sres, "negc_tm"], writes=[ptres])
                        if j >= 0:
                            P.op("dve", lambda e, Pt=Pt, q0=q0: e.tensor_tensor(
                                out=Pt[:, q0:q0 + 128], in0=Pt[:, q0:q0 + 128], in1=tri_b[:], op=ALU.mult),
                                reads=[ptres, "tri_b"], writes=[ptres])
                    else:
                        P.op("act", lambda e, Pt=Pt, S=S: e.activation(out=Pt[:], in_=S[:], func=AF.Exp),
                             reads=[sres], writes=[ptres])
                        P.op("dve", lambda e, Pt=Pt, ko=ko: e.tensor_tensor(
                            out=Pt[:], in0=Pt[:], in1=wcat_b[:, ko, :], op=ALU.mult),
                            reads=[ptres, "wcat_b"], writes=[ptres])
                    P.op("pe", lambda e, ot=ot, vg=vg, Pt=Pt, kb=kb, hh=hh, q0=q0, first=first, last=last: e.matmul(
                        ot[0:65, q0:512], lhsT=vg[:, kb, hh * 65:(hh + 1) * 65], rhs=Pt[:, q0:512],
                        start=first, stop=last),
                        reads=[vgres, ptres], writes=[otres])
                    if last:
                        osb, osres = osr.next()
                        P.op("act", lambda e, osb=osb, ot=ot: e.activation(out=osb[:], in_=ot[0:65, :], func=AF.Copy),
                             reads=[otres], writes=[osres])
                        deferred.append((ui + 3, lambda osb=osb, osres=osres, g=g, h=h: finalize2(osb, osres, g, h)))

                NU = len(units)
                for idx in range(NU + LA):
                    if idx < NU:
                        emit_S(idx)
                    jdx = idx - LA
                    if jdx >= 0:
                        emit_rest(jdx)
                        while deferred and deferred[0][0] <= jdx:
                            deferred.pop(0)[1]()
                while deferred:
                    deferred.pop(0)[1]()
                P.barrier()
                P.emit()
                P.new_phase()
        if upto <= 2:
            return nc

        with ExitStack() as st:
            wst = Ring(nc, st, "p4ws", 2, [128, D], F32)
            wo_b = sb(st, "p4wo", [128, NCH, D], BF16)
            gmx = sb(st, "p4gm", [128, NCH], F32)
            atr = Ring(nc, st, "p4at", 2, [128, D], F32)
            xr = Ring(nc, st, "p4x", 2, [128, D], F32)
            jr = Ring(nc, st, "p4j", 1, [128, 1024], BF16)
            mxr = Ring(nc, st, "p4mx", 2, [128, D], BF16)
            mtr = Ring(nc, st, "p4mt", 2, [128, NCH, 128], BF16)
            htr = Ring(nc, st, "p4ht", 2, [128, D], F32)
            tpr = Ring(nc, st, "p4tp", 2, [128, 1024], BF16, psum=True)
            psr = Ring(nc, st, "p4ps", 4, [128, 512], F32, psum=True)
            ss = sb(st, "p4ss", [128, 2 * NOB], F32)
            rs = sb(st, "p4rs", [128, 2 * NOB], F32)
            rstd = sb(st, "p4rstd", [128, 2 * NOB], F32)
            P.op("sp", lambda e: e.dma_start(out=gmx[:], in_=g_mix.ap()), writes=["gmx"], dma="c0")
            for c in range(NCH):
                ws, wsres = wst.next()
                P.op("sp" if c % 2 == 0 else "act", lambda e, ws=ws, c=c: e.dma_start(
                    out=ws[:], in_=w_out[c * 128:(c + 1) * 128, :]), writes=[wsres], dma=("p4w", wsres[1]))
                if c % 2 == 0:
                    P.op("act", lambda e, ws=ws, c=c: e.activation(out=wo_b[:, c, :], in_=ws[:], func=AF.Copy,
                                                                  scale=gmx[:, c:c + 1]),
                         reads=[wsres, "gmx"], writes=["wo_b"])
                else:
                    P.op("dve", lambda e, ws=ws, c=c: e.tensor_scalar(
                        out=wo_b[:, c, :], in0=ws[:], scalar1=gmx[:, c:c + 1], scalar2=None, op0=ALU.mult),
                        reads=[wsres, "gmx"], writes=["wo_b"])
            for i in range(NOB):
                at, atres = atr.next()
                xt, xres = xr.next()
                P.op("sp", lambda e, at=at, i=i: e.dma_start(out=at[:], in_=attn_o[i * 128:(i + 1) * 128, :]),
                     reads=[("attn_o", i // 4)], writes=[atres], dma=("p4a", atres[1]))
                P.op("act", lambda e, xt=xt, i=i: e.dma_start(out=xt[:], in_=x_ctx[(NPB + i) * 128:(NPB + i + 1) * 128, :]),
                     writes=[xres], dma=("p4x", xres[1]))
                mx, mxres = mxr.next()
                for k in range(2):
                    col = 2 * i + k
                    jt, jres = jr.next()
                    P.op("act", lambda e, jt=jt, at=at, k=k, col=col: e.activation(
                        out=jt[:], in_=at[:, k * 1024:(k + 1) * 1024], func=AF.Square, accum_out=ss[:, col:col + 1]),
                        reads=[atres], writes=[jres, ("p4ss", col)])
                    P.op("dve", lambda e, col=col: e.tensor_scalar(out=rs[:, col:col + 1], in0=ss[:, col:col + 1],
                                                                 scalar1=1.0 / 1024, scalar2=RMS_EPS, op0=ALU.mult, op1=ALU.add),
                         reads=[("p4ss", col)], writes=[("p4rs", col)])
                    P.op("act", lambda e, col=col: e.activation(out=rs[:, col:col + 1], in_=rs[:, col:col + 1], func=AF.Sqrt),
                         reads=[("p4rs", col)], writes=[("p4rs", col)])
                    P.op("dve", lambda e, col=col: e.reciprocal(out=rstd[:, col:col + 1], in_=rs[:, col:col + 1]),
                         reads=[("p4rs", col)], writes=[("p4rstd", col)])
                    P.op("dve", lambda e, mx=mx, at=at, k=k, col=col: e.tensor_scalar(
                        out=mx[:, k * 1024:(k + 1) * 1024], in0=at[:, k * 1024:(k + 1) * 1024],
                        scalar1=rstd[:, col:col + 1], scalar2=None, op0=ALU.mult),
                        reads=[atres, ("p4rstd", col)], writes=[mxres])
                mt, mtres = mtr.next()
                for half in range(NCH // 8):
                    tp, tpres = tpr.next()
                    for j in range(8):
                        c = half * 8 + j
                        P.op("pe", lambda e, tp=tp, mx=mx, j=j, c=c: e.transpose(
                            out=tp[:, j * 128:(j + 1) * 128], in_=mx[:, c * 128:(c + 1) * 128], identity=ident_b[:]),
                            reads=[mxres, "ident_b"], writes=[tpres])
                    P.op("act", lambda e, mt=mt, tp=tp, half=half: e.activation(
                        out=mt[:, half * 8:(half + 1) * 8, :], in_=tp[:].rearrange("p (c t) -> p c t", c=8), func=AF.Copy),
                        reads=[tpres], writes=[mtres])
                ht, htres = htr.next()
                for n in range(D // 512):
                    ps, psres = psr.next()
                    for c in range(NCH):
                        P.op("pe", lambda e, ps=ps, mt=mt, c=c, n=n: e.matmul(
                            ps[:], lhsT=mt[:, c, :], rhs=wo_b[:, c, n * 512:(n + 1) * 512], start=(c == 0),
                            stop=(c == NCH - 1)), reads=[mtres, "wo_b"], writes=[psres])
                    P.op("dve", lambda e, ht=ht, xt=xt, ps=ps, n=n: e.tensor_tensor(
                        out=ht[:, n * 512:(n + 1) * 512], in0=xt[:, n * 512:(n + 1) * 512], in1=ps[:], op=ALU.add),
                        reads=[xres, psres], writes=[htres])
                P.op("sp", lambda e, ht=ht, i=i: e.dma_start(out=hbuf[i * 128:(i + 1) * 128, :], in_=ht[:]),
                     reads=[htres], writes=[("hbuf", i)], dma=("p4h", htres[1]))
            P.barrier()
            P.emit()
            P.new_phase()
        if upto <= 3:
            return nc

        with ExitStack() as st:
            cir = Ring(nc, st, "cvi", 2, [128, 4 * D], F32)
            cor = Ring(nc, st, "cvo", 2, [128, 4 * D], BF16)
            n_ = 0
            for (src_t, dst_t) in ((p_down, dn_b), (p_up, up_b)):
                for ch in range(NEXP // 512):
                    ci, cires = cir.next()
                    co, cores = cor.next()
                    P.op("sp" if n_ % 2 == 0 else "act", lambda e, ci=ci, src_t=src_t, ch=ch: e.dma_start(
                        out=ci[:], in_=AP(src_t, ch * 512 * D, [[4 * D, 128], [1, 4 * D]])),
                        writes=[cires], dma=("cvl", cires[1]))
                    eng = ("act", "dve", "pool")[n_ % 3]
                    if eng == "act":
                        P.op("act", lambda e, ci=ci, co=co: e.activation(out=co[:], in_=ci[:], func=AF.Copy),
                             reads=[cires], writes=[cores])
                    else:
                        P.op(eng, lambda e, ci=ci, co=co: e.tensor_copy(out=co[:], in_=ci[:]),
                             reads=[cires], writes=[cores])
                    P.op("sp" if n_ % 2 == 1 else "act", lambda e, co=co, dst_t=dst_t, ch=ch: e.dma_start(
                        out=AP(dst_t, ch * 512 * D, [[4 * D, 128], [1, 4 * D]]), in_=co[:]),
                        reads=[cores], writes=[("cv", n_)], dma=("cvs", cores[1]))
                    n_ += 1
            P.barrier()
            P.emit()
            P.new_phase()

        with ExitStack() as st:
            NS = 128
            wq_b = sb(st, "p5wq", [128, NCH, 2048], BF16)
            sk_b = sb(st, "p5sk", [128, 16, 128], BF16)
            gffn = sb(st, "p5gf", [128, D], F32)
            gfin = sb(st, "p5gn", [128, D], F32)
            io_i = sb(st, "p5ioi", [128, 16], I32)
            io_f = sb(st, "p5iof", [128, 16], F32)

            for (t_, src, key) in ((gffn, g_ffn, "gffn"), (gfin, g_fin, "gfin")):
                P.op("sp", lambda e, t_=t_, src=src: e.dma_start(out=t_[:], in_=src.ap()), writes=[key], dma="c0")
            P.op("pool", lambda e: e.iota(io_i[:], [[1, 16]], base=0, channel_multiplier=0), writes=["io_i"])
            P.op("dve", lambda e: e.tensor_copy(out=io_f[:], in_=io_i[:]), reads=["io_i"], writes=["io_f"])
            st2 = ExitStack()
            wst = Ring(nc, st2, "p5ws", 2, [128, D], F32)
            for c in range(NCH):
                ws, wsres = wst.next()
                P.op("sp" if c % 2 == 0 else "act", lambda e, ws=ws, c=c: e.dma_start(
                    out=ws[:], in_=w_pq[c * 128:(c + 1) * 128, :]), writes=[wsres], dma=("p5w", wsres[1]))
                if c % 2 == 0:
                    P.op("act", lambda e, ws=ws, c=c: e.activation(out=wq_b[:, c, :], in_=ws[:], func=AF.Copy),
                         reads=[wsres], writes=["wq_b"])
                else:
                    P.op("dve", lambda e, ws=ws, c=c: e.tensor_copy(out=wq_b[:, c, :], in_=ws[:]),
                         reads=[wsres], writes=["wq_b"])
            ws, wsres = wst.next()
            P.op("sp", lambda e, ws=ws: e.dma_start(out=ws[:], in_=skT_in.ap()), writes=[wsres], dma=("p5w", wsres[1]))
            P.op("dve", lambda e, ws=ws: e.tensor_copy(out=sk_b[:], in_=ws[:].rearrange("p (g n) -> p g n", g=16)),
                 reads=[wsres], writes=["sk_b"])
            P.barrier()
            P.emit()
            P.new_phase()
            st2.close()
            hts = [sb(st, "p5ht%d" % k_, [128, D], F32) for k_ in range(3)]
            hnbs = [sb(st, "p5hnb%d" % k_, [128, D], BF16) for k_ in range(2)]
            hnT = sb(st, "p5hnT", [128, NCH, 128], BF16)
            qTs = sb(st, "p5qT", [128, 4, 128], BF16)
            scr_ = Ring(nc, st, "p5sc", 2, [128, 512], F32)
            scx = sb(st, "p5scx", [128, 128], F32)
            v16 = sb(st, "p5v16", [128, 16, 16], F32)
            i16u = sb(st, "p5i16u", [128, 16, 16], U32)
            i16f = sb(st, "p5i16f", [128, 16, 16], F32)
            work = sb(st, "p5work", [128, 2048], F32)
            cand = work[:].rearrange("p (h c) -> p h c", h=8)
            cscr = sb(st, "p5cscr", [128, 256], F32)
            tv = sb(st, "p5tv", [128, 8, 16], F32)
            tcu = sb(st, "p5tcu", [128, 8, 16], U32)
            abu = sb(st, "p5abu", [128, 2, 128], U32)
            abf = sb(st, "p5abf", [128, 2, 128], F32)
            eq = work[:].rearrange("p (h k a) -> p h k a", h=8, k=16)
            i12 = sb(st, "p5i12", [128, 2, 128], F32)
            eidf = sb(st, "p5eidf", [128, NS], F32)
            eidxs = [sb(st, "p5eidx%d" % k_, [128, NS], I32) for k_ in range(3)]
            gtss = [sb(st, "p5gts%d" % k_, [128, 8, 16], F32) for k_ in range(2)]
            gsum = sb(st, "p5gsum", [128, 8], F32)
            hdn = sb(st, "p5hdn", [128, NS], F32)
            zt = sb(st, "p5zt", [128, NS], F32)
            z = sb(st, "p5z", [128, NS], F32)
            ur = Ring(nc, st, "p5u", 8, [128, D], BF16)
            junk = sb(st, "p5junk", [128, D], BF16)
            dgr = Ring(nc, st, "p5dg", 3, [128, 128], BF16)
            ot_ = work
            ss = sb(st, "p5ss", [128, 4], F32)
            tpr = Ring(nc, st, "p5tp", 1, [128, 1024], BF16, psum=True)
            mmr = Ring(nc, st, "p5mm", 3, [128, 512], F32, psum=True)
            ybk = [st.enter_context(nc.psum_tensor("p5y%d" % n, [128, 512], F32)) for n in range(4)]

            def rms_rstd(src_ap, col, key):
                jt = junk
                P.op("act", lambda e: e.activation(out=jt[:], in_=src_ap, func=AF.Square, accum_out=ss[:, col:col + 1]),
                     reads=[key], writes=["junk", ("p5ss", col)])
                P.op("dve", lambda e: e.tensor_scalar(out=ss[:, col:col + 1], in0=ss[:, col:col + 1], scalar1=1.0 / D,
                                                      scalar2=RMS_EPS, op0=ALU.mult, op1=ALU.add),
                     reads=[("p5ss", col)], writes=[("p5ss", col)])
                P.op("act", lambda e: e.activation(out=ss[:, col:col + 1], in_=ss[:, col:col + 1], func=AF.Sqrt),
                     reads=[("p5ss", col)], writes=[("p5ss", col)])
                P.op("dve", lambda e: e.reciprocal(out=ss[:, col + 2:col + 3], in_=ss[:, col:col + 1]),
                     reads=[("p5ss", col)], writes=[("p5ss", col + 2)])
                return ss[:, col + 2:col + 3], ("p5ss", col + 2)

            def top16(src_ap, scratch_ap, vout, iout, rkeys, skey, wkeys):
                P.op("dve", lambda e: e.max(out=vout[0], in_=src_ap), reads=rkeys, writes=[wkeys[0]])
                P.op("dve", lambda e: e.max_index(out=iout[0], in_max=vout[0], in_values=src_ap),
                     reads=rkeys + [wkeys[0]], writes=[wkeys[1]])
                P.op("dve", lambda e: e.match_replace(out=scratch_ap, in_to_replace=vout[0], in_values=src_ap,
                                                      imm_value=-1e30), reads=rkeys + [wkeys[0]], writes=[skey])
                P.op("dve", lambda e: e.max(out=vout[1], in_=scratch_ap), reads=[skey], writes=[wkeys[0]])
                P.op("dve", lambda e: e.max_index(out=iout[1], in_max=vout[1], in_values=scratch_ap),
                     reads=[skey, wkeys[0]], writes=[wkeys[1]])

            def tile_4b(i):
                ht, eidx, hnb, gts = hts[i % 3], eidxs[i % 3], hnbs[i % 2], gtss[i % 2]
                P.op("sp", lambda e, i=i: e.dma_start(out=ht[:], in_=hbuf[i * 128:(i + 1) * 128, :]),
                     reads=[("hbuf", i)], writes=[("ht", i % 3)], dma="p5h")
                rstd_ap, rkey = rms_rstd(ht[:], 0, ("ht", i % 3))
                P.op("dve", lambda e, rstd_ap=rstd_ap: e.scalar_tensor_tensor(
                    out=hnb[:], in0=ht[:], scalar=rstd_ap, in1=gffn[:], op0=ALU.mult, op1=ALU.mult),
                    reads=[("ht", i % 3), rkey, "gffn"], writes=[("hnb", i % 2)])
                for half in range(NCH // 8):
                    tp, tpres = tpr.next()
                    for j in range(8):
                        c = half * 8 + j
                        P.op("pe", lambda e, tp=tp, j=j, c=c: e.transpose(
                            out=tp[:, j * 128:(j + 1) * 128], in_=hnb[:, c * 128:(c + 1) * 128], identity=ident_b[:]),
                            reads=[("hnb", i % 2), "ident_b"], writes=[tpres])
                    P.op("act", lambda e, tp=tp, half=half: e.activation(
                        out=hnT[:, half * 8:(half + 1) * 8, :], in_=tp[:].rearrange("p (c t) -> p c t", c=8), func=AF.Copy),
                        reads=[tpres], writes=["hnT"])
                for qq in range(4):
                    mm, mmres = mmr.next()
                    for gi in range(4):
                        G = 4 * qq + gi
                        for c in range(NCH):
                            P.op("pe", lambda e, mm=mm, gi=gi, G=G, c=c: e.matmul(
                                mm[:, gi * 128:(gi + 1) * 128], lhsT=wq_b[:, c, G * 128:(G + 1) * 128], rhs=hnT[:, c, :],
                                start=(c == 0), stop=(c == NCH - 1)), reads=["wq_b", "hnT"], writes=[mmres])
                    P.op("act", lambda e, mm=mm: e.activation(out=qTs[:], in_=mm[:].rearrange("p (g t) -> p g t", g=4),
                                                              func=AF.Copy), reads=[mmres], writes=["qTs"])
                    mm2, mm2res = mmr.next()
                    for gi in range(4):
                        G = 4 * qq + gi
                        P.op("pe", lambda e, mm2=mm2, gi=gi, G=G: e.matmul(
                            mm2[:, gi * 128:(gi + 1) * 128], lhsT=qTs[:, gi, :], rhs=sk_b[:, G, :], start=True, stop=True),
                            reads=["qTs", "sk_b"], writes=[mm2res])
                    sc, scres = scr_.next()
                    P.op("act", lambda e, sc=sc, mm2=mm2: e.activation(out=sc[:], in_=mm2[:], func=AF.Copy),
                         reads=[mm2res], writes=[scres])
                    for gi in range(4):
                        G = 4 * qq + gi
                        top16(sc[:, gi * 128:(gi + 1) * 128], scx[:], (v16[:, G, 0:8], v16[:, G, 8:16]),
                              (i16u[:, G, 0:8], i16u[:, G, 8:16]), [scres], "scx", ["v16", "i16u"])
                P.op("dve", lambda e: e.tensor_copy(out=i16f[:], in_=i16u[:]), reads=["i16u"], writes=["i16f"])
                P.op("dve", lambda e: e.tensor_tensor(
                    out=cand.rearrange("p h (a b) -> p h a b", a=16),
                    in0=AP(v16, 0, [[256, 128], [32, 8], [1, 16], [0, 16]]),
                    in1=AP(v16, 16, [[256, 128], [32, 8], [0, 16], [1, 16]]), op=ALU.add),
                    reads=["v16"], writes=["work"])
                for h in range(8):
                    top16(cand[:, h, :], cscr[:], (tv[:, h, 0:8], tv[:, h, 8:16]), (tcu[:, h, 0:8], tcu[:, h, 8:16]),
                          ["work"], "cscr", ["tv", "tcu"])
                P.op("dve", lambda e: e.tensor_tensor(out=gts[:], in0=tv[:], in1=AP(tv, 0, [[128, 128], [16, 8], [0, 16]]),
                                                      op=ALU.subtract), reads=["tv"], writes=[("gts", i % 2)])
                P.op("act", lambda e: e.activation(out=gts[:], in_=gts[:], func=AF.Exp), reads=[("gts", i % 2)], writes=[("gts", i % 2)])
                P.op("dve", lambda e: e.tensor_reduce(out=gsum[:], in_=gts[:], axis=AX.X, op=ALU.add),
                     reads=[("gts", i % 2)], writes=["gsum"])
                P.op("dve", lambda e: e.reciprocal(out=gsum[:], in_=gsum[:]), reads=["gsum"], writes=["gsum"])
                P.op("dve", lambda e: e.tensor_tensor(out=gts[:], in0=gts[:], in1=AP(gsum, 0, [[8, 128], [1, 8], [0, 16]]),
                                                      op=ALU.mult), reads=[("gts", i % 2), "gsum"], writes=[("gts", i % 2)])
                tcu2 = tcu[:].rearrange("p h k -> p (h k)")
                P.op("dve", lambda e: e.tensor_scalar(out=abu[:, 0, :], in0=tcu2, scalar1=4, scalar2=None,
                                                      op0=ALU.logical_shift_right), reads=["tcu"], writes=["abu"])
                P.op("dve", lambda e: e.tensor_scalar(out=abu[:, 1, :], in0=tcu2, scalar1=15, scalar2=None,
                                                      op0=ALU.bitwise_and), reads=["tcu"], writes=["abu"])
                P.op("dve", lambda e: e.tensor_copy(out=abf[:], in_=abu[:]), reads=["abu"], writes=["abf"])
                for w in range(2):
                    P.op("dve", lambda e, w=w: e.tensor_tensor(
                        out=eq, in0=AP(abf, w * 128, [[256, 128], [16, 8], [1, 16], [0, 16]]),
                        in1=AP(io_f, 0, [[16, 128], [0, 8], [0, 16], [1, 16]]), op=ALU.is_equal),
                        reads=["abf", "io_f"], writes=["work"])
                    P.op("dve", lambda e, w=w: e.tensor_tensor(
                        out=eq, in0=eq, in1=AP(i16f, w * 16, [[256, 128], [32, 8], [0, 16], [1, 16]]), op=ALU.mult),
                        reads=["work", "i16f"], writes=["work"])
                    P.op("dve", lambda e, w=w: e.tensor_reduce(
                        out=i12[:, w, :], in_=eq.rearrange("p h k a -> p (h k) a"), axis=AX.X, op=ALU.add),
                        reads=["work"], writes=["i12"])
                P.op("dve", lambda e: e.scalar_tensor_tensor(out=eidf[:], in0=i12[:, 0, :], scalar=128.0, in1=i12[:, 1, :],
                                                             op0=ALU.mult, op1=ALU.add), reads=["i12"], writes=["eidf"])
                P.op("dve", lambda e: e.tensor_copy(out=eidx[:], in_=eidf[:]), reads=["eidf"], writes=[("eidx", i % 3)])
            def down_slot(i, s_):
                eidx = eidxs[i % 3]
                u, ures = ur.next()
                P.op("pool", lambda e, u=u: e.indirect_dma_start(
                    out=u[:], out_offset=None, in_=dn_b.ap(),
                    in_offset=bass.IndirectOffsetOnAxis(ap=eidx[:, s_:s_ + 1], axis=0)),
                    reads=[("eidx", i % 3)], writes=[ures], dma=("p5g", ures[1], i % 2))
                P.op("dve", lambda e, u=u: e.scalar_tensor_tensor(
                    out=junk[:], in0=u[:], scalar=1.0, in1=hnbs[i % 2][:], op0=ALU.mult, op1=ALU.mult,
                    accum_out=hdn[:, s_:s_ + 1]), reads=[ures, ("hnb", i % 2)], writes=["junk", "hdn"])

            def gelu_gate(i):
                gts = gtss[i % 2]
                gk = ("gts", i % 2)
                P.op("dve", lambda e: e.tensor_tensor(out=zt[:], in0=hdn[:], in1=hdn[:], op=ALU.mult),
                     reads=["hdn"], writes=["zt"])
                P.op("dve", lambda e: e.tensor_scalar(out=zt[:], in0=zt[:], scalar1=0.044715, scalar2=1.0, op0=ALU.mult,
                                                      op1=ALU.add), reads=["zt"], writes=["zt"])
                P.op("dve", lambda e: e.tensor_tensor(out=zt[:], in0=zt[:], in1=hdn[:], op=ALU.mult),
                     reads=["zt", "hdn"], writes=["zt"])
                P.op("act", lambda e: e.activation(out=zt[:], in_=zt[:], func=AF.Sigmoid, scale=2.0 * 0.7978845608028654),
                     reads=["zt"], writes=["zt"])
                P.op("dve", lambda e: e.tensor_tensor(out=zt[:], in0=zt[:], in1=hdn[:], op=ALU.mult),
                     reads=["zt", "hdn"], writes=["zt"])
                P.op("dve", lambda e: e.tensor_tensor(out=z[:], in0=zt[:], in1=gts[:].rearrange("p h k -> p (h k)"),
                                                      op=ALU.mult), reads=["zt", gk], writes=["z"])

            def up_slot(i, s_):
                eidx = eidxs[i % 3]
                u, ures = ur.next()
                P.op("pool", lambda e, u=u: e.indirect_dma_start(
                    out=u[:], out_offset=None, in_=up_b.ap(),
                    in_offset=bass.IndirectOffsetOnAxis(ap=eidx[:, s_:s_ + 1], axis=0)),
                    reads=[("eidx", i % 3)], writes=[ures], dma=("p5g", ures[1], i % 2))
                dg, dgres = dgr.next()
                P.op("dve", lambda e, dg=dg: e.tensor_scalar(out=dg[:], in0=ident_f[:], scalar1=z[:, s_:s_ + 1],
                                                             scalar2=None, op0=ALU.mult),
                     reads=["ident_f", "z"], writes=[dgres])
                for n in range(4):
                    P.op("pe", lambda e, dg=dg, u=u, n=n: e.matmul(
                        ybk[n][:], lhsT=dg[:], rhs=u[:, n * 512:(n + 1) * 512], start=(s_ == 0), stop=(s_ == NS - 1)),
                        reads=[dgres, ures], writes=[("y", n)])

            def final(i):
                ht = hts[i % 3]
                for n in range(4):
                    P.op("dve", lambda e, n=n: e.tensor_tensor(out=ot_[:, n * 512:(n + 1) * 512],
                                                                in0=ht[:, n * 512:(n + 1) * 512], in1=ybk[n][:], op=ALU.add),
                         reads=[("ht", i % 3), ("y", n)], writes=["work"])
                rstd_ap, rkey = rms_rstd(ot_[:], 1, "work")
                P.op("dve", lambda e, rstd_ap=rstd_ap: e.scalar_tensor_tensor(
                    out=ot_[:], in0=ot_[:], scalar=rstd_ap, in1=gfin[:], op0=ALU.mult, op1=ALU.mult),
                    reads=["work", rkey, "gfin"], writes=["work"])
                P.op("sp", lambda e, i=i: e.dma_start(out=out[i * 128:(i + 1) * 128, :], in_=ot_[:]),
                     reads=["work"], writes=[("out", i)], dma="p5o")

            def cap4b(i):
                if i >= NOB:
                    return []
                P.capture()
                tile_4b(i)
                return P.end_capture()

            tile_4b(0)
            inter = cap4b(1)
            per = (len(inter) + NS - 1) // NS
            for s_ in range(NS):
                down_slot(0, s_)
                P.replay(inter[s_ * per:(s_ + 1) * per])
            gelu_gate(0)
            for i in range(NOB):
                inter = cap4b(i + 2)
                per = (len(inter) + NS - 1) // NS
                for s_ in range(NS):
                    up_slot(i, s_)
                    if i + 1 < NOB:
                        down_slot(i + 1, s_)
                    P.replay(inter[s_ * per:(s_ + 1) * per])
                final(i)
                if i + 1 < NOB:
                    gelu_gate(i + 1)
            P.barrier()
            P.emit()
            P.finish()
    return nc


def _const_tables(cfg, p):
    NCB, NPB = cfg.NCB, cfg.NPB
    TP = NPB * 128
    idx = np.arange(cfg.TC)
    pos = idx - (0 if p == 1 else TP)
    valid = (pos >= 0).astype(np.float32)
    kvalid = np.ascontiguousarray(valid.reshape(NCB, 128).T)
    inv = (500000.0 ** (-np.arange(0, 16, 2, dtype=np.float32) / 16.0)).astype(np.float32)
    ang = np.maximum(pos, 0).astype(np.float32)[:, None] * inv[None, :]
    cos = np.cos(ang).astype(np.float32).reshape(NCB, 128, 8).transpose(1, 0, 2).reshape(128, NCB * 8)
    sin = np.sin(ang).astype(np.float32).reshape(NCB, 128, 8).transpose(1, 0, 2).reshape(128, NCB * 8)
    kl = np.arange(128)[:, None, None]
    ko = np.arange(20)[None, :, None]
    ql = np.arange(512)[None, None, :]
    delta = (16 - ko) * 128 + ql - kl
    w = ((delta >= 0) & (delta <= 128)).astype(np.float32)
    w += ((delta >= 0) & (delta <= 512) & (delta % 4 == 0)).astype(np.float32)
    w += ((delta >= 0) & (delta <= 2048) & (delta % 16 == 0)).astype(np.float32)
    wcat = np.ascontiguousarray(w.reshape(128, 20 * 512))
    tri = (np.arange(128)[:, None] <= np.arange(128)[None, :]).astype(np.float32)
    return dict(kvalid=kvalid, rope_cos=np.ascontiguousarray(cos), rope_sin=np.ascontiguousarray(sin),
                wcat=wcat, tri=tri, ident=np.eye(128, dtype=np.float32))


def _pc(v, nch):
    return np.ascontiguousarray(np.asarray(v, np.float32).reshape(nch, 128).T)


def shared_inputs(cfg, attn_norm_gain, w_in, forget_bias, fox_out_gain, dil_out_gain, w_out,
                  ffn_norm_gain, peer_query, peer_sub_keys, peer_down, peer_up, final_norm_gain):
    f = lambda a: np.ascontiguousarray(np.asarray(a, np.float32))
    NCH = cfg.NCH
    skT = np.asarray(peer_sub_keys[0], np.float32).reshape(16, 128, 128).transpose(2, 0, 1)
    return dict(
        w_in=f(w_in[0]), g_attn=_pc(attn_norm_gain[0], NCH), fbias=f(forget_bias[0]).reshape(16, 1),
        g_mix=_pc(np.concatenate([np.asarray(fox_out_gain[0]), np.asarray(dil_out_gain[0])]), NCH),
        w_out=f(w_out[0]), g_ffn=np.ascontiguousarray(np.broadcast_to(f(ffn_norm_gain[0])[None, :], (128, cfg.D))),
        w_pq=f(peer_query[0]), skT=np.ascontiguousarray(skT.reshape(128, 16 * 128)),
        p_down=f(peer_down[0]), p_up=f(peer_up[0]),
        g_fin=np.ascontiguousarray(np.broadcast_to(f(final_norm_gain)[None, :], (128, cfg.D))))


def core_inputs(cfg, xb, p, shared, tables):
    TP, TO = cfg.NPB * 128, cfg.TO
    if p == 1:
        x_ctx = np.ascontiguousarray(xb[0:TP + TO])
    else:
        x_ctx = np.concatenate([np.zeros((TP, cfg.D), np.float32), xb[0:TO]], axis=0)
    d = dict(shared)
    d.update(tables[p])
    d["x_ctx"] = x_ctx
    return d


def kernel(**inputs):
    cfg = Cfg(32, 32)
    x = np.asarray(inputs["x"], np.float32)
    B, S, D = x.shape
    sh = shared_inputs(cfg, **{k: np.asarray(v) for k, v in inputs.items() if k != "x"})
    tables = [_const_tables(cfg, 0), _const_tables(cfg, 1)]
    in_maps = [core_inputs(cfg, x[c // 2], c % 2, sh, tables) for c in range(2 * B)]
    nc = build(cfg)
    res = run_bass_kernel_spmd(nc, in_maps, core_ids=list(range(2 * B)))
    outp = np.empty((B, S, D), np.float32)
    for c in range(2 * B):
        outp[c // 2, (c % 2) * cfg.TO:(c % 2 + 1) * cfg.TO] = np.asarray(res.results[c]["out"], np.float32)
    return outp
```

```python
import numpy as np
from contextlib import ExitStack

import concourse.bass as bass
import concourse.mybir as mybir
from concourse.bass_utils import run_bass_kernel_spmd

F32 = mybir.dt.float32
BF16 = mybir.dt.bfloat16
I32 = mybir.dt.int32
U32 = mybir.dt.uint32
AF = mybir.ActivationFunctionType
ALU = mybir.AluOpType
AX = mybir.AxisListType

HD = 64
NH = 16
RMS_EPS = 1e-6
NEXP = 16384


class Cfg:
    def __init__(self, npb=32, nob=32, D=2048):
        self.NPB = npb
        self.NOB = nob
        self.NCB = npb + nob
        self.D = D
        self.NCH = D // 128
        self.TC = self.NCB * 128
        self.TO = self.NOB * 128
        self.IN_COLS = 3 * 1024 + 16 + 3 * 1024


class Prog:
    ENGS = ("pe", "act", "dve", "pool", "sp")

    def __init__(self, nc, stack):
        self.nc = nc
        self.stack = stack
        self.all_sems = []
        self.dpool = []
        self.nphase = 0
        self.streams = {e: [] for e in self.ENGS}
        self.new_phase()

    def _alloc(self, name):
        h = self.stack.enter_context(self.nc.semaphore(name))
        self.all_sems.append(h)
        return h

    def new_phase(self):
        self.nphase += 1
        self.sem = {e: self._alloc("s%d_%s" % (self.nphase, e)) for e in self.ENGS}
        self.cnt = {e: 0 for e in self.ENGS}
        self.dsem = {}
        self.dnext = 0
        self.known = {e: {} for e in self.ENGS}
        self.last_w = {}
        self.readers = {}

    def finish(self):
        sems = list(self.all_sems)
        with self.nc.Block() as block:
            @block.gpsimd
            def _(e):
                for h in sems:
                    e.sem_clear(h)

    def _semh(self, key):
        if isinstance(key, str):
            return self.sem[key]
        return self.dsem[key[1]][0]

    def capture(self):
        self._cap = []

    def end_capture(self):
        c, self._cap = self._cap, None
        return c

    def replay(self, items):
        for it in items:
            self.op(*it)

    def op(self, eng, fn, reads=(), writes=(), dma=None):
        if getattr(self, "_cap", None) is not None:
            self._cap.append((eng, fn, tuple(reads), tuple(writes), dma))
            return
        deps = {}

        def add(ev):
            if ev is None:
                return
            k, v = ev
            if deps.get(k, 0) < v:
                deps[k] = v

        writes = list(writes)
        if dma is not None:
            writes.append(("dmakey", dma))
        for r in reads:
            add(self.last_w.get(r))
        for w in writes:
            add(self.last_w.get(w))
            for k, v in self.readers.get(w, {}).items():
                add((k, v))
        is_dma = dma is not None
        if is_dma:
            if dma not in self.dsem:
                if self.dnext >= len(self.dpool):
                    self.dpool.append([self._alloc("dsem%d" % len(self.dpool)), 0])
                self.dsem[dma] = self.dpool[self.dnext]
                self.dnext += 1
            self.dsem[dma][1] += 1
            ev = (("dma", dma), 16 * self.dsem[dma][1])
            inc = (self.dsem[dma][0], 16)
        else:
            self.cnt[eng] += 1
            ev = (eng, self.cnt[eng])
            inc = (self.sem[eng], 1)
        waits = []
        kn = self.known[eng]
        for k, v in deps.items():
            if k == eng and eng == "pe" and not is_dma:
                continue
            if kn.get(k, 0) >= v:
                continue
            kn[k] = v
            waits.append((self._semh(k), v))
        for w in writes:
            self.last_w[w] = ev
            self.readers[w] = {}
        for r in reads:
            d = self.readers.setdefault(r, {})
            if d.get(ev[0], 0) < ev[1]:
                d[ev[0]] = ev[1]
        self.streams[eng].append((waits, fn, inc))

    def barrier(self):
        allev = [(e, self.cnt[e]) for e in self.ENGS if self.cnt[e] > 0]
        allev += [(("dma", k), 16 * v[1]) for k, v in self.dsem.items()]
        for eng in self.ENGS:
            waits = []
            kn = self.known[eng]
            for k, v in allev:
                if kn.get(k, 0) >= v:
                    continue
                kn[k] = v
                waits.append((self._semh(k), v))
            if waits:
                self.streams[eng].append((waits, None, None))

    def emit(self):
        nc = self.nc
        streams = self.streams
        with nc.Block() as block:
            def run(name, e):
                for waits, fn, inc in streams[name]:
                    for s, v in waits:
                        e.wait_ge(s, v)
                    if fn is not None:
                        fn(e).then_inc(inc[0], inc[1])

            @block.tensor
            def _(e):
                run("pe", e)

            @block.scalar
            def _(e):
                run("act", e)

            @block.vector
            def _(e):
                run("dve", e)

            @block.gpsimd
            def _(e):
                run("pool", e)

            @block.sync
            def _(e):
                run("sp", e)
        self.streams = {e: [] for e in self.ENGS}


def AP(t, off, pat):
    return bass.AP(t, off, [list(x) for x in pat])


class Ring:
    def __init__(self, nc, stack, name, n, shape, dtype, psum=False):
        mk = nc.psum_tensor if psum else nc.sbuf_tensor
        self.t = [stack.enter_context(mk("%s%d" % (name, i), shape, dtype)) for i in range(n)]
        self.name = name
        self.n = n
        self.i = -1

    def next(self):
        self.i += 1
        s = self.i % self.n
        return self.t[s], (self.name, s)


def build(cfg, upto=99, dbg_out=()):
    nc = bass.Bass("TRN2", target_bir_lowering=False)
    D, NCH, NCB, NOB, NPB, TC, TO = cfg.D, cfg.NCH, cfg.NCB, cfg.NOB, cfg.NPB, cfg.TC, cfg.TO
    NTG = NCB // 4
    OTG0 = NPB // 4

    def din(name, shape, dt=F32):
        return nc.dram_tensor(name, list(shape), dt, kind="ExternalInput")

    def dscr(name, shape, dt):
        return nc.dram_tensor(name, list(shape), dt, kind="ExternalOutput" if name in dbg_out else "Internal")

    x_ctx = din("x_ctx", [TC, D])
    kvalid = din("kvalid", [128, NCB])
    rope_cos = din("rope_cos", [128, NCB * 8])
    rope_sin = din("rope_sin", [128, NCB * 8])
    wcat_in = din("wcat", [128, 20 * 512])
    tri_in = din("tri", [128, 128])
    ident_in = din("ident", [128, 128])
    w_in = din("w_in", [D, cfg.IN_COLS])
    g_attn = din("g_attn", [128, NCH])
    fbias = din("fbias", [16, 1])
    g_mix = din("g_mix", [128, NCH])
    w_out = din("w_out", [D, D])
    g_ffn = din("g_ffn", [128, D])
    w_pq = din("w_pq", [D, 2048])
    skT_in = din("skT", [128, 16 * 128])
    p_down = din("p_down", [NEXP, D])
    p_up = din("p_up", [NEXP, D])
    g_fin = din("g_fin", [128, D])
    out = nc.dram_tensor("out", [TO, D], F32, kind="ExternalOutput")

    xnT = dscr("xnT", [NCH, 128, TC], BF16)
    qfT = dscr("qfT", [NH, 65, TO], BF16)
    kfT = dscr("kfT", [NH, 65, TC], BF16)
    vf = dscr("vf", [128, NCB, NH, 65], BF16)
    qdT = dscr("qdT", [NH * 64, TO], BF16)
    kdT = dscr("kdT", [NH * 64, TC], BF16)
    vd = dscr("vd", [128, NCB, NH, 65], BF16)
    attn_o = dscr("attn_o", [TO, D], F32)
    hbuf = dscr("hbuf", [TO, D], F32)
    ctab = dscr("ctab", [NEXP, 2 * D], BF16)
    dbg = {}

    with ExitStack() as gstack:
        P = Prog(nc, gstack)
        sb = lambda st, name, shape, dt: st.enter_context(nc.sbuf_tensor(name, list(shape), dt))

        ident_f = sb(gstack, "ident_f", [128, 128], F32)
        ident_b = sb(gstack, "ident_b", [128, 128], BF16)
        negc_tm = sb(gstack, "negc_tm", [128, NCB, 16], F32)
        P.op("sp", lambda e: e.dma_start(out=ident_f[:], in_=ident_in.ap()), writes=["ident_f"], dma="c0")
        P.op("dve", lambda e: e.tensor_copy(out=ident_b[:], in_=ident_f[:]), reads=["ident_f"], writes=["ident_b"])

        with ExitStack() as st:
            xr = Ring(nc, st, "p0x", 2, [128, D], F32)
            jr = Ring(nc, st, "p0j", 1, [128, D], BF16)
            xbr = Ring(nc, st, "p0xb", 2, [128, D], BF16)
            tpr = Ring(nc, st, "p0tp", 4, [128, 1024], BF16, psum=True)
            xtr = Ring(nc, st, "p0xt", 2, [128, NCH, 512], BF16)
            ss = sb(st, "p0ss", [128, NCB], F32)
            rs = sb(st, "p0rs", [128, NCB], F32)
            rstd = sb(st, "p0rstd", [128, NCB], F32)
            for i in range(NCB):
                xt, xres = xr.next()
                P.op("sp", lambda e, xt=xt, i=i: e.dma_start(out=xt[:], in_=x_ctx[i * 128:(i + 1) * 128, :]),
                     writes=[xres], dma=("p0x", xres[1]))
                jt, jres = jr.next()
                P.op("act", lambda e, xt=xt, jt=jt, i=i: e.activation(out=jt[:], in_=xt[:], func=AF.Square,
                                                                    accum_out=ss[:, i:i + 1]),
                     reads=[xres], writes=[jres, ("ss", i)])
                P.op("dve", lambda e, i=i: e.tensor_scalar(out=rs[:, i:i + 1], in0=ss[:, i:i + 1], scalar1=1.0 / D,
                                                           scalar2=RMS_EPS, op0=ALU.mult, op1=ALU.add),
                     reads=[("ss", i)], writes=[("rs", i)])
                P.op("act", lambda e, i=i: e.activation(out=rs[:, i:i + 1], in_=rs[:, i:i + 1], func=AF.Sqrt),
                     reads=[("rs", i)], writes=[("rs", i)])
                P.op("dve", lambda e, i=i: e.reciprocal(out=rstd[:, i:i + 1], in_=rs[:, i:i + 1]),
                     reads=[("rs", i)], writes=[("rstd", i)])
                xb, xbres = xbr.next()
                P.op("dve", lambda e, xt=xt, xb=xb, i=i: e.tensor_scalar(out=xb[:], in0=xt[:], scalar1=rstd[:, i:i + 1],
                                                                        scalar2=None, op0=ALU.mult),
                     reads=[xres, ("rstd", i)], writes=[xbres])
                if i % 4 == 0:
                    xtt, xtres = xtr.next()
                for half in range(NCH // 8):
                    tp, tpres = tpr.next()
                    for j in range(8):
                        c = half * 8 + j
                        P.op("pe", lambda e, tp=tp, xb=xb, j=j, c=c: e.transpose(
                            out=tp[:, j * 128:(j + 1) * 128], in_=xb[:, c * 128:(c + 1) * 128], identity=ident_b[:]),
                            reads=[xbres, "ident_b"], writes=[tpres])
                    eng = "act" if half % 2 == 0 else "dve"
                    o_ap = xtt[:, half * 8:(half + 1) * 8, (i % 4) * 128:(i % 4 + 1) * 128]
                    i_ap = tp[:].rearrange("p (c t) -> p c t", c=8)
                    if eng == "act":
                        P.op("act", lambda e, o_ap=o_ap, i_ap=i_ap: e.activation(out=o_ap, in_=i_ap, func=AF.Copy),
                             reads=[tpres], writes=[xtres])
                    else:
                        P.op("dve", lambda e, o_ap=o_ap, i_ap=i_ap: e.tensor_copy(out=o_ap, in_=i_ap),
                             reads=[tpres], writes=[xtres])
                if i % 4 == 3:
                    g = i // 4
                    dst = AP(xnT, g * 512, [[TC, 128], [128 * TC, NCH], [1, 512]])
                    P.op("sp", lambda e, dst=dst, xtt=xtt: e.dma_start(out=dst, in_=xtt[:]),
                         reads=[xtres], writes=[("xnT", g)], dma=("p0st", xtres[1]))
            P.barrier()
            P.emit()
            P.new_phase()

        if upto <= 0:
            return nc

        with ExitStack() as st:
            wst = Ring(nc, st, "p1ws", 3, [128, 1024], F32)
            wgr = Ring(nc, st, "p1wg", 1, [128, NCH, 1024], BF16)
            xgr = Ring(nc, st, "p1xg", 2, [128, NCH, 512], BF16)
            psr = Ring(nc, st, "p1ps", 4, [128, 512], F32, psum=True)
            tpr = Ring(nc, st, "p1tp", 2, [128, 1024], BF16, psum=True)
            obr = Ring(nc, st, "p1ob", 3, [128, 512], BF16)
            vtr = Ring(nc, st, "p1vt", 2, [128, NH, 65], BF16)
            tmr = Ring(nc, st, "p1tm", 2, [128, 1024], F32)
            rbr = Ring(nc, st, "p1rb", 2, [128, 1024], BF16)
            dtr = Ring(nc, st, "p1dt", 2, [128, 8, 512], BF16)
            gat = sb(st, "p1gat", [128, NCH], F32)
            kv_sb = sb(st, "p1kv", [128, NCB], F32)
            cos_sb = sb(st, "p1cos", [128, NCB * 8], F32)
            sin_sb = sb(st, "p1sin", [128, NCB * 8], F32)
            ra = sb(st, "p1ra", [128, NH, 8], F32)
            rb_ = sb(st, "p1rb_", [128, NH, 8], F32)
            fb_sb = sb(st, "p1fb", [16, 1], F32)
            nfb = sb(st, "p1nfb", [16, 1], F32)
            negc_fm = sb(st, "p1negc", [16, TC], F32)
            ones16 = sb(st, "p1ones", [16, 512], F32)
            onesb = sb(st, "p1onesb", [16, 512], BF16)
            spt = sb(st, "p1spt", [16, 512], F32)
            rqr = Ring(nc, st, "p1rq", 2, [16, 512], BF16)
            for (t_, src, key) in ((gat, g_attn, "gat"), (kv_sb, kvalid, "kv"), (cos_sb, rope_cos, "cos"),
                                   (sin_sb, rope_sin, "sin"), (fb_sb, fbias, "fb")):
                P.op("sp", lambda e, t_=t_, src=src: e.dma_start(out=t_[:], in_=src.ap()), writes=[key], dma="c0")
            P.op("dve", lambda e: e.tensor_scalar(out=nfb[:], in0=fb_sb[:], scalar1=-1.0, scalar2=None, op0=ALU.mult),
                 reads=["fb"], writes=["nfb"])
            P.op("pool", lambda e: e.memset(ones16[:], 1.0), writes=["ones16"])
            P.op("pool", lambda e: e.memset(onesb[:], 1.0), writes=["onesb"])

            groups = [("fq", 0, 1024, "fm", OTG0), ("fk", 1024, 1024, "fm", 0), ("fv", 2048, 1024, "tmv", 0),
                      ("fg", 3072, 16, "fg", 0), ("dq", 3088, 1024, "tmr", OTG0), ("dk", 4112, 1024, "tmr", 0),
                      ("dv", 5136, 1024, "tmv", 0)]
            for (gname, col0, ncols, mode, tg0) in groups:
                wg, wgres = wgr.next()
                for c in range(NCH):
                    ws, wsres = wst.next()
                    P.op("sp" if c % 2 == 0 else "act",
                         lambda e, ws=ws, c=c, col0=col0, ncols=ncols: e.dma_start(
                             out=ws[:, 0:ncols], in_=w_in[c * 128:(c + 1) * 128, col0:col0 + ncols]),
                         writes=[wsres], dma=("p1w", wsres[1]))
                    if c % 2 == 0:
                        P.op("act", lambda e, ws=ws, wg=wg, c=c, ncols=ncols: e.activation(
                            out=wg[:, c, 0:ncols], in_=ws[:, 0:ncols], func=AF.Copy, scale=gat[:, c:c + 1]),
                            reads=[wsres, "gat"], writes=[wgres])
                    else:
                        P.op("dve", lambda e, ws=ws, wg=wg, c=c, ncols=ncols: e.tensor_scalar(
                            out=wg[:, c, 0:ncols], in0=ws[:, 0:ncols], scalar1=gat[:, c:c + 1], scalar2=None,
                            op0=ALU.mult),
                            reads=[wsres, "gat"], writes=[wgres])
                for tg in range(tg0, NTG):
                    xg, xgres = xgr.next()
                    src = AP(xnT, tg * 512, [[TC, 128], [128 * TC, NCH], [1, 512]])
                    P.op("sp", lambda e, xg=xg, src=src: e.dma_start(out=xg[:], in_=src),
                         reads=[("xnT", tg)], writes=[xgres], dma=("p1x", xgres[1]))
                    otg = tg - OTG0
                    if gname == "fk":
                        P.op("act", lambda e, tg=tg: e.dma_start(
                            out=AP(kfT, 64 * TC + tg * 512, [[65 * TC, 16], [1, 512]]), in_=onesb[:]),
                            reads=["onesb"], writes=[("kaug", tg)], dma="kaug")
                    if mode == "fm":
                        dstT, TT, toff, scl = (qfT, TO, otg * 512, 0.125) if gname == "fq" else (kfT, TC, tg * 512, 1.0)
                        for sub in range(8):
                            ps, psres = psr.next()
                            for c in range(NCH):
                                P.op("pe", lambda e, ps=ps, wg=wg, xg=xg, c=c, sub=sub: e.matmul(
                                    ps[:], lhsT=wg[:, c, sub * 128:(sub + 1) * 128], rhs=xg[:, c, :],
                                    start=(c == 0), stop=(c == NCH - 1)),
                                    reads=[wgres, xgres], writes=[psres])
                            ob, obres = obr.next()
                            P.op("act", lambda e, ob=ob, ps=ps, scl=scl: e.activation(out=ob[:], in_=ps[:], func=AF.Copy,
                                                                                      scale=scl),
                                 reads=[psres], writes=[obres])
                            for hh in range(2):
                                h = 2 * sub + hh
                                dst = AP(dstT, h * 65 * TT + toff, [[TT, 64], [1, 512]])
                                P.op("sp" if hh == 0 else "act", lambda e, dst=dst, ob=ob, hh=hh: e.dma_start(
                                    out=dst, in_=ob[hh * 64:(hh + 1) * 64, :]),
                                    reads=[obres], writes=[(gname, h, tg)], dma=("p1o", obres[1], hh))
                    elif mode == "fg":
                        ps, psres = psr.next()
                        for c in range(NCH):
                            P.op("pe", lambda e, ps=ps, wg=wg, xg=xg, c=c: e.matmul(
                                ps[0:16, :], lhsT=wg[:, c, 0:16], rhs=xg[:, c, :], start=(c == 0), stop=(c == NCH - 1)),
                                reads=[wgres, xgres], writes=[psres])
                        P.op("act", lambda e, ps=ps: e.activation(out=spt[:], in_=ps[0:16, :], func=AF.Exp, scale=-1.0,
                                                                  bias=nfb[:]),
                             reads=[psres, "nfb"], writes=["spt"])
                        P.op("act", lambda e: e.activation(out=spt[:], in_=spt[:], func=AF.Ln, bias=1.0),
                             reads=["spt"], writes=["spt"])
                        init = 0.0 if tg == 0 else negc_fm[:, tg * 512 - 1:tg * 512]
                        P.op("dve", lambda e, tg=tg, init=init: e.tensor_tensor_scan(
                            out=negc_fm[:, tg * 512:(tg + 1) * 512], data0=ones16[:], data1=spt[:], initial=init,
                            op0=ALU.mult, op1=ALU.add),
                            reads=["spt", "ones16", "negc_fm"], writes=["negc_fm"])
                        if tg >= OTG0:
                            rq, rqres = rqr.next()
                            P.op("dve", lambda e, rq=rq, tg=tg: e.tensor_scalar(
                                out=rq[:], in0=negc_fm[:, tg * 512:(tg + 1) * 512], scalar1=-1.0, scalar2=None,
                                op0=ALU.mult), reads=["negc_fm"], writes=[rqres])
                            P.op("act", lambda e, rq=rq, otg=otg: e.dma_start(
                                out=AP(qfT, 64 * TO + otg * 512, [[65 * TO, 16], [1, 512]]), in_=rq[:]),
                                reads=[rqres], writes=[("qaug", otg)], dma=("qaug", rqres[1]))
                    else:
                        for ti in range(4):
                            blk = tg * 4 + ti
                            if mode == "tmv":
                                vt, vtres = vtr.next()
                            else:
                                tm, tmres = tmr.next()
                            for half in range(2):
                                ps, psres = psr.next()
                                for c in range(NCH):
                                    P.op("pe", lambda e, ps=ps, wg=wg, xg=xg, c=c, ti=ti, half=half: e.matmul(
                                        ps[:], lhsT=xg[:, c, ti * 128:(ti + 1) * 128],
                                        rhs=wg[:, c, half * 512:(half + 1) * 512], start=(c == 0), stop=(c == NCH - 1)),
                                        reads=[wgres, xgres], writes=[psres])
                                if mode == "tmv":
                                    P.op("act", lambda e, vt=vt, ps=ps, half=half: e.activation(
                                        out=vt[:, half * 8:(half + 1) * 8, 0:64],
                                        in_=ps[:].rearrange("p (h d) -> p h d", h=8), func=AF.Copy),
                                        reads=[psres], writes=[vtres])
                                else:
                                    scl = 0.125 if gname == "dq" else 1.0
                                    P.op("act", lambda e, tm=tm, ps=ps, half=half, scl=scl: e.activation(
                                        out=tm[:, half * 512:(half + 1) * 512], in_=ps[:], func=AF.Copy, scale=scl),
                                        reads=[psres], writes=[tmres])
                            if mode == "tmv":
                                P.op("pool", lambda e, vt=vt, blk=blk: e.tensor_copy(
                                    out=vt[:, :, 64:65], in_=AP(kv_sb, blk, [[NCB, 128], [0, NH], [1, 1]])),
                                    reads=["kv"], writes=[vtres])
                                dstv = vf if gname == "fv" else vd
                                dst = AP(dstv, blk * NH * 65, [[NCB * NH * 65, 128], [1, NH * 65]])
                                P.op("sp", lambda e, dst=dst, vt=vt: e.dma_start(
                                    out=dst, in_=vt[:].rearrange("p h d -> p (h d)")),
                                    reads=[vtres], writes=[(gname, blk)], dma=("p1v", vtres[1]))
                            else:
                                rbt, rbres = rbr.next()
                                P.op("act", lambda e, rbt=rbt, tm=tm: e.activation(out=rbt[:], in_=tm[:], func=AF.Copy),
                                     reads=[tmres], writes=[rbres])
                                tm3 = tm[:].rearrange("p (h d) -> p h d", h=NH)
                                rb3 = rbt[:].rearrange("p (h d) -> p h d", h=NH)
                                x1, x2 = tm3[:, :, 0:8], tm3[:, :, 8:16]
                                cosb = AP(cos_sb, blk * 8, [[NCB * 8, 128], [0, NH], [1, 8]])
                                sinb = AP(sin_sb, blk * 8, [[NCB * 8, 128], [0, NH], [1, 8]])
                                for (o_, a_, ca, b_, cb, opx) in ((rb3[:, :, 0:8], x1, cosb, x2, sinb, ALU.subtract),
                                                                 (rb3[:, :, 8:16], x2, cosb, x1, sinb, ALU.add)):
                                    P.op("dve", lambda e, a_=a_, ca=ca: e.tensor_tensor(out=ra[:], in0=a_, in1=ca, op=ALU.mult),
                                         reads=[tmres, "cos"], writes=["ra"])
                                    P.op("dve", lambda e, b_=b_, cb=cb: e.tensor_tensor(out=rb_[:], in0=b_, in1=cb, op=ALU.mult),
                                         reads=[tmres, "sin"], writes=["rb_"])
                                    P.op("dve", lambda e, o_=o_, opx=opx: e.tensor_tensor(out=o_, in0=ra[:], in1=rb_[:], op=opx),
                                         reads=["ra", "rb_"], writes=[rbres])
                                if ti == 0:
                                    dt_, dtres = dtr.next()
                                tp, tpres = tpr.next()
                                for j in range(8):
                                    P.op("pe", lambda e, tp=tp, rbt=rbt, j=j: e.transpose(
                                        out=tp[:, j * 128:(j + 1) * 128], in_=rbt[:, j * 128:(j + 1) * 128],
                                        identity=ident_b[:]),
                                        reads=[rbres, "ident_b"], writes=[tpres])
                                P.op("dve", lambda e, dt_=dt_, tp=tp, ti=ti: e.tensor_copy(
                                    out=dt_[:, :, ti * 128:(ti + 1) * 128], in_=tp[:].rearrange("p (j t) -> p j t", j=8)),
                                    reads=[tpres], writes=[dtres])
                                if ti == 3:
                                    dstT, TT, toff = (qdT, TO, otg * 512) if gname == "dq" else (kdT, TC, tg * 512)
                                    dst = AP(dstT, toff, [[TT, 128], [128 * TT, 8], [1, 512]])
                                    P.op("sp", lambda e, dst=dst, dt_=dt_: e.dma_start(out=dst, in_=dt_[:]),
                                         reads=[dtres], writes=[(gname, tg)], dma=("p1d", dtres[1]))
                if mode == "fg":
                    for b0 in range(0, NCB, 32):
                        ps, psres = psr.next()
                        nb = min(32, NCB - b0)
                        for bb in range(nb):
                            P.op("pe", lambda e, ps=ps, b0=b0, bb=bb: e.transpose(
                                out=ps[:, bb * 16:(bb + 1) * 16], in_=negc_fm[:, (b0 + bb) * 128:(b0 + bb + 1) * 128],
                                identity=ident_f[0:16, 0:16]),
                                reads=["negc_fm", "ident_f"], writes=[psres])
                        P.op("dve", lambda e, ps=ps, b0=b0, nb=nb: e.tensor_copy(
                            out=negc_tm[:, b0:b0 + nb, :], in_=ps[:, 0:nb * 16].rearrange("p (b h) -> p b h", h=16)),
                            reads=[psres], writes=["negc_tm"])
            P.barrier()
            P.emit()
            P.new_phase()
        if upto <= 1:
            return nc

        for kind in ("fox", "dil"):
            with ExitStack() as st:
                KR = 65 if kind == "fox" else 64
                qr = Ring(nc, st, kind + "q", 2, [65, TO], BF16)
                kr = Ring(nc, st, kind + "k", 2, [65, TC], BF16)
                if kind == "dil":
                    for r_ in range(2):
                        P.op("pool", lambda e, r_=r_: e.memset(qr.t[r_][64:65, :], 0.0), writes=[(qr.name, r_)])
                        P.op("pool", lambda e, r_=r_: e.memset(kr.t[r_][64:65, :], 0.0), writes=[(kr.name, r_)])
                vr = Ring(nc, st, kind + "v", 2, [128, NCB, 4 * 65], BF16)
                sr = Ring(nc, st, kind + "s", 4, [128, 512], F32, psum=True)
                otr = Ring(nc, st, kind + "ot", 2, [128, 512], F32, psum=True)
                tor = Ring(nc, st, kind + "to", 2, [128, 4, 128], F32, psum=True)
                ptr_ = Ring(nc, st, kind + "pt", 4, [128, 512], BF16)
                osr = Ring(nc, st, kind + "os", 2, [65, 512], F32)
                rcr = Ring(nc, st, kind + "rc", 2, [128, 4], F32)
                ofr = Ring(nc, st, kind + "of", 2, [128, 4, 64], F32)
                cst = sb(st, kind + "cst", [128, 5 * 512], F32)
                if kind == "fox":
                    tri_b = sb(st, "tri_b", [128, 128], BF16)
                    P.op("sp", lambda e: e.dma_start(out=cst[:, 0:128], in_=tri_in.ap()), writes=["cst"], dma="c0")
                    P.op("dve", lambda e: e.tensor_copy(out=tri_b[:], in_=cst[:, 0:128]), reads=["cst"], writes=["tri_b"])
                else:
                    wcat_b = sb(st, "wcat_b", [128, 20, 512], BF16)
                    for pc in range(4):
                        P.op("sp", lambda e, pc=pc: e.dma_start(out=cst[:], in_=wcat_in[:, pc * 2560:(pc + 1) * 2560]),
                             writes=["cst"], dma="c0")
                        P.op("dve", lambda e, pc=pc: e.tensor_copy(
                            out=wcat_b[:, pc * 5:(pc + 1) * 5, :], in_=cst[:].rearrange("p (a b) -> p a b", a=5)),
                            reads=["cst"], writes=["wcat_b"])
                qsrc, ksrc, vsrc = (qfT, kfT, vf) if kind == "fox" else (qdT, kdT, vd)
                coff = 0 if kind == "fox" else 1024
                LA = 3
                units = []
                loads = {}
                head_starts = []
                for hg in range(4):
                    for hh in range(4):
                        h = hg * 4 + hh
                        head_starts.append((len(units), hg, hh))
                        for g in range(NOB // 4):
                            qb0 = NPB + 4 * g
                            kb_lo = 0 if kind == "fox" else max(0, qb0 - 16)
                            kbs = list(range(kb_lo, qb0 + 4))
                            for kb in kbs:
                                j = kb - qb0
                                q0 = (max(0, j) * 128) if kind == "fox" else 0
                                units.append((h, hh, g, kb, q0, kb == kbs[0], kb == kbs[-1], kb - (qb0 - 16), j))
                for n_, (u0, hg_, hh_) in enumerate(head_starts):
                    loads.setdefault(0 if n_ == 0 else head_starts[n_ - 1][0], []).append((hg_, hh_))
                cur = {}
                hstate = {}
                sinfo = {}
                deferred = []
                state = {"vg": None}

                def issue_loads(hg, hh):
                    h = hg * 4 + hh
                    if hh == 0:
                        vg, vgres = vr.next()
                        nvp = max(1, NCB // 16)
                        for vp in range(nvp):
                            nb_ = NCB // nvp
                            P.op("act", lambda e, vg=vg, hg=hg, vp=vp, nb_=nb_: e.dma_start(
                                out=vg[:, vp * nb_:(vp + 1) * nb_, :],
                                in_=AP(vsrc, hg * 4 * 65 + vp * nb_ * NH * 65,
                                       [[NCB * NH * 65, 128], [NH * 65, nb_], [1, 4 * 65]])),
                                writes=[vgres], dma=(kind + "v", vgres[1], vp))
                        state["vg"] = (vg, vgres)
                    qT, qres = qr.next()
                    kT, kres = kr.next()
                    P.op("sp", lambda e, qT=qT, h=h: e.dma_start(
                        out=qT[0:KR, :], in_=AP(qsrc, h * KR * TO, [[TO, KR], [1, TO]])),
                        writes=[qres], dma=(kind + "q", qres[1]))
                    P.op("sp", lambda e, kT=kT, h=h: e.dma_start(
                        out=kT[0:KR, :], in_=AP(ksrc, h * KR * TC, [[TC, KR], [1, TC]])),
                        writes=[kres], dma=(kind + "k", kres[1]))
                    hstate[h] = (qT, qres, kT, kres, state["vg"][0], state["vg"][1])

                def emit_S(ui):
                    (h, hh, g, kb, q0, first, last, ko, j) = units[ui]
                    for (hg_, hh_) in loads.get(ui, []):
                        issue_loads(hg_, hh_)
                    qT, qres, kT, kres, vg, vgres = hstate[h]
                    S, sres = sr.next()
                    sinfo[ui] = (S, sres)
                    P.op("pe", lambda e, S=S, kT=kT, qT=qT, kb=kb, g=g, q0=q0: e.matmul(
                        S[:, q0:512], lhsT=kT[:, kb * 128:(kb + 1) * 128],
                        rhs=qT[:, g * 512 + q0:(g + 1) * 512], start=True, stop=True),
                        reads=[kres, qres], writes=[sres])

                def finalize2(osb, osres, g, h):
                    to, tores = tor.next()
                    for jj in range(4):
                        P.op("pe", lambda e, to=to, osb=osb, jj=jj: e.transpose(
                            out=to[:, jj, 0:65], in_=osb[:, jj * 128:(jj + 1) * 128], identity=ident_f[0:65, 0:65]),
                            reads=[osres, "ident_f"], writes=[tores])
                    rc, rcres = rcr.next()
                    of, ofres = ofr.next()
                    P.op("dve", lambda e, rc=rc, to=to: e.reciprocal(out=rc[:], in_=to[:, :, 64]),
                         reads=[tores], writes=[rcres])
                    P.op("dve", lambda e, rc=rc, to=to, of=of: e.tensor_tensor(
                        out=of[:], in0=to[:, :, 0:64], in1=AP(rc, 0, [[4, 128], [1, 4], [0, 64]]), op=ALU.mult),
                        reads=[tores, rcres], writes=[ofres])
                    P.op("sp", lambda e, of=of, g=g, h=h: e.dma_start(
                        out=AP(attn_o, g * 512 * D + coff + h * 64, [[D, 128], [128 * D, 4], [1, 64]]), in_=of[:]),
                        reads=[ofres], writes=[("attn_o", g)], dma=(kind + "o", ofres[1]))

                def emit_rest(ui):
                    (h, hh, g, kb, q0, first, last, ko, j) = units[ui]
                    qT, qres, kT, kres, vg, vgres = hstate[h]
                    S, sres = sinfo.pop(ui)
                    if first:
                        cur["ot"] = otr.next()
                    ot, otres = cur["ot"]
                    Pt, ptres = ptr_.next()
                    if kind == "fox":
                        P.op("act", lambda e, Pt=Pt, S=S, q0=q0, kb=kb, h=h: e.activation(
                            out=Pt[:, q0:512], in_=S[:, q0:512], func=AF.Exp, bias=negc_tm[:, kb, h:h + 1]),
                            reads=[sres, "negc_tm"], writes=[ptres])
                        if j >= 0:
                            P.op("dve", lambda e, Pt=Pt, q0=q0: e.tensor_tensor(
                                out=Pt[:, q0:q0 + 128], in0=Pt[:, q0:q0 + 128], in1=tri_b[:], op=ALU.mult),
                                reads=[ptres, "tri_b"], writes=[ptres])
                    else:
                        P.op("act", lambda e, Pt=Pt, S=S: e.activation(out=Pt[:], in_=S[:], func=AF.Exp),
                             reads=[sres], writes=[ptres])
                        P.op("dve", lambda e, Pt=Pt, ko=ko: e.tensor_tensor(
                            out=Pt[:], in0=Pt[:], in1=wcat_b[:, ko, :], op=ALU.mult),
                            reads=[ptres, "wcat_b"], writes=[ptres])
                    P.op("pe", lambda e, ot=ot, vg=vg, Pt=Pt, kb=kb, hh=hh, q0=q0, first=first, last=last: e.matmul(
                        ot[0:65, q0:512], lhsT=vg[:, kb, hh * 65:(hh + 1) * 65], rhs=Pt[:, q0:512],
                        start=first, stop=last),
                        reads=[vgres, ptres], writes=[otres])
                    if last:
                        osb, osres = osr.next()
                        P.op("act", lambda e, osb=osb, ot=ot: e.activation(out=osb[:], in_=ot[0:65, :], func=AF.Copy),
                             reads=[otres], writes=[osres])
                        deferred.append((ui + 3, lambda osb=osb, osres=osres, g=g, h=h: finalize2(osb, osres, g, h)))

                NU = len(units)
                for idx in range(NU + LA):
                    if idx < NU:
                        emit_S(idx)
                    jdx = idx - LA
                    if jdx >= 0:
                        emit_rest(jdx)
                        while deferred and deferred[0][0] <= jdx:
                            deferred.pop(0)[1]()
                while deferred:
                    deferred.pop(0)[1]()
                P.barrier()
                P.emit()
                P.new_phase()
        if upto <= 2:
            return nc

        with ExitStack() as st:
            wst = Ring(nc, st, "p4ws", 2, [128, D], F32)
            wo_b = sb(st, "p4wo", [128, NCH, D], BF16)
            gmx = sb(st, "p4gm", [128, NCH], F32)
            atr = Ring(nc, st, "p4at", 2, [128, D], F32)
            xr = Ring(nc, st, "p4x", 2, [128, D], F32)
            jr = Ring(nc, st, "p4j", 1, [128, 1024], BF16)
            mxr = Ring(nc, st, "p4mx", 2, [128, D], BF16)
            mtr = Ring(nc, st, "p4mt", 2, [128, NCH, 128], BF16)
            htr = Ring(nc, st, "p4ht", 2, [128, D], F32)
            tpr = Ring(nc, st, "p4tp", 2, [128, 1024], BF16, psum=True)
            psr = Ring(nc, st, "p4ps", 4, [128, 512], F32, psum=True)
            ss = sb(st, "p4ss", [128, 2 * NOB], F32)
            rs = sb(st, "p4rs", [128, 2 * NOB], F32)
            rstd = sb(st, "p4rstd", [128, 2 * NOB], F32)
            P.op("sp", lambda e: e.dma_start(out=gmx[:], in_=g_mix.ap()), writes=["gmx"], dma="c0")
            for c in range(NCH):
                ws, wsres = wst.next()
                P.op("sp" if c % 2 == 0 else "act", lambda e, ws=ws, c=c: e.dma_start(
                    out=ws[:], in_=w_out[c * 128:(c + 1) * 128, :]), writes=[wsres], dma=("p4w", wsres[1]))
                if c % 2 == 0:
                    P.op("act", lambda e, ws=ws, c=c: e.activation(out=wo_b[:, c, :], in_=ws[:], func=AF.Copy,
                                                                  scale=gmx[:, c:c + 1]),
                         reads=[wsres, "gmx"], writes=["wo_b"])
                else:
                    P.op("dve", lambda e, ws=ws, c=c: e.tensor_scalar(
                        out=wo_b[:, c, :], in0=ws[:], scalar1=gmx[:, c:c + 1], scalar2=None, op0=ALU.mult),
                        reads=[wsres, "gmx"], writes=["wo_b"])
            for i in range(NOB):
                at, atres = atr.next()
                xt, xres = xr.next()
                P.op("sp", lambda e, at=at, i=i: e.dma_start(out=at[:], in_=attn_o[i * 128:(i + 1) * 128, :]),
                     reads=[("attn_o", i // 4)], writes=[atres], dma=("p4a", atres[1]))
                P.op("act", lambda e, xt=xt, i=i: e.dma_start(out=xt[:], in_=x_ctx[(NPB + i) * 128:(NPB + i + 1) * 128, :]),
                     writes=[xres], dma=("p4x", xres[1]))
                mx, mxres = mxr.next()
                for k in range(2):
                    col = 2 * i + k
                    jt, jres = jr.next()
                    P.op("act", lambda e, jt=jt, at=at, k=k, col=col: e.activation(
                        out=jt[:], in_=at[:, k * 1024:(k + 1) * 1024], func=AF.Square, accum_out=ss[:, col:col + 1]),
                        reads=[atres], writes=[jres, ("p4ss", col)])
                    P.op("dve", lambda e, col=col: e.tensor_scalar(out=rs[:, col:col + 1], in0=ss[:, col:col + 1],
                                                                 scalar1=1.0 / 1024, scalar2=RMS_EPS, op0=ALU.mult, op1=ALU.add),
                         reads=[("p4ss", col)], writes=[("p4rs", col)])
                    P.op("act", lambda e, col=col: e.activation(out=rs[:, col:col + 1], in_=rs[:, col:col + 1], func=AF.Sqrt),
                         reads=[("p4rs", col)], writes=[("p4rs", col)])
                    P.op("dve", lambda e, col=col: e.reciprocal(out=rstd[:, col:col + 1], in_=rs[:, col:col + 1]),
                         reads=[("p4rs", col)], writes=[("p4rstd", col)])
                    P.op("dve", lambda e, mx=mx, at=at, k=k, col=col: e.tensor_scalar(
                        out=mx[:, k * 1024:(k + 1) * 1024], in0=at[:, k * 1024:(k + 1) * 1024],
                        scalar1=rstd[:, col:col + 1], scalar2=None, op0=ALU.mult),
                        reads=[atres, ("p4rstd", col)], writes=[mxres])
                mt, mtres = mtr.next()
                for half in range(NCH // 8):
                    tp, tpres = tpr.next()
                    for j in range(8):
                        c = half * 8 + j
                        P.op("pe", lambda e, tp=tp, mx=mx, j=j, c=c: e.transpose(
                            out=tp[:, j * 128:(j + 1) * 128], in_=mx[:, c * 128:(c + 1) * 128], identity=ident_b[:]),
                            reads=[mxres, "ident_b"], writes=[tpres])
                    P.op("act", lambda e, mt=mt, tp=tp, half=half: e.activation(
                        out=mt[:, half * 8:(half + 1) * 8, :], in_=tp[:].rearrange("p (c t) -> p c t", c=8), func=AF.Copy),
                        reads=[tpres], writes=[mtres])
                ht, htres = htr.next()
                for n in range(D // 512):
                    ps, psres = psr.next()
                    for c in range(NCH):
                        P.op("pe", lambda e, ps=ps, mt=mt, c=c, n=n: e.matmul(
                            ps[:], lhsT=mt[:, c, :], rhs=wo_b[:, c, n * 512:(n + 1) * 512], start=(c == 0),
                            stop=(c == NCH - 1)), reads=[mtres, "wo_b"], writes=[psres])
                    P.op("dve", lambda e, ht=ht, xt=xt, ps=ps, n=n: e.tensor_tensor(
                        out=ht[:, n * 512:(n + 1) * 512], in0=xt[:, n * 512:(n + 1) * 512], in1=ps[:], op=ALU.add),
                        reads=[xres, psres], writes=[htres])
                P.op("sp", lambda e, ht=ht, i=i: e.dma_start(out=hbuf[i * 128:(i + 1) * 128, :], in_=ht[:]),
                     reads=[htres], writes=[("hbuf", i)], dma=("p4h", htres[1]))
            P.barrier()
            P.emit()
            P.new_phase()
        if upto <= 3:
            return nc

        with ExitStack() as st:
            cir = Ring(nc, st, "cvi", 2, [128, 4 * D], F32)
            cor = Ring(nc, st, "cvo", 2, [128, 4 * D], BF16)
            n_ = 0
            for (src_t, tbl) in ((p_down, 0), (p_up, 1)):
                for ch in range(NEXP // 512):
                    ci, cires = cir.next()
                    co, cores = cor.next()
                    P.op("sp" if n_ % 2 == 0 else "act", lambda e, ci=ci, src_t=src_t, ch=ch: e.dma_start(
                        out=ci[:], in_=AP(src_t, ch * 512 * D, [[4 * D, 128], [1, 4 * D]])),
                        writes=[cires], dma=("cvl", cires[1]))
                    eng = ("act", "dve", "pool")[n_ % 3]
                    if eng == "act":
                        P.op("act", lambda e, ci=ci, co=co: e.activation(out=co[:], in_=ci[:], func=AF.Copy),
                             reads=[cires], writes=[cores])
                    else:
                        P.op(eng, lambda e, ci=ci, co=co: e.tensor_copy(out=co[:], in_=ci[:]),
                             reads=[cires], writes=[cores])
                    P.op("sp" if n_ % 2 == 1 else "act", lambda e, co=co, tbl=tbl, ch=ch: e.dma_start(
                        out=AP(ctab, ch * 512 * 2 * D + tbl * D, [[8 * D, 128], [2 * D, 4], [1, D]]),
                        in_=co[:].rearrange("p (j d) -> p j d", j=4)),
                        reads=[cores], writes=[("cv", n_)], dma=("cvs", cores[1]))
                    n_ += 1
            P.barrier()
            P.emit()
            P.new_phase()

        with ExitStack() as st:
            NS = 128
            wq_b = sb(st, "p5wq", [128, NCH, 2048], BF16)
            sk_b = sb(st, "p5sk", [128, 16, 128], BF16)
            gffn = sb(st, "p5gf", [128, D], F32)
            gfin = sb(st, "p5gn", [128, D], F32)
            io_i = sb(st, "p5ioi", [128, 16], I32)
            io_f = sb(st, "p5iof", [128, 16], F32)

            for (t_, src, key) in ((gffn, g_ffn, "gffn"), (gfin, g_fin, "gfin")):
                P.op("sp", lambda e, t_=t_, src=src: e.dma_start(out=t_[:], in_=src.ap()), writes=[key], dma="c0")
            P.op("pool", lambda e: e.iota(io_i[:], [[1, 16]], base=0, channel_multiplier=0), writes=["io_i"])
            P.op("dve", lambda e: e.tensor_copy(out=io_f[:], in_=io_i[:]), reads=["io_i"], writes=["io_f"])
            st2 = ExitStack()
            wst = Ring(nc, st2, "p5ws", 2, [128, D], F32)
            for c in range(NCH):
                ws, wsres = wst.next()
                P.op("sp" if c % 2 == 0 else "act", lambda e, ws=ws, c=c: e.dma_start(
                    out=ws[:], in_=w_pq[c * 128:(c + 1) * 128, :]), writes=[wsres], dma=("p5w", wsres[1]))
                if c % 2 == 0:
                    P.op("act", lambda e, ws=ws, c=c: e.activation(out=wq_b[:, c, :], in_=ws[:], func=AF.Copy),
                         reads=[wsres], writes=["wq_b"])
                else:
                    P.op("dve", lambda e, ws=ws, c=c: e.tensor_copy(out=wq_b[:, c, :], in_=ws[:]),
                         reads=[wsres], writes=["wq_b"])
            ws, wsres = wst.next()
            P.op("sp", lambda e, ws=ws: e.dma_start(out=ws[:], in_=skT_in.ap()), writes=[wsres], dma=("p5w", wsres[1]))
            P.op("dve", lambda e, ws=ws: e.tensor_copy(out=sk_b[:], in_=ws[:].rearrange("p (g n) -> p g n", g=16)),
                 reads=[wsres], writes=["sk_b"])
            P.barrier()
            P.emit()
            P.new_phase()
            st2.close()
            hts = [sb(st, "p5ht%d" % k_, [128, D], F32) for k_ in range(2)]
            hnbs = [sb(st, "p5hnb%d" % k_, [128, D], BF16) for k_ in range(2)]
            hnT = sb(st, "p5hnT", [128, NCH, 128], BF16)
            qTs = sb(st, "p5qT", [128, 4, 128], BF16)
            scr_ = Ring(nc, st, "p5sc", 2, [128, 512], F32)
            scx = sb(st, "p5scx", [128, 128], F32)
            v16 = sb(st, "p5v16", [128, 16, 16], F32)
            i16u = sb(st, "p5i16u", [128, 16, 16], U32)
            i16f = sb(st, "p5i16f", [128, 16, 16], F32)
            work = sb(st, "p5work", [128, 2048], F32)
            cand = work[:].rearrange("p (h c) -> p h c", h=8)
            cscr = sb(st, "p5cscr", [128, 256], F32)
            tv = sb(st, "p5tv", [128, 8, 16], F32)
            tcu = sb(st, "p5tcu", [128, 8, 16], U32)
            abu = sb(st, "p5abu", [128, 2, 128], U32)
            abf = sb(st, "p5abf", [128, 2, 128], F32)
            eq = work[:].rearrange("p (h k a) -> p h k a", h=8, k=16)
            i12 = sb(st, "p5i12", [128, 2, 128], F32)
            eidf = sb(st, "p5eidf", [128, NS], F32)
            eidxs = [sb(st, "p5eidx%d" % k_, [128, NS], I32) for k_ in range(2)]
            gtss = [sb(st, "p5gts%d" % k_, [128, 8, 16], F32) for k_ in range(2)]
            gsum = sb(st, "p5gsum", [128, 8], F32)
            hdn = sb(st, "p5hdn", [128, NS], F32)
            zc = sb(st, "p5zc", [128, NS], F32)
            ur = Ring(nc, st, "p5u", 5, [128, 2 * D], BF16)
            junk = sb(st, "p5junk", [128, D], BF16)
            dgr = Ring(nc, st, "p5dg", 3, [128, 128], BF16)
            ot_ = work
            ss = sb(st, "p5ss", [128, 4], F32)
            tpr = Ring(nc, st, "p5tp", 1, [128, 1024], BF16, psum=True)
            mmr = Ring(nc, st, "p5mm", 3, [128, 512], F32, psum=True)
            ybk = [st.enter_context(nc.psum_tensor("p5y%d" % n, [128, 512], F32)) for n in range(4)]

            def rms_rstd(src_ap, col, key):
                jt = junk
                P.op("act", lambda e: e.activation(out=jt[:], in_=src_ap, func=AF.Square, accum_out=ss[:, col:col + 1]),
                     reads=[key], writes=[("p5ss", col)])
                P.op("dve", lambda e: e.tensor_scalar(out=ss[:, col:col + 1], in0=ss[:, col:col + 1], scalar1=1.0 / D,
                                                      scalar2=RMS_EPS, op0=ALU.mult, op1=ALU.add),
                     reads=[("p5ss", col)], writes=[("p5ss", col)])
                P.op("act", lambda e: e.activation(out=ss[:, col:col + 1], in_=ss[:, col:col + 1], func=AF.Sqrt),
                     reads=[("p5ss", col)], writes=[("p5ss", col)])
                P.op("dve", lambda e: e.reciprocal(out=ss[:, col + 2:col + 3], in_=ss[:, col:col + 1]),
                     reads=[("p5ss", col)], writes=[("p5ss", col + 2)])
                return ss[:, col + 2:col + 3], ("p5ss", col + 2)

            def top16(src_ap, scratch_ap, vout, iout, rkeys, skey, wkeys):
                P.op("dve", lambda e: e.max(out=vout[0], in_=src_ap), reads=rkeys, writes=[wkeys[0]])
                P.op("dve", lambda e: e.max_index(out=iout[0], in_max=vout[0], in_values=src_ap),
                     reads=rkeys + [wkeys[0]], writes=[wkeys[1]])
                P.op("dve", lambda e: e.match_replace(out=scratch_ap, in_to_replace=vout[0], in_values=src_ap,
                                                      imm_value=-1e30), reads=rkeys + [wkeys[0]], writes=[skey])
                P.op("dve", lambda e: e.max(out=vout[1], in_=scratch_ap), reads=[skey], writes=[wkeys[0]])
                P.op("dve", lambda e: e.max_index(out=iout[1], in_max=vout[1], in_values=scratch_ap),
                     reads=[skey, wkeys[0]], writes=[wkeys[1]])

            def tile_4b(i):
                ht, eidx, hnb, gts = hts[i % 2], eidxs[i % 2], hnbs[i % 2], gtss[i % 2]
                P.op("sp", lambda e, i=i: e.dma_start(out=ht[:], in_=hbuf[i * 128:(i + 1) * 128, :]),
                     reads=[("hbuf", i)], writes=[("ht", i % 2)], dma="p5h")
                rstd_ap, rkey = rms_rstd(ht[:], 0, ("ht", i % 2))
                P.op("dve", lambda e, rstd_ap=rstd_ap: e.scalar_tensor_tensor(
                    out=hnb[:], in0=ht[:], scalar=rstd_ap, in1=gffn[:], op0=ALU.mult, op1=ALU.mult),
                    reads=[("ht", i % 2), rkey, "gffn"], writes=[("hnb", i % 2)])
                for half in range(NCH // 8):
                    tp, tpres = tpr.next()
                    for j in range(8):
                        c = half * 8 + j
                        P.op("pe", lambda e, tp=tp, j=j, c=c: e.transpose(
                            out=tp[:, j * 128:(j + 1) * 128], in_=hnb[:, c * 128:(c + 1) * 128], identity=ident_b[:]),
                            reads=[("hnb", i % 2), "ident_b"], writes=[tpres])
                    P.op("act", lambda e, tp=tp, half=half: e.activation(
                        out=hnT[:, half * 8:(half + 1) * 8, :], in_=tp[:].rearrange("p (c t) -> p c t", c=8), func=AF.Copy),
                        reads=[tpres], writes=["hnT"])
                for qq in range(4):
                    mm, mmres = mmr.next()
                    for gi in range(4):
                        G = 4 * qq + gi
                        for c in range(NCH):
                            P.op("pe", lambda e, mm=mm, gi=gi, G=G, c=c: e.matmul(
                                mm[:, gi * 128:(gi + 1) * 128], lhsT=wq_b[:, c, G * 128:(G + 1) * 128], rhs=hnT[:, c, :],
                                start=(c == 0), stop=(c == NCH - 1)), reads=["wq_b", "hnT"], writes=[mmres])
                    P.op("act", lambda e, mm=mm: e.activation(out=qTs[:], in_=mm[:].rearrange("p (g t) -> p g t", g=4),
                                                              func=AF.Copy), reads=[mmres], writes=["qTs"])
                    mm2, mm2res = mmr.next()
                    for gi in range(4):
                        G = 4 * qq + gi
                        P.op("pe", lambda e, mm2=mm2, gi=gi, G=G: e.matmul(
                            mm2[:, gi * 128:(gi + 1) * 128], lhsT=qTs[:, gi, :], rhs=sk_b[:, G, :], start=True, stop=True),
                            reads=["qTs", "sk_b"], writes=[mm2res])
                    sc, scres = scr_.next()
                    P.op("act", lambda e, sc=sc, mm2=mm2: e.activation(out=sc[:], in_=mm2[:], func=AF.Copy),
                         reads=[mm2res], writes=[scres])
                    for gi in range(4):
                        G = 4 * qq + gi
                        top16(sc[:, gi * 128:(gi + 1) * 128], scx[:], (v16[:, G, 0:8], v16[:, G, 8:16]),
                              (i16u[:, G, 0:8], i16u[:, G, 8:16]), [scres], "scx", ["v16", "i16u"])
                P.op("dve", lambda e: e.tensor_copy(out=i16f[:], in_=i16u[:]), reads=["i16u"], writes=["i16f"])
                P.op("dve", lambda e: e.tensor_tensor(
                    out=cand.rearrange("p h (a b) -> p h a b", a=16),
                    in0=AP(v16, 0, [[256, 128], [32, 8], [1, 16], [0, 16]]),
                    in1=AP(v16, 16, [[256, 128], [32, 8], [0, 16], [1, 16]]), op=ALU.add),
                    reads=["v16"], writes=["work"])
                for h in range(8):
                    top16(cand[:, h, :], cscr[:], (tv[:, h, 0:8], tv[:, h, 8:16]), (tcu[:, h, 0:8], tcu[:, h, 8:16]),
                          ["work"], "cscr", ["tv", "tcu"])
                P.op("dve", lambda e: e.tensor_tensor(out=gts[:], in0=tv[:], in1=AP(tv, 0, [[128, 128], [16, 8], [0, 16]]),
                                                      op=ALU.subtract), reads=["tv"], writes=[("gts", i % 2)])
                P.op("act", lambda e: e.activation(out=gts[:], in_=gts[:], func=AF.Exp), reads=[("gts", i % 2)], writes=[("gts", i % 2)])
                P.op("dve", lambda e: e.tensor_reduce(out=gsum[:], in_=gts[:], axis=AX.X, op=ALU.add),
                     reads=[("gts", i % 2)], writes=["gsum"])
                P.op("dve", lambda e: e.reciprocal(out=gsum[:], in_=gsum[:]), reads=["gsum"], writes=["gsum"])
                P.op("dve", lambda e: e.tensor_tensor(out=gts[:], in0=gts[:], in1=AP(gsum, 0, [[8, 128], [1, 8], [0, 16]]),
                                                      op=ALU.mult), reads=[("gts", i % 2), "gsum"], writes=[("gts", i % 2)])
                tcu2 = tcu[:].rearrange("p h k -> p (h k)")
                P.op("dve", lambda e: e.tensor_scalar(out=abu[:, 0, :], in0=tcu2, scalar1=4, scalar2=None,
                                                      op0=ALU.logical_shift_right), reads=["tcu"], writes=["abu"])
                P.op("dve", lambda e: e.tensor_scalar(out=abu[:, 1, :], in0=tcu2, scalar1=15, scalar2=None,
                                                      op0=ALU.bitwise_and), reads=["tcu"], writes=["abu"])
                P.op("dve", lambda e: e.tensor_copy(out=abf[:], in_=abu[:]), reads=["abu"], writes=["abf"])
                for w in range(2):
                    P.op("dve", lambda e, w=w: e.tensor_tensor(
                        out=eq, in0=AP(abf, w * 128, [[256, 128], [16, 8], [1, 16], [0, 16]]),
                        in1=AP(io_f, 0, [[16, 128], [0, 8], [0, 16], [1, 16]]), op=ALU.is_equal),
                        reads=["abf", "io_f"], writes=["work"])
                    P.op("dve", lambda e, w=w: e.tensor_tensor(
                        out=eq, in0=eq, in1=AP(i16f, w * 16, [[256, 128], [32, 8], [0, 16], [1, 16]]), op=ALU.mult),
                        reads=["work", "i16f"], writes=["work"])
                    P.op("dve", lambda e, w=w: e.tensor_reduce(
                        out=i12[:, w, :], in_=eq.rearrange("p h k a -> p (h k) a"), axis=AX.X, op=ALU.add),
                        reads=["work"], writes=["i12"])
                P.op("dve", lambda e: e.scalar_tensor_tensor(out=eidf[:], in0=i12[:, 0, :], scalar=128.0, in1=i12[:, 1, :],
                                                             op0=ALU.mult, op1=ALU.add), reads=["i12"], writes=["eidf"])
                P.op("dve", lambda e: e.tensor_copy(out=eidx[:], in_=eidf[:]), reads=["eidf"], writes=[("eidx", i % 2)])
            def slot(i, s_):
                eidx, gts, hnb = eidxs[i % 2], gtss[i % 2], hnbs[i % 2]
                u, ures = ur.next()
                P.op("pool", lambda e, u=u: e.indirect_dma_start(
                    out=u[:], out_offset=None, in_=ctab.ap(),
                    in_offset=bass.IndirectOffsetOnAxis(ap=eidx[:, s_:s_ + 1], axis=0)),
                    reads=[("eidx", i % 2)], writes=[ures], dma=("p5g", ures[1], i % 2))
                P.op("dve", lambda e, u=u: e.scalar_tensor_tensor(
                    out=junk[:], in0=u[:, 0:D], scalar=1.0, in1=hnb[:], op0=ALU.mult, op1=ALU.mult,
                    accum_out=hdn[:, s_:s_ + 1]), reads=[ures, ("hnb", i % 2)], writes=[("hdn", s_)])
                P.op("act", lambda e: e.activation(out=zc[:, s_:s_ + 1], in_=hdn[:, s_:s_ + 1], func=AF.Gelu_apprx_tanh),
                     reads=[("hdn", s_)], writes=[("zc", s_)])
                dg, dgres = dgr.next()
                P.op("dve", lambda e, dg=dg: e.tensor_scalar(
                    out=dg[:], in0=ident_f[:], scalar1=zc[:, s_:s_ + 1],
                    scalar2=gts[:].rearrange("p h k -> p (h k)")[:, s_:s_ + 1], op0=ALU.mult, op1=ALU.mult),
                    reads=["ident_f", ("zc", s_), ("gts", i % 2)], writes=[dgres])
                for n in range(4):
                    P.op("pe", lambda e, dg=dg, u=u, n=n: e.matmul(
                        ybk[n][:], lhsT=dg[:], rhs=u[:, D + n * 512:D + (n + 1) * 512], start=(s_ == 0),
                        stop=(s_ == NS - 1)), reads=[dgres, ures], writes=[("y", n)])

            def final(i):
                ht = hts[i % 2]
                for n in range(4):
                    P.op("dve", lambda e, n=n: e.tensor_tensor(out=ot_[:, n * 512:(n + 1) * 512],
                                                                in0=ht[:, n * 512:(n + 1) * 512], in1=ybk[n][:], op=ALU.add),
                         reads=[("ht", i % 2), ("y", n)], writes=["work"])
                rstd_ap, rkey = rms_rstd(ot_[:], 1, "work")
                P.op("dve", lambda e, rstd_ap=rstd_ap: e.scalar_tensor_tensor(
                    out=ot_[:], in0=ot_[:], scalar=rstd_ap, in1=gfin[:], op0=ALU.mult, op1=ALU.mult),
                    reads=["work", rkey, "gfin"], writes=["work"])
                P.op("sp", lambda e, i=i: e.dma_start(out=out[i * 128:(i + 1) * 128, :], in_=ot_[:]),
                     reads=["work"], writes=[("out", i)], dma="p5o")

            def cap4b(i):
                if i >= NOB:
                    return []
                P.capture()
                tile_4b(i)
                return P.end_capture()

            tile_4b(0)
            for i in range(NOB):
                inter = cap4b(i + 1)
                per = (len(inter) + NS - 1) // NS
                for s_ in range(NS):
                    slot(i, s_)
                    P.replay(inter[s_ * per:(s_ + 1) * per])
                final(i)
            P.barrier()
            P.emit()
            P.finish()
    return nc


def _const_tables(cfg, p):
    NCB, NPB = cfg.NCB, cfg.NPB
    TP = NPB * 128
    idx = np.arange(cfg.TC)
    pos = idx - (0 if p == 1 else TP)
    valid = (pos >= 0).astype(np.float32)
    kvalid = np.ascontiguousarray(valid.reshape(NCB, 128).T)
    inv = (500000.0 ** (-np.arange(0, 16, 2, dtype=np.float32) / 16.0)).astype(np.float32)
    ang = np.maximum(pos, 0).astype(np.float32)[:, None] * inv[None, :]
    cos = np.cos(ang).astype(np.float32).reshape(NCB, 128, 8).transpose(1, 0, 2).reshape(128, NCB * 8)
    sin = np.sin(ang).astype(np.float32).reshape(NCB, 128, 8).transpose(1, 0, 2).reshape(128, NCB * 8)
    kl = np.arange(128)[:, None, None]
    ko = np.arange(20)[None, :, None]
    ql = np.arange(512)[None, None, :]
    delta = (16 - ko) * 128 + ql - kl
    w = ((delta >= 0) & (delta <= 128)).astype(np.float32)
    w += ((delta >= 0) & (delta <= 512) & (delta % 4 == 0)).astype(np.float32)
    w += ((delta >= 0) & (delta <= 2048) & (delta % 16 == 0)).astype(np.float32)
    wcat = np.ascontiguousarray(w.reshape(128, 20 * 512))
    tri = (np.arange(128)[:, None] <= np.arange(128)[None, :]).astype(np.float32)
    return dict(kvalid=kvalid, rope_cos=np.ascontiguousarray(cos), rope_sin=np.ascontiguousarray(sin),
                wcat=wcat, tri=tri, ident=np.eye(128, dtype=np.float32))


def _pc(v, nch):
    return np.ascontiguousarray(np.asarray(v, np.float32).reshape(nch, 128).T)


def shared_inputs(cfg, attn_norm_gain, w_in, forget_bias, fox_out_gain, dil_out_gain, w_out,
                  ffn_norm_gain, peer_query, peer_sub_keys, peer_down, peer_up, final_norm_gain):
    f = lambda a: np.ascontiguousarray(np.asarray(a, np.float32))
    NCH = cfg.NCH
    skT = np.asarray(peer_sub_keys[0], np.float32).reshape(16, 128, 128).transpose(2, 0, 1)
    return dict(
        w_in=f(w_in[0]), g_attn=_pc(attn_norm_gain[0], NCH), fbias=f(forget_bias[0]).reshape(16, 1),
        g_mix=_pc(np.concatenate([np.asarray(fox_out_gain[0]), np.asarray(dil_out_gain[0])]), NCH),
        w_out=f(w_out[0]), g_ffn=np.ascontiguousarray(np.broadcast_to(f(ffn_norm_gain[0])[None, :], (128, cfg.D))),
        w_pq=f(peer_query[0]), skT=np.ascontiguousarray(skT.reshape(128, 16 * 128)),
        p_down=f(peer_down[0]), p_up=f(peer_up[0]),
        g_fin=np.ascontiguousarray(np.broadcast_to(f(final_norm_gain)[None, :], (128, cfg.D))))


def core_inputs(cfg, xb, p, shared, tables):
    TP, TO = cfg.NPB * 128, cfg.TO
    if p == 1:
        x_ctx = np.ascontiguousarray(xb[0:TP + TO])
    else:
        x_ctx = np.concatenate([np.zeros((TP, cfg.D), np.float32), xb[0:TO]], axis=0)
    d = dict(shared)
    d.update(tables[p])
    d["x_ctx"] = x_ctx
    return d


def kernel(**inputs):
    cfg = Cfg(32, 32)
    x = np.asarray(inputs["x"], np.float32)
    B, S, D = x.shape
    sh = shared_inputs(cfg, **{k: np.asarray(v) for k, v in inputs.items() if k != "x"})
    tables = [_const_tables(cfg, 0), _const_tables(cfg, 1)]
    in_maps = [core_inputs(cfg, x[c // 2], c % 2, sh, tables) for c in range(2 * B)]
    nc = build(cfg)
    res = run_bass_kernel_spmd(nc, in_maps, core_ids=list(range(2 * B)))
    outp = np.empty((B, S, D), np.float32)
    for c in range(2 * B):
        outp[c // 2, (c % 2) * cfg.TO:(c % 2 + 1) * cfg.TO] = np.asarray(res.results[c]["out"], np.float32)
    return outp
```

```python
import numpy as np
from contextlib import ExitStack

import concourse.bass as bass
import concourse.mybir as mybir
from concourse.bass_utils import run_bass_kernel_spmd

F32 = mybir.dt.float32
BF16 = mybir.dt.bfloat16
I32 = mybir.dt.int32
U32 = mybir.dt.uint32
AF = mybir.ActivationFunctionType
ALU = mybir.AluOpType
AX = mybir.AxisListType

HD = 64
NH = 16
RMS_EPS = 1e-6
NEXP = 16384


class Cfg:
    def __init__(self, npb=32, nob=32, D=2048):
        self.NPB = npb
        self.NOB = nob
        self.NCB = npb + nob
        self.D = D
        self.NCH = D // 128
        self.TC = self.NCB * 128
        self.TO = self.NOB * 128
        self.IN_COLS = 3 * 1024 + 16 + 3 * 1024


class Prog:
    ENGS = ("pe", "act", "dve", "pool", "sp")

    def __init__(self, nc, stack):
        self.nc = nc
        self.stack = stack
        self.all_sems = []
        self.dpool = []
        self.nphase = 0
        self.streams = {e: [] for e in self.ENGS}
        self.new_phase()

    def _alloc(self, name):
        h = self.stack.enter_context(self.nc.semaphore(name))
        self.all_sems.append(h)
        return h

    def new_phase(self):
        self.nphase += 1
        self.sem = {e: self._alloc("s%d_%s" % (self.nphase, e)) for e in self.ENGS}
        self.cnt = {e: 0 for e in self.ENGS}
        self.dsem = {}
        self.dnext = 0
        self.known = {e: {} for e in self.ENGS}
        self.last_w = {}
        self.readers = {}

    def finish(self):
        sems = list(self.all_sems)
        with self.nc.Block() as block:
            @block.gpsimd
            def _(e):
                for h in sems:
                    e.sem_clear(h)

    def _semh(self, key):
        if isinstance(key, str):
            return self.sem[key]
        return self.dsem[key[1]][0]

    def capture(self):
        self._cap = []

    def end_capture(self):
        c, self._cap = self._cap, None
        return c

    def replay(self, items):
        for it in items:
            self.op(*it)

    def op(self, eng, fn, reads=(), writes=(), dma=None):
        if getattr(self, "_cap", None) is not None:
            self._cap.append((eng, fn, tuple(reads), tuple(writes), dma))
            return
        deps = {}

        def add(ev):
            if ev is None:
                return
            k, v = ev
            if deps.get(k, 0) < v:
                deps[k] = v

        writes = list(writes)
        if dma is not None:
            writes.append(("dmakey", dma))
        for r in reads:
            add(self.last_w.get(r))
        for w in writes:
            add(self.last_w.get(w))
            for k, v in self.readers.get(w, {}).items():
                add((k, v))
        is_dma = dma is not None
        if is_dma:
            if dma not in self.dsem:
                if self.dnext >= len(self.dpool):
                    self.dpool.append([self._alloc("dsem%d" % len(self.dpool)), 0])
                self.dsem[dma] = self.dpool[self.dnext]
                self.dnext += 1
            self.dsem[dma][1] += 1
            ev = (("dma", dma), 16 * self.dsem[dma][1])
            inc = (self.dsem[dma][0], 16)
        else:
            self.cnt[eng] += 1
            ev = (eng, self.cnt[eng])
            inc = (self.sem[eng], 1)
        waits = []
        kn = self.known[eng]
        for k, v in deps.items():
            if k == eng and eng == "pe" and not is_dma:
                continue
            if kn.get(k, 0) >= v:
                continue
            kn[k] = v
            waits.append((self._semh(k), v))
        for w in writes:
            self.last_w[w] = ev
            self.readers[w] = {}
        for r in reads:
            d = self.readers.setdefault(r, {})
            if d.get(ev[0], 0) < ev[1]:
                d[ev[0]] = ev[1]
        self.streams[eng].append((waits, fn, inc))

    def barrier(self):
        allev = [(e, self.cnt[e]) for e in self.ENGS if self.cnt[e] > 0]
        allev += [(("dma", k), 16 * v[1]) for k, v in self.dsem.items()]
        for eng in self.ENGS:
            waits = []
            kn = self.known[eng]
            for k, v in allev:
                if kn.get(k, 0) >= v:
                    continue
                kn[k] = v
                waits.append((self._semh(k), v))
            if waits:
                self.streams[eng].append((waits, None, None))

    def emit(self):
        nc = self.nc
        streams = self.streams
        with nc.Block() as block:
            def run(name, e):
                for waits, fn, inc in streams[name]:
                    for s, v in waits:
                        e.wait_ge(s, v)
                    if fn is not None:
                        fn(e).then_inc(inc[0], inc[1])

            @block.tensor
            def _(e):
                run("pe", e)

            @block.scalar
            def _(e):
                run("act", e)

            @block.vector
            def _(e):
                run("dve", e)

            @block.gpsimd
            def _(e):
                run("pool", e)

            @block.sync
            def _(e):
                run("sp", e)
        self.streams = {e: [] for e in self.ENGS}


def AP(t, off, pat):
    return bass.AP(t, off, [list(x) for x in pat])


class Ring:
    def __init__(self, nc, stack, name, n, shape, dtype, psum=False):
        mk = nc.psum_tensor if psum else nc.sbuf_tensor
        self.t = [stack.enter_context(mk("%s%d" % (name, i), shape, dtype)) for i in range(n)]
        self.name = name
        self.n = n
        self.i = -1

    def next(self):
        self.i += 1
        s = self.i % self.n
        return self.t[s], (self.name, s)


def build(cfg, upto=99, dbg_out=()):
    nc = bass.Bass("TRN2", target_bir_lowering=False)
    D, NCH, NCB, NOB, NPB, TC, TO = cfg.D, cfg.NCH, cfg.NCB, cfg.NOB, cfg.NPB, cfg.TC, cfg.TO
    NTG = NCB // 4
    OTG0 = NPB // 4

    def din(name, shape, dt=F32):
        return nc.dram_tensor(name, list(shape), dt, kind="ExternalInput")

    def dscr(name, shape, dt):
        return nc.dram_tensor(name, list(shape), dt, kind="ExternalOutput" if name in dbg_out else "Internal")

    x_ctx = din("x_ctx", [TC, D])
    kvalid = din("kvalid", [128, NCB])
    rope_cos = din("rope_cos", [128, NCB * 8])
    rope_sin = din("rope_sin", [128, NCB * 8])
    wcat_in = din("wcat", [128, 20 * 512])
    tri_in = din("tri", [128, 128])
    ident_in = din("ident", [128, 128])
    w_in = din("w_in", [D, cfg.IN_COLS])
    g_attn = din("g_attn", [128, NCH])
    fbias = din("fbias", [16, 1])
    g_mix = din("g_mix", [128, NCH])
    w_out = din("w_out", [D, D])
    g_ffn = din("g_ffn", [128, D])
    w_pq = din("w_pq", [D, 2048])
    skT_in = din("skT", [128, 16 * 128])
    p_down = din("p_down", [NEXP, D])
    p_up = din("p_up", [NEXP, D])
    g_fin = din("g_fin", [128, D])
    out = nc.dram_tensor("out", [TO, D], F32, kind="ExternalOutput")

    xnT = dscr("xnT", [NCH, 128, TC], BF16)
    qfT = dscr("qfT", [NH, 65, TO], BF16)
    kfT = dscr("kfT", [NH, 65, TC], BF16)
    vf = dscr("vf", [128, NCB, NH, 65], BF16)
    qdT = dscr("qdT", [NH * 64, TO], BF16)
    kdT = dscr("kdT", [NH * 64, TC], BF16)
    vd = dscr("vd", [128, NCB, NH, 65], BF16)
    attn_o = dscr("attn_o", [TO, D], F32)
    hbuf = dscr("hbuf", [TO, D], F32)
    ctab = dscr("ctab", [NEXP, 2 * D], BF16)
    dbg = {}

    with ExitStack() as gstack:
        P = Prog(nc, gstack)
        sb = lambda st, name, shape, dt: st.enter_context(nc.sbuf_tensor(name, list(shape), dt))

        ident_f = sb(gstack, "ident_f", [128, 128], F32)
        ident_b = sb(gstack, "ident_b", [128, 128], BF16)
        negc_tm = sb(gstack, "negc_tm", [128, NCB, 16], F32)
        P.op("sp", lambda e: e.dma_start(out=ident_f[:], in_=ident_in.ap()), writes=["ident_f"], dma="c0")
        P.op("dve", lambda e: e.tensor_copy(out=ident_b[:], in_=ident_f[:]), reads=["ident_f"], writes=["ident_b"])

        with ExitStack() as st:
            xr = Ring(nc, st, "p0x", 2, [128, D], F32)
            jr = Ring(nc, st, "p0j", 1, [128, D], BF16)
            xbr = Ring(nc, st, "p0xb", 2, [128, D], BF16)
            tpr = Ring(nc, st, "p0tp", 4, [128, 1024], BF16, psum=True)
            xtr = Ring(nc, st, "p0xt", 2, [128, NCH, 512], BF16)
            ss = sb(st, "p0ss", [128, NCB], F32)
            rs = sb(st, "p0rs", [128, NCB], F32)
            rstd = sb(st, "p0rstd", [128, NCB], F32)
            for i in range(NCB):
                xt, xres = xr.next()
                P.op("sp", lambda e, xt=xt, i=i: e.dma_start(out=xt[:], in_=x_ctx[i * 128:(i + 1) * 128, :]),
                     writes=[xres], dma=("p0x", xres[1]))
                jt, jres = jr.next()
                P.op("act", lambda e, xt=xt, jt=jt, i=i: e.activation(out=jt[:], in_=xt[:], func=AF.Square,
                                                                    accum_out=ss[:, i:i + 1]),
                     reads=[xres], writes=[jres, ("ss", i)])
                P.op("dve", lambda e, i=i: e.tensor_scalar(out=rs[:, i:i + 1], in0=ss[:, i:i + 1], scalar1=1.0 / D,
                                                           scalar2=RMS_EPS, op0=ALU.mult, op1=ALU.add),
                     reads=[("ss", i)], writes=[("rs", i)])
                P.op("act", lambda e, i=i: e.activation(out=rs[:, i:i + 1], in_=rs[:, i:i + 1], func=AF.Sqrt),
                     reads=[("rs", i)], writes=[("rs", i)])
                P.op("dve", lambda e, i=i: e.reciprocal(out=rstd[:, i:i + 1], in_=rs[:, i:i + 1]),
                     reads=[("rs", i)], writes=[("rstd", i)])
                xb, xbres = xbr.next()
                P.op("dve", lambda e, xt=xt, xb=xb, i=i: e.tensor_scalar(out=xb[:], in0=xt[:], scalar1=rstd[:, i:i + 1],
                                                                        scalar2=None, op0=ALU.mult),
                     reads=[xres, ("rstd", i)], writes=[xbres])
                if i % 4 == 0:
                    xtt, xtres = xtr.next()
                for half in range(NCH // 8):
                    tp, tpres = tpr.next()
                    for j in range(8):
                        c = half * 8 + j
                        P.op("pe", lambda e, tp=tp, xb=xb, j=j, c=c: e.transpose(
                            out=tp[:, j * 128:(j + 1) * 128], in_=xb[:, c * 128:(c + 1) * 128], identity=ident_b[:]),
                            reads=[xbres, "ident_b"], writes=[tpres])
                    eng = "act" if half % 2 == 0 else "dve"
                    o_ap = xtt[:, half * 8:(half + 1) * 8, (i % 4) * 128:(i % 4 + 1) * 128]
                    i_ap = tp[:].rearrange("p (c t) -> p c t", c=8)
                    if eng == "act":
                        P.op("act", lambda e, o_ap=o_ap, i_ap=i_ap: e.activation(out=o_ap, in_=i_ap, func=AF.Copy),
                             reads=[tpres], writes=[xtres])
                    else:
                        P.op("dve", lambda e, o_ap=o_ap, i_ap=i_ap: e.tensor_copy(out=o_ap, in_=i_ap),
                             reads=[tpres], writes=[xtres])
                if i % 4 == 3:
                    g = i // 4
                    dst = AP(xnT, g * 512, [[TC, 128], [128 * TC, NCH], [1, 512]])
                    P.op("sp", lambda e, dst=dst, xtt=xtt: e.dma_start(out=dst, in_=xtt[:]),
                         reads=[xtres], writes=[("xnT", g)], dma=("p0st", xtres[1]))
            P.barrier()
            P.emit()
            P.new_phase()

        if upto <= 0:
            return nc

        with ExitStack() as st:
            wst = Ring(nc, st, "p1ws", 3, [128, 1024], F32)
            wgr = Ring(nc, st, "p1wg", 1, [128, NCH, 1024], BF16)
            xgr = Ring(nc, st, "p1xg", 2, [128, NCH, 512], BF16)
            psr = Ring(nc, st, "p1ps", 4, [128, 512], F32, psum=True)
            tpr = Ring(nc, st, "p1tp", 2, [128, 1024], BF16, psum=True)
            obr = Ring(nc, st, "p1ob", 3, [128, 512], BF16)
            vtr = Ring(nc, st, "p1vt", 2, [128, NH, 65], BF16)
            tmr = Ring(nc, st, "p1tm", 2, [128, 1024], F32)
            rbr = Ring(nc, st, "p1rb", 2, [128, 1024], BF16)
            dtr = Ring(nc, st, "p1dt", 2, [128, 8, 512], BF16)
            gat = sb(st, "p1gat", [128, NCH], F32)
            kv_sb = sb(st, "p1kv", [128, NCB], F32)
            cos_sb = sb(st, "p1cos", [128, NCB * 8], F32)
            sin_sb = sb(st, "p1sin", [128, NCB * 8], F32)
            ra = sb(st, "p1ra", [128, NH, 8], F32)
            rb_ = sb(st, "p1rb_", [128, NH, 8], F32)
            fb_sb = sb(st, "p1fb", [16, 1], F32)
            nfb = sb(st, "p1nfb", [16, 1], F32)
            negc_fm = sb(st, "p1negc", [16, TC], F32)
            ones16 = sb(st, "p1ones", [16, 512], F32)
            onesb = sb(st, "p1onesb", [16, 512], BF16)
            spt = sb(st, "p1spt", [16, 512], F32)
            rqr = Ring(nc, st, "p1rq", 2, [16, 512], BF16)
            for (t_, src, key) in ((gat, g_attn, "gat"), (kv_sb, kvalid, "kv"), (cos_sb, rope_cos, "cos"),
                                   (sin_sb, rope_sin, "sin"), (fb_sb, fbias, "fb")):
                P.op("sp", lambda e, t_=t_, src=src: e.dma_start(out=t_[:], in_=src.ap()), writes=[key], dma="c0")
            P.op("dve", lambda e: e.tensor_scalar(out=nfb[:], in0=fb_sb[:], scalar1=-1.0, scalar2=None, op0=ALU.mult),
                 reads=["fb"], writes=["nfb"])
            P.op("pool", lambda e: e.memset(ones16[:], 1.0), writes=["ones16"])
            P.op("pool", lambda e: e.memset(onesb[:], 1.0), writes=["onesb"])

            groups = [("fq", 0, 1024, "fm", OTG0), ("fk", 1024, 1024, "fm", 0), ("fv", 2048, 1024, "tmv", 0),
                      ("fg", 3072, 16, "fg", 0), ("dq", 3088, 1024, "tmr", OTG0), ("dk", 4112, 1024, "tmr", 0),
                      ("dv", 5136, 1024, "tmv", 0)]
            for (gname, col0, ncols, mode, tg0) in groups:
                wg, wgres = wgr.next()
                for c in range(NCH):
                    ws, wsres = wst.next()
                    P.op("sp" if c % 2 == 0 else "act",
                         lambda e, ws=ws, c=c, col0=col0, ncols=ncols: e.dma_start(
                             out=ws[:, 0:ncols], in_=w_in[c * 128:(c + 1) * 128, col0:col0 + ncols]),
                         writes=[wsres], dma=("p1w", wsres[1]))
                    if c % 2 == 0:
                        P.op("act", lambda e, ws=ws, wg=wg, c=c, ncols=ncols: e.activation(
                            out=wg[:, c, 0:ncols], in_=ws[:, 0:ncols], func=AF.Copy, scale=gat[:, c:c + 1]),
                            reads=[wsres, "gat"], writes=[wgres])
                    else:
                        P.op("dve", lambda e, ws=ws, wg=wg, c=c, ncols=ncols: e.tensor_scalar(
                            out=wg[:, c, 0:ncols], in0=ws[:, 0:ncols], scalar1=gat[:, c:c + 1], scalar2=None,
                            op0=ALU.mult),
                            reads=[wsres, "gat"], writes=[wgres])
                for tg in range(tg0, NTG):
                    xg, xgres = xgr.next()
                    src = AP(xnT, tg * 512, [[TC, 128], [128 * TC, NCH], [1, 512]])
                    P.op("sp", lambda e, xg=xg, src=src: e.dma_start(out=xg[:], in_=src),
                         reads=[("xnT", tg)], writes=[xgres], dma=("p1x", xgres[1]))
                    otg = tg - OTG0
                    if gname == "fk":
                        P.op("act", lambda e, tg=tg: e.dma_start(
                            out=AP(kfT, 64 * TC + tg * 512, [[65 * TC, 16], [1, 512]]), in_=onesb[:]),
                            reads=["onesb"], writes=[("kaug", tg)], dma="kaug")
                    if mode == "fm":
                        dstT, TT, toff, scl = (qfT, TO, otg * 512, 0.125) if gname == "fq" else (kfT, TC, tg * 512, 1.0)
                        for sub in range(8):
                            ps, psres = psr.next()
                            for c in range(NCH):
                                P.op("pe", lambda e, ps=ps, wg=wg, xg=xg, c=c, sub=sub: e.matmul(
                                    ps[:], lhsT=wg[:, c, sub * 128:(sub + 1) * 128], rhs=xg[:, c, :],
                                    start=(c == 0), stop=(c == NCH - 1)),
                                    reads=[wgres, xgres], writes=[psres])
                            ob, obres = obr.next()
                            P.op("act", lambda e, ob=ob, ps=ps, scl=scl: e.activation(out=ob[:], in_=ps[:], func=AF.Copy,
                                                                                      scale=scl),
                                 reads=[psres], writes=[obres])
                            for hh in range(2):
                                h = 2 * sub + hh
                                dst = AP(dstT, h * 65 * TT + toff, [[TT, 64], [1, 512]])
                                P.op("sp" if hh == 0 else "act", lambda e, dst=dst, ob=ob, hh=hh: e.dma_start(
                                    out=dst, in_=ob[hh * 64:(hh + 1) * 64, :]),
                                    reads=[obres], writes=[(gname, h, tg)], dma=("p1o", obres[1], hh))
                    elif mode == "fg":
                        ps, psres = psr.next()
                        for c in range(NCH):
                            P.op("pe", lambda e, ps=ps, wg=wg, xg=xg, c=c: e.matmul(
                                ps[0:16, :], lhsT=wg[:, c, 0:16], rhs=xg[:, c, :], start=(c == 0), stop=(c == NCH - 1)),
                                reads=[wgres, xgres], writes=[psres])
                        P.op("act", lambda e, ps=ps: e.activation(out=spt[:], in_=ps[0:16, :], func=AF.Exp, scale=-1.0,
                                                                  bias=nfb[:]),
                             reads=[psres, "nfb"], writes=["spt"])
                        P.op("act", lambda e: e.activation(out=spt[:], in_=spt[:], func=AF.Ln, bias=1.0),
                             reads=["spt"], writes=["spt"])
                        init = 0.0 if tg == 0 else negc_fm[:, tg * 512 - 1:tg * 512]
                        P.op("dve", lambda e, tg=tg, init=init: e.tensor_tensor_scan(
                            out=negc_fm[:, tg * 512:(tg + 1) * 512], data0=ones16[:], data1=spt[:], initial=init,
                            op0=ALU.mult, op1=ALU.add),
                            reads=["spt", "ones16", "negc_fm"], writes=["negc_fm"])
                        if tg >= OTG0:
                            rq, rqres = rqr.next()
                            P.op("dve", lambda e, rq=rq, tg=tg: e.tensor_scalar(
                                out=rq[:], in0=negc_fm[:, tg * 512:(tg + 1) * 512], scalar1=-1.0, scalar2=None,
                                op0=ALU.mult), reads=["negc_fm"], writes=[rqres])
                            P.op("act", lambda e, rq=rq, otg=otg: e.dma_start(
                                out=AP(qfT, 64 * TO + otg * 512, [[65 * TO, 16], [1, 512]]), in_=rq[:]),
                                reads=[rqres], writes=[("qaug", otg)], dma=("qaug", rqres[1]))
                    else:
                        for ti in range(4):
                            blk = tg * 4 + ti
                            if mode == "tmv":
                                vt, vtres = vtr.next()
                            else:
                                tm, tmres = tmr.next()
                            for half in range(2):
                                ps, psres = psr.next()
                                for c in range(NCH):
                                    P.op("pe", lambda e, ps=ps, wg=wg, xg=xg, c=c, ti=ti, half=half: e.matmul(
                                        ps[:], lhsT=xg[:, c, ti * 128:(ti + 1) * 128],
                                        rhs=wg[:, c, half * 512:(half + 1) * 512], start=(c == 0), stop=(c == NCH - 1)),
                                        reads=[wgres, xgres], writes=[psres])
                                if mode == "tmv":
                                    P.op("act", lambda e, vt=vt, ps=ps, half=half: e.activation(
                                        out=vt[:, half * 8:(half + 1) * 8, 0:64],
                                        in_=ps[:].rearrange("p (h d) -> p h d", h=8), func=AF.Copy),
                                        reads=[psres], writes=[vtres])
                                else:
                                    scl = 0.125 if gname == "dq" else 1.0
                                    P.op("act", lambda e, tm=tm, ps=ps, half=half, scl=scl: e.activation(
                                        out=tm[:, half * 512:(half + 1) * 512], in_=ps[:], func=AF.Copy, scale=scl),
                                        reads=[psres], writes=[tmres])
                            if mode == "tmv":
                                P.op("pool", lambda e, vt=vt, blk=blk: e.tensor_copy(
                                    out=vt[:, :, 64:65], in_=AP(kv_sb, blk, [[NCB, 128], [0, NH], [1, 1]])),
                                    reads=["kv"], writes=[vtres])
                                dstv = vf if gname == "fv" else vd
                                dst = AP(dstv, blk * NH * 65, [[NCB * NH * 65, 128], [1, NH * 65]])
                                P.op("sp", lambda e, dst=dst, vt=vt: e.dma_start(
                                    out=dst, in_=vt[:].rearrange("p h d -> p (h d)")),
                                    reads=[vtres], writes=[(gname, blk)], dma=("p1v", vtres[1]))
                            else:
                                rbt, rbres = rbr.next()
                                P.op("act", lambda e, rbt=rbt, tm=tm: e.activation(out=rbt[:], in_=tm[:], func=AF.Copy),
                                     reads=[tmres], writes=[rbres])
                                tm3 = tm[:].rearrange("p (h d) -> p h d", h=NH)
                                rb3 = rbt[:].rearrange("p (h d) -> p h d", h=NH)
                                x1, x2 = tm3[:, :, 0:8], tm3[:, :, 8:16]
                                cosb = AP(cos_sb, blk * 8, [[NCB * 8, 128], [0, NH], [1, 8]])
                                sinb = AP(sin_sb, blk * 8, [[NCB * 8, 128], [0, NH], [1, 8]])
                                for (o_, a_, ca, b_, cb, opx) in ((rb3[:, :, 0:8], x1, cosb, x2, sinb, ALU.subtract),
                                                                 (rb3[:, :, 8:16], x2, cosb, x1, sinb, ALU.add)):
                                    P.op("dve", lambda e, a_=a_, ca=ca: e.tensor_tensor(out=ra[:], in0=a_, in1=ca, op=ALU.mult),
                                         reads=[tmres, "cos"], writes=["ra"])
                                    P.op("dve", lambda e, b_=b_, cb=cb: e.tensor_tensor(out=rb_[:], in0=b_, in1=cb, op=ALU.mult),
                                         reads=[tmres, "sin"], writes=["rb_"])
                                    P.op("dve", lambda e, o_=o_, opx=opx: e.tensor_tensor(out=o_, in0=ra[:], in1=rb_[:], op=opx),
                                         reads=["ra", "rb_"], writes=[rbres])
                                if ti == 0:
                                    dt_, dtres = dtr.next()
                                tp, tpres = tpr.next()
                                for j in range(8):
                                    P.op("pe", lambda e, tp=tp, rbt=rbt, j=j: e.transpose(
                                        out=tp[:, j * 128:(j + 1) * 128], in_=rbt[:, j * 128:(j + 1) * 128],
                                        identity=ident_b[:]),
                                        reads=[rbres, "ident_b"], writes=[tpres])
                                P.op("dve", lambda e, dt_=dt_, tp=tp, ti=ti: e.tensor_copy(
                                    out=dt_[:, :, ti * 128:(ti + 1) * 128], in_=tp[:].rearrange("p (j t) -> p j t", j=8)),
                                    reads=[tpres], writes=[dtres])
                                if ti == 3:
                                    dstT, TT, toff = (qdT, TO, otg * 512) if gname == "dq" else (kdT, TC, tg * 512)
                                    dst = AP(dstT, toff, [[TT, 128], [128 * TT, 8], [1, 512]])
                                    P.op("sp", lambda e, dst=dst, dt_=dt_: e.dma_start(out=dst, in_=dt_[:]),
                                         reads=[dtres], writes=[(gname, tg)], dma=("p1d", dtres[1]))
                if mode == "fg":
                    for b0 in range(0, NCB, 32):
                        ps, psres = psr.next()
                        nb = min(32, NCB - b0)
                        for bb in range(nb):
                            P.op("pe", lambda e, ps=ps, b0=b0, bb=bb: e.transpose(
                                out=ps[:, bb * 16:(bb + 1) * 16], in_=negc_fm[:, (b0 + bb) * 128:(b0 + bb + 1) * 128],
                                identity=ident_f[0:16, 0:16]),
                                reads=["negc_fm", "ident_f"], writes=[psres])
                        P.op("dve", lambda e, ps=ps, b0=b0, nb=nb: e.tensor_copy(
                            out=negc_tm[:, b0:b0 + nb, :], in_=ps[:, 0:nb * 16].rearrange("p (b h) -> p b h", h=16)),
                            reads=[psres], writes=["negc_tm"])
            P.barrier()
            P.emit()
            P.new_phase()
        if upto <= 1:
            return nc

        for kind in ("fox", "dil"):
            with ExitStack() as st:
                KR = 65 if kind == "fox" else 64
                qr = Ring(nc, st, kind + "q", 2, [65, TO], BF16)
                kr = Ring(nc, st, kind + "k", 2, [65, TC], BF16)
                if kind == "dil":
                    for r_ in range(2):
                        P.op("pool", lambda e, r_=r_: e.memset(qr.t[r_][64:65, :], 0.0), writes=[(qr.name, r_)])
                        P.op("pool", lambda e, r_=r_: e.memset(kr.t[r_][64:65, :], 0.0), writes=[(kr.name, r_)])
                vr = Ring(nc, st, kind + "v", 2, [128, NCB, 4 * 65], BF16)
                sr = Ring(nc, st, kind + "s", 4, [128, 512], F32, psum=True)
                otr = Ring(nc, st, kind + "ot", 2, [128, 512], F32, psum=True)
                tor = Ring(nc, st, kind + "to", 2, [128, 4, 128], F32, psum=True)
                ptr_ = Ring(nc, st, kind + "pt", 4, [128, 512], BF16)
                osr = Ring(nc, st, kind + "os", 2, [65, 512], F32)
                rcr = Ring(nc, st, kind + "rc", 2, [128, 4], F32)
                ofr = Ring(nc, st, kind + "of", 2, [128, 4, 64], F32)
                cst = sb(st, kind + "cst", [128, 5 * 512], F32)
                if kind == "fox":
                    tri_b = sb(st, "tri_b", [128, 128], BF16)
                    P.op("sp", lambda e: e.dma_start(out=cst[:, 0:128], in_=tri_in.ap()), writes=["cst"], dma="c0")
                    P.op("dve", lambda e: e.tensor_copy(out=tri_b[:], in_=cst[:, 0:128]), reads=["cst"], writes=["tri_b"])
                else:
                    wcat_b = sb(st, "wcat_b", [128, 20, 512], BF16)
                    for pc in range(4):
                        P.op("sp", lambda e, pc=pc: e.dma_start(out=cst[:], in_=wcat_in[:, pc * 2560:(pc + 1) * 2560]),
                             writes=["cst"], dma="c0")
                        P.op("dve", lambda e, pc=pc: e.tensor_copy(
                            out=wcat_b[:, pc * 5:(pc + 1) * 5, :], in_=cst[:].rearrange("p (a b) -> p a b", a=5)),
                            reads=["cst"], writes=["wcat_b"])
                qsrc, ksrc, vsrc = (qfT, kfT, vf) if kind == "fox" else (qdT, kdT, vd)
                coff = 0 if kind == "fox" else 1024
                LA = 3
                units = []
                loads = {}
                head_starts = []
                for hg in range(4):
                    for hh in range(4):
                        h = hg * 4 + hh
                        head_starts.append((len(units), hg, hh))
                        for g in range(NOB // 4):
                            qb0 = NPB + 4 * g
                            kb_lo = 0 if kind == "fox" else max(0, qb0 - 16)
                            kbs = list(range(kb_lo, qb0 + 4))
                            for kb in kbs:
                                j = kb - qb0
                                q0 = (max(0, j) * 128) if kind == "fox" else 0
                                units.append((h, hh, g, kb, q0, kb == kbs[0], kb == kbs[-1], kb - (qb0 - 16), j))
                for n_, (u0, hg_, hh_) in enumerate(head_starts):
                    loads.setdefault(0 if n_ == 0 else head_starts[n_ - 1][0], []).append((hg_, hh_))
                cur = {}
                hstate = {}
                sinfo = {}
                deferred = []
                state = {"vg": None}

                def issue_loads(hg, hh):
                    h = hg * 4 + hh
                    if hh == 0:
                        vg, vgres = vr.next()
                        nvp = max(1, NCB // 16)
                        for vp in range(nvp):
                            nb_ = NCB // nvp
                            P.op("act", lambda e, vg=vg, hg=hg, vp=vp, nb_=nb_: e.dma_start(
                                out=vg[:, vp * nb_:(vp + 1) * nb_, :],
                                in_=AP(vsrc, hg * 4 * 65 + vp * nb_ * NH * 65,
                                       [[NCB * NH * 65, 128], [NH * 65, nb_], [1, 4 * 65]])),
                                writes=[vgres], dma=(kind + "v", vgres[1], vp))
                        state["vg"] = (vg, vgres)
                    qT, qres = qr.next()
                    kT, kres = kr.next()
                    P.op("sp", lambda e, qT=qT, h=h: e.dma_start(
                        out=qT[0:KR, :], in_=AP(qsrc, h * KR * TO, [[TO, KR], [1, TO]])),
                        writes=[qres], dma=(kind + "q", qres[1]))
                    P.op("sp", lambda e, kT=kT, h=h: e.dma_start(
                        out=kT[0:KR, :], in_=AP(ksrc, h * KR * TC, [[TC, KR], [1, TC]])),
                        writes=[kres], dma=(kind + "k", kres[1]))
                    hstate[h] = (qT, qres, kT, kres, state["vg"][0], state["vg"][1])

                def emit_S(ui):
                    (h, hh, g, kb, q0, first, last, ko, j) = units[ui]
                    for (hg_, hh_) in loads.get(ui, []):
                        issue_loads(hg_, hh_)
                    qT, qres, kT, kres, vg, vgres = hstate[h]
                    S, sres = sr.next()
                    sinfo[ui] = (S, sres)
                    P.op("pe", lambda e, S=S, kT=kT, qT=qT, kb=kb, g=g, q0=q0: e.matmul(
                        S[:, q0:512], lhsT=kT[:, kb * 128:(kb + 1) * 128],
                        rhs=qT[:, g * 512 + q0:(g + 1) * 512], start=True, stop=True),
                        reads=[kres, qres], writes=[sres])

                def finalize2(osb, osres, g, h):
                    to, tores = tor.next()
                    for jj in range(4):
                        P.op("pe", lambda e, to=to, osb=osb, jj=jj: e.transpose(
                            out=to[:, jj, 0:65], in_=osb[:, jj * 128:(jj + 1) * 128], identity=ident_f[0:65, 0:65]),
                            reads=[osres, "ident_f"], writes=[tores])
                    rc, rcres = rcr.next()
                    of, ofres = ofr.next()
                    P.op("dve", lambda e, rc=rc, to=to: e.reciprocal(out=rc[:], in_=to[:, :, 64]),
                         reads=[tores], writes=[rcres])
                    P.op("dve", lambda e, rc=rc, to=to, of=of: e.tensor_tensor(
                        out=of[:], in0=to[:, :, 0:64], in1=AP(rc, 0, [[4, 128], [1, 4], [0, 64]]), op=ALU.mult),
                        reads=[tores, rcres], writes=[ofres])
                    P.op("sp", lambda e, of=of, g=g, h=h: e.dma_start(
                        out=AP(attn_o, g * 512 * D + coff + h * 64, [[D, 128], [128 * D, 4], [1, 64]]), in_=of[:]),
                        reads=[ofres], writes=[("attn_o", g)], dma=(kind + "o", ofres[1]))

                def emit_rest(ui):
                    (h, hh, g, kb, q0, first, last, ko, j) = units[ui]
                    qT, qres, kT, kres, vg, vgres = hstate[h]
                    S, sres = sinfo.pop(ui)
                    if first:
                        cur["ot"] = otr.next()
                    ot, otres = cur["ot"]
                    Pt, ptres = ptr_.next()
                    if kind == "fox":
                        P.op("act", lambda e, Pt=Pt, S=S, q0=q0, kb=kb, h=h: e.activation(
                            out=Pt[:, q0:512], in_=S[:, q0:512], func=AF.Exp, bias=negc_tm[:, kb, h:h + 1]),
                            reads=[sres, "negc_tm"], writes=[ptres])
                        if j >= 0:
                            P.op("dve", lambda e, Pt=Pt, q0=q0: e.tensor_tensor(
                                out=Pt[:, q0:q0 + 128], in0=Pt[:, q0:q0 + 128], in1=tri_b[:], op=ALU.mult),
                                reads=[ptres, "tri_b"], writes=[ptres])
                    else:
                        P.op("act", lambda e, Pt=Pt, S=S: e.activation(out=Pt[:], in_=S[:], func=AF.Exp),
                             reads=[sres], writes=[ptres])
                        P.op("dve", lambda e, Pt=Pt, ko=ko: e.tensor_tensor(
                            out=Pt[:], in0=Pt[:], in1=wcat_b[:, ko, :], op=ALU.mult),
                            reads=[ptres, "wcat_b"], writes=[ptres])
                    P.op("pe", lambda e, ot=ot, vg=vg, Pt=Pt, kb=kb, hh=hh, q0=q0, first=first, last=last: e.matmul(
                        ot[0:65, q0:512], lhsT=vg[:, kb, hh * 65:(hh + 1) * 65], rhs=Pt[:, q0:512],
                        start=first, stop=last),
                        reads=[vgres, ptres], writes=[otres])
                    if last:
                        osb, osres = osr.next()
                        P.op("act", lambda e, osb=osb, ot=ot: e.activation(out=osb[:], in_=ot[0:65, :], func=AF.Copy),
                             reads=[otres], writes=[osres])
                        deferred.append((ui + 3, lambda osb=osb, osres=osres, g=g, h=h: finalize2(osb, osres, g, h)))

                NU = len(units)
                for idx in range(NU + LA):
                    if idx < NU:
                        emit_S(idx)
                    jdx = idx - LA
                    if jdx >= 0:
                        emit_rest(jdx)
                        while deferred and deferred[0][0] <= jdx:
                            deferred.pop(0)[1]()
                while deferred:
                    deferred.pop(0)[1]()
                P.barrier()
                P.emit()
                P.new_phase()
        if upto <= 2:
            return nc

        with ExitStack() as st:
            wst = Ring(nc, st, "p4ws", 2, [128, D], F32)
            wo_b = sb(st, "p4wo", [128, NCH, D], BF16)
            gmx = sb(st, "p4gm", [128, NCH], F32)
            atr = Ring(nc, st, "p4at", 2, [128, D], F32)
            xr = Ring(nc, st, "p4x", 2, [128, D], F32)
            jr = Ring(nc, st, "p4j", 1, [128, 1024], BF16)
            mxr = Ring(nc, st, "p4mx", 2, [128, D], BF16)
            mtr = Ring(nc, st, "p4mt", 2, [128, NCH, 128], BF16)
            htr = Ring(nc, st, "p4ht", 2, [128, D], F32)
            tpr = Ring(nc, st, "p4tp", 2, [128, 1024], BF16, psum=True)
            psr = Ring(nc, st, "p4ps", 4, [128, 512], F32, psum=True)
            ss = sb(st, "p4ss", [128, 2 * NOB], F32)
            rs = sb(st, "p4rs", [128, 2 * NOB], F32)
            rstd = sb(st, "p4rstd", [128, 2 * NOB], F32)
            P.op("sp", lambda e: e.dma_start(out=gmx[:], in_=g_mix.ap()), writes=["gmx"], dma="c0")
            for c in range(NCH):
                ws, wsres = wst.next()
                P.op("sp" if c % 2 == 0 else "act", lambda e, ws=ws, c=c: e.dma_start(
                    out=ws[:], in_=w_out[c * 128:(c + 1) * 128, :]), writes=[wsres], dma=("p4w", wsres[1]))
                if c % 2 == 0:
                    P.op("act", lambda e, ws=ws, c=c: e.activation(out=wo_b[:, c, :], in_=ws[:], func=AF.Copy,
                                                                  scale=gmx[:, c:c + 1]),
                         reads=[wsres, "gmx"], writes=["wo_b"])
                else:
                    P.op("dve", lambda e, ws=ws, c=c: e.tensor_scalar(
                        out=wo_b[:, c, :], in0=ws[:], scalar1=gmx[:, c:c + 1], scalar2=None, op0=ALU.mult),
                        reads=[wsres, "gmx"], writes=["wo_b"])
            for i in range(NOB):
                at, atres = atr.next()
                xt, xres = xr.next()
                P.op("sp", lambda e, at=at, i=i: e.dma_start(out=at[:], in_=attn_o[i * 128:(i + 1) * 128, :]),
                     reads=[("attn_o", i // 4)], writes=[atres], dma=("p4a", atres[1]))
                P.op("act", lambda e, xt=xt, i=i: e.dma_start(out=xt[:], in_=x_ctx[(NPB + i) * 128:(NPB + i + 1) * 128, :]),
                     writes=[xres], dma=("p4x", xres[1]))
                mx, mxres = mxr.next()
                for k in range(2):
                    col = 2 * i + k
                    jt, jres = jr.next()
                    P.op("act", lambda e, jt=jt, at=at, k=k, col=col: e.activation(
                        out=jt[:], in_=at[:, k * 1024:(k + 1) * 1024], func=AF.Square, accum_out=ss[:, col:col + 1]),
                        reads=[atres], writes=[jres, ("p4ss", col)])
                    P.op("dve", lambda e, col=col: e.tensor_scalar(out=rs[:, col:col + 1], in0=ss[:, col:col + 1],
                                                                 scalar1=1.0 / 1024, scalar2=RMS_EPS, op0=ALU.mult, op1=ALU.add),
                         reads=[("p4ss", col)], writes=[("p4rs", col)])
                    P.op("act", lambda e, col=col: e.activation(out=rs[:, col:col + 1], in_=rs[:, col:col + 1], func=AF.Sqrt),
                         reads=[("p4rs", col)], writes=[("p4rs", col)])
                    P.op("dve", lambda e, col=col: e.reciprocal(out=rstd[:, col:col + 1], in_=rs[:, col:col + 1]),
                         reads=[("p4rs", col)], writes=[("p4rstd", col)])
                    P.op("dve", lambda e, mx=mx, at=at, k=k, col=col: e.tensor_scalar(
                        out=mx[:, k * 1024:(k + 1) * 1024], in0=at[:, k * 1024:(k + 1) * 1024],
                        scalar1=rstd[:, col:col + 1], scalar2=None, op0=ALU.mult),
                        reads=[atres, ("p4rstd", col)], writes=[mxres])
                mt, mtres = mtr.next()
                for half in range(NCH // 8):
                    tp, tpres = tpr.next()
                    for j in range(8):
                        c = half * 8 + j
                        P.op("pe", lambda e, tp=tp, mx=mx, j=j, c=c: e.transpose(
                            out=tp[:, j * 128:(j + 1) * 128], in_=mx[:, c * 128:(c + 1) * 128], identity=ident_b[:]),
                            reads=[mxres, "ident_b"], writes=[tpres])
                    P.op("act", lambda e, mt=mt, tp=tp, half=half: e.activation(
                        out=mt[:, half * 8:(half + 1) * 8, :], in_=tp[:].rearrange("p (c t) -> p c t", c=8), func=AF.Copy),
                        reads=[tpres], writes=[mtres])
                ht, htres = htr.next()
                for n in range(D // 512):
                    ps, psres = psr.next()
                    for c in range(NCH):
                        P.op("pe", lambda e, ps=ps, mt=mt, c=c, n=n: e.matmul(
                            ps[:], lhsT=mt[:, c, :], rhs=wo_b[:, c, n * 512:(n + 1) * 512], start=(c == 0),
                            stop=(c == NCH - 1)), reads=[mtres, "wo_b"], writes=[psres])
                    P.op("dve", lambda e, ht=ht, xt=xt, ps=ps, n=n: e.tensor_tensor(
                        out=ht[:, n * 512:(n + 1) * 512], in0=xt[:, n * 512:(n + 1) * 512], in1=ps[:], op=ALU.add),
                        reads=[xres, psres], writes=[htres])
                P.op("sp", lambda e, ht=ht, i=i: e.dma_start(out=hbuf[i * 128:(i + 1) * 128, :], in_=ht[:]),
                     reads=[htres], writes=[("hbuf", i)], dma=("p4h", htres[1]))
            P.barrier()
            P.emit()
            P.new_phase()
        if upto <= 3:
            return nc

        with ExitStack() as st:
            cir = Ring(nc, st, "cvi", 2, [128, 4 * D], F32)
            cor = Ring(nc, st, "cvo", 2, [128, 4 * D], BF16)
            n_ = 0
            for (src_t, tbl) in ((p_down, 0), (p_up, 1)):
                for ch in range(NEXP // 512):
                    ci, cires = cir.next()
                    co, cores = cor.next()
                    P.op("sp" if n_ % 2 == 0 else "act", lambda e, ci=ci, src_t=src_t, ch=ch: e.dma_start(
                        out=ci[:], in_=AP(src_t, ch * 512 * D, [[4 * D, 128], [1, 4 * D]])),
                        writes=[cires], dma=("cvl", cires[1]))
                    eng = ("act", "dve", "pool")[n_ % 3]
                    if eng == "act":
                        P.op("act", lambda e, ci=ci, co=co: e.activation(out=co[:], in_=ci[:], func=AF.Copy),
                             reads=[cires], writes=[cores])
                    else:
                        P.op(eng, lambda e, ci=ci, co=co: e.tensor_copy(out=co[:], in_=ci[:]),
                             reads=[cires], writes=[cores])
                    P.op("sp" if n_ % 2 == 1 else "act", lambda e, co=co, tbl=tbl, ch=ch: e.dma_start(
                        out=AP(ctab, ch * 512 * 2 * D + tbl * D, [[8 * D, 128], [2 * D, 4], [1, D]]),
                        in_=co[:].rearrange("p (j d) -> p j d", j=4)),
                        reads=[cores], writes=[("cv", n_)], dma=("cvs", cores[1]))
                    n_ += 1
            P.barrier()
            P.emit()
            P.new_phase()

        with ExitStack() as st:
            NS = 128
            wq_b = sb(st, "p5wq", [128, NCH, 2048], BF16)
            sk_b = sb(st, "p5sk", [128, 16, 128], BF16)
            gffn = sb(st, "p5gf", [128, D], F32)
            gfin = sb(st, "p5gn", [128, D], F32)
            io_i = sb(st, "p5ioi", [128, 16], I32)
            io_f = sb(st, "p5iof", [128, 16], F32)

            for (t_, src, key) in ((gffn, g_ffn, "gffn"), (gfin, g_fin, "gfin")):
                P.op("sp", lambda e, t_=t_, src=src: e.dma_start(out=t_[:], in_=src.ap()), writes=[key], dma="c0")
            P.op("pool", lambda e: e.iota(io_i[:], [[1, 16]], base=0, channel_multiplier=0), writes=["io_i"])
            P.op("dve", lambda e: e.tensor_copy(out=io_f[:], in_=io_i[:]), reads=["io_i"], writes=["io_f"])
            st2 = ExitStack()
            wst = Ring(nc, st2, "p5ws", 2, [128, D], F32)
            for c in range(NCH):
                ws, wsres = wst.next()
                P.op("sp" if c % 2 == 0 else "act", lambda e, ws=ws, c=c: e.dma_start(
                    out=ws[:], in_=w_pq[c * 128:(c + 1) * 128, :]), writes=[wsres], dma=("p5w", wsres[1]))
                if c % 2 == 0:
                    P.op("act", lambda e, ws=ws, c=c: e.activation(out=wq_b[:, c, :], in_=ws[:], func=AF.Copy),
                         reads=[wsres], writes=["wq_b"])
                else:
                    P.op("dve", lambda e, ws=ws, c=c: e.tensor_copy(out=wq_b[:, c, :], in_=ws[:]),
                         reads=[wsres], writes=["wq_b"])
            ws, wsres = wst.next()
            P.op("sp", lambda e, ws=ws: e.dma_start(out=ws[:], in_=skT_in.ap()), writes=[wsres], dma=("p5w", wsres[1]))
            P.op("dve", lambda e, ws=ws: e.tensor_copy(out=sk_b[:], in_=ws[:].rearrange("p (g n) -> p g n", g=16)),
                 reads=[wsres], writes=["sk_b"])
            P.barrier()
            P.emit()
            P.new_phase()
            st2.close()
            hts = [sb(st, "p5ht%d" % k_, [128, D], F32) for k_ in range(2)]
            hnbs = [sb(st, "p5hnb%d" % k_, [128, D], BF16) for k_ in range(2)]
            hnT = sb(st, "p5hnT", [128, NCH, 128], BF16)
            qTs = sb(st, "p5qT", [128, 4, 128], BF16)
            scr_ = Ring(nc, st, "p5sc", 2, [128, 512], F32)
            scx = sb(st, "p5scx", [128, 128], F32)
            v16 = sb(st, "p5v16", [128, 16, 16], F32)
            i16u = sb(st, "p5i16u", [128, 16, 16], U32)
            i16f = sb(st, "p5i16f", [128, 16, 16], F32)
            work = sb(st, "p5work", [128, 2048], F32)
            cand = work[:].rearrange("p (h c) -> p h c", h=8)
            cscr = sb(st, "p5cscr", [128, 256], F32)
            tv = sb(st, "p5tv", [128, 8, 16], F32)
            tcu = sb(st, "p5tcu", [128, 8, 16], U32)
            abu = sb(st, "p5abu", [128, 2, 128], U32)
            abf = sb(st, "p5abf", [128, 2, 128], F32)
            eq = work[:].rearrange("p (h k a) -> p h k a", h=8, k=16)
            i12 = sb(st, "p5i12", [128, 2, 128], F32)
            eidf = sb(st, "p5eidf", [128, NS], F32)
            eidxs = [sb(st, "p5eidx%d" % k_, [128, NS], I32) for k_ in range(2)]
            gtss = [sb(st, "p5gts%d" % k_, [128, 8, 16], F32) for k_ in range(2)]
            gsum = sb(st, "p5gsum", [128, 8], F32)
            hdn = sb(st, "p5hdn", [128, NS], F32)
            zc = sb(st, "p5zc", [128, NS], F32)
            ur = Ring(nc, st, "p5u", 6, [128, 2 * D], BF16)
            junk = sb(st, "p5junk", [128, D], BF16)
            dgr = Ring(nc, st, "p5dg", 4, [128, 128], BF16)
            ot_ = work
            ss = sb(st, "p5ss", [128, 4], F32)
            tpr = Ring(nc, st, "p5tp", 1, [128, 1024], BF16, psum=True)
            mmr = Ring(nc, st, "p5mm", 3, [128, 512], F32, psum=True)
            ybk = [st.enter_context(nc.psum_tensor("p5y%d" % n, [128, 512], F32)) for n in range(4)]

            def rms_rstd(src_ap, col, key):
                jt = junk
                P.op("act", lambda e: e.activation(out=jt[:], in_=src_ap, func=AF.Square, accum_out=ss[:, col:col + 1]),
                     reads=[key], writes=[("p5ss", col)])
                P.op("dve", lambda e: e.tensor_scalar(out=ss[:, col:col + 1], in0=ss[:, col:col + 1], scalar1=1.0 / D,
                                                      scalar2=RMS_EPS, op0=ALU.mult, op1=ALU.add),
                     reads=[("p5ss", col)], writes=[("p5ss", col)])
                P.op("act", lambda e: e.activation(out=ss[:, col:col + 1], in_=ss[:, col:col + 1], func=AF.Sqrt),
                     reads=[("p5ss", col)], writes=[("p5ss", col)])
                P.op("dve", lambda e: e.reciprocal(out=ss[:, col + 2:col + 3], in_=ss[:, col:col + 1]),
                     reads=[("p5ss", col)], writes=[("p5ss", col + 2)])
                return ss[:, col + 2:col + 3], ("p5ss", col + 2)

            def top16(src_ap, scratch_ap, vout, iout, rkeys, skey, wkeys):
                P.op("dve", lambda e: e.max(out=vout[0], in_=src_ap), reads=rkeys, writes=[wkeys[0]])
                P.op("dve", lambda e: e.max_index(out=iout[0], in_max=vout[0], in_values=src_ap),
                     reads=rkeys + [wkeys[0]], writes=[wkeys[1]])
                P.op("dve", lambda e: e.match_replace(out=scratch_ap, in_to_replace=vout[0], in_values=src_ap,
                                                      imm_value=-1e30), reads=rkeys + [wkeys[0]], writes=[skey])
                P.op("dve", lambda e: e.max(out=vout[1], in_=scratch_ap), reads=[skey], writes=[wkeys[0]])
                P.op("dve", lambda e: e.max_index(out=iout[1], in_max=vout[1], in_values=scratch_ap),
                     reads=[skey, wkeys[0]], writes=[wkeys[1]])

            def tile_4b(i):
                ht, eidx, hnb, gts = hts[i % 2], eidxs[i % 2], hnbs[i % 2], gtss[i % 2]
                P.op("sp", lambda e, i=i: e.dma_start(out=ht[:], in_=hbuf[i * 128:(i + 1) * 128, :]),
                     reads=[("hbuf", i)], writes=[("ht", i % 2)], dma="p5h")
                rstd_ap, rkey = rms_rstd(ht[:], 0, ("ht", i % 2))
                P.op("dve", lambda e, rstd_ap=rstd_ap: e.scalar_tensor_tensor(
                    out=hnb[:], in0=ht[:], scalar=rstd_ap, in1=gffn[:], op0=ALU.mult, op1=ALU.mult),
                    reads=[("ht", i % 2), rkey, "gffn"], writes=[("hnb", i % 2)])
                for half in range(NCH // 8):
                    tp, tpres = tpr.next()
                    for j in range(8):
                        c = half * 8 + j
                        P.op("pe", lambda e, tp=tp, j=j, c=c: e.transpose(
                            out=tp[:, j * 128:(j + 1) * 128], in_=hnb[:, c * 128:(c + 1) * 128], identity=ident_b[:]),
                            reads=[("hnb", i % 2), "ident_b"], writes=[tpres])
                    P.op("act", lambda e, tp=tp, half=half: e.activation(
                        out=hnT[:, half * 8:(half + 1) * 8, :], in_=tp[:].rearrange("p (c t) -> p c t", c=8), func=AF.Copy),
                        reads=[tpres], writes=["hnT"])
                for qq in range(4):
                    mm, mmres = mmr.next()
                    for gi in range(4):
                        G = 4 * qq + gi
                        for c in range(NCH):
                            P.op("pe", lambda e, mm=mm, gi=gi, G=G, c=c: e.matmul(
                                mm[:, gi * 128:(gi + 1) * 128], lhsT=wq_b[:, c, G * 128:(G + 1) * 128], rhs=hnT[:, c, :],
                                start=(c == 0), stop=(c == NCH - 1)), reads=["wq_b", "hnT"], writes=[mmres])
                    P.op("act", lambda e, mm=mm: e.activation(out=qTs[:], in_=mm[:].rearrange("p (g t) -> p g t", g=4),
                                                              func=AF.Copy), reads=[mmres], writes=["qTs"])
                    mm2, mm2res = mmr.next()
                    for gi in range(4):
                        G = 4 * qq + gi
                        P.op("pe", lambda e, mm2=mm2, gi=gi, G=G: e.matmul(
                            mm2[:, gi * 128:(gi + 1) * 128], lhsT=qTs[:, gi, :], rhs=sk_b[:, G, :], start=True, stop=True),
                            reads=["qTs", "sk_b"], writes=[mm2res])
                    sc, scres = scr_.next()
                    P.op("act", lambda e, sc=sc, mm2=mm2: e.activation(out=sc[:], in_=mm2[:], func=AF.Copy),
                         reads=[mm2res], writes=[scres])
                    for gi in range(4):
                        G = 4 * qq + gi
                        top16(sc[:, gi * 128:(gi + 1) * 128], scx[:], (v16[:, G, 0:8], v16[:, G, 8:16]),
                              (i16u[:, G, 0:8], i16u[:, G, 8:16]), [scres], "scx", ["v16", "i16u"])
                P.op("dve", lambda e: e.tensor_copy(out=i16f[:], in_=i16u[:]), reads=["i16u"], writes=["i16f"])
                P.op("dve", lambda e: e.tensor_tensor(
                    out=cand.rearrange("p h (a b) -> p h a b", a=16),
                    in0=AP(v16, 0, [[256, 128], [32, 8], [1, 16], [0, 16]]),
                    in1=AP(v16, 16, [[256, 128], [32, 8], [0, 16], [1, 16]]), op=ALU.add),
                    reads=["v16"], writes=["work"])
                for h in range(8):
                    top16(cand[:, h, :], cscr[:], (tv[:, h, 0:8], tv[:, h, 8:16]), (tcu[:, h, 0:8], tcu[:, h, 8:16]),
                          ["work"], "cscr", ["tv", "tcu"])
                P.op("dve", lambda e: e.tensor_tensor(out=gts[:], in0=tv[:], in1=AP(tv, 0, [[128, 128], [16, 8], [0, 16]]),
                                                      op=ALU.subtract), reads=["tv"], writes=[("gts", i % 2)])
                P.op("act", lambda e: e.activation(out=gts[:], in_=gts[:], func=AF.Exp), reads=[("gts", i % 2)], writes=[("gts", i % 2)])
                P.op("dve", lambda e: e.tensor_reduce(out=gsum[:], in_=gts[:], axis=AX.X, op=ALU.add),
                     reads=[("gts", i % 2)], writes=["gsum"])
                P.op("dve", lambda e: e.reciprocal(out=gsum[:], in_=gsum[:]), reads=["gsum"], writes=["gsum"])
                P.op("dve", lambda e: e.tensor_tensor(out=gts[:], in0=gts[:], in1=AP(gsum, 0, [[8, 128], [1, 8], [0, 16]]),
                                                      op=ALU.mult), reads=[("gts", i % 2), "gsum"], writes=[("gts", i % 2)])
                tcu2 = tcu[:].rearrange("p h k -> p (h k)")
                P.op("dve", lambda e: e.tensor_scalar(out=abu[:, 0, :], in0=tcu2, scalar1=4, scalar2=None,
                                                      op0=ALU.logical_shift_right), reads=["tcu"], writes=["abu"])
                P.op("dve", lambda e: e.tensor_scalar(out=abu[:, 1, :], in0=tcu2, scalar1=15, scalar2=None,
                                                      op0=ALU.bitwise_and), reads=["tcu"], writes=["abu"])
                P.op("dve", lambda e: e.tensor_copy(out=abf[:], in_=abu[:]), reads=["abu"], writes=["abf"])
                for w in range(2):
                    P.op("dve", lambda e, w=w: e.tensor_tensor(
                        out=eq, in0=AP(abf, w * 128, [[256, 128], [16, 8], [1, 16], [0, 16]]),
                        in1=AP(io_f, 0, [[16, 128], [0, 8], [0, 16], [1, 16]]), op=ALU.is_equal),
                        reads=["abf", "io_f"], writes=["work"])
                    P.op("dve", lambda e, w=w: e.tensor_tensor(
                        out=eq, in0=eq, in1=AP(i16f, w * 16, [[256, 128], [32, 8], [0, 16], [1, 16]]), op=ALU.mult),
                        reads=["work", "i16f"], writes=["work"])
                    P.op("dve", lambda e, w=w: e.tensor_reduce(
                        out=i12[:, w, :], in_=eq.rearrange("p h k a -> p (h k) a"), axis=AX.X, op=ALU.add),
                        reads=["work"], writes=["i12"])
                P.op("dve", lambda e: e.scalar_tensor_tensor(out=eidf[:], in0=i12[:, 0, :], scalar=128.0, in1=i12[:, 1, :],
                                                             op0=ALU.mult, op1=ALU.add), reads=["i12"], writes=["eidf"])
                P.op("dve", lambda e: e.tensor_copy(out=eidx[:], in_=eidf[:]), reads=["eidf"], writes=[("eidx", i % 2)])
            slot_st = {}

            def slot_a(i, s_):
                eidx, hnb = eidxs[i % 2], hnbs[i % 2]
                u, ures = ur.next()
                slot_st[(i, s_)] = (u, ures)
                P.op("pool", lambda e, u=u: e.indirect_dma_start(
                    out=u[:], out_offset=None, in_=ctab.ap(),
                    in_offset=bass.IndirectOffsetOnAxis(ap=eidx[:, s_:s_ + 1], axis=0)),
                    reads=[("eidx", i % 2)], writes=[ures], dma=("p5g", ures[1], i % 2))
                P.op("dve", lambda e, u=u: e.scalar_tensor_tensor(
                    out=junk[:], in0=u[:, 0:D], scalar=1.0, in1=hnb[:], op0=ALU.mult, op1=ALU.mult,
                    accum_out=hdn[:, s_:s_ + 1]), reads=[ures, ("hnb", i % 2)], writes=[("hdn", s_)])
                P.op("act", lambda e: e.activation(out=zc[:, s_:s_ + 1], in_=hdn[:, s_:s_ + 1], func=AF.Gelu_apprx_tanh),
                     reads=[("hdn", s_)], writes=[("zc", s_)])

            def slot_b(i, s_):
                gts = gtss[i % 2]
                u, ures = slot_st.pop((i, s_))
                dg, dgres = dgr.next()
                P.op("dve", lambda e, dg=dg: e.tensor_scalar(
                    out=dg[:], in0=ident_f[:], scalar1=zc[:, s_:s_ + 1],
                    scalar2=gts[:].rearrange("p h k -> p (h k)")[:, s_:s_ + 1], op0=ALU.mult, op1=ALU.mult),
                    reads=["ident_f", ("zc", s_), ("gts", i % 2)], writes=[dgres])
                for n in range(4):
                    P.op("pe", lambda e, dg=dg, u=u, n=n: e.matmul(
                        ybk[n][:], lhsT=dg[:], rhs=u[:, D + n * 512:D + (n + 1) * 512], start=(s_ == 0),
                        stop=(s_ == NS - 1)), reads=[dgres, ures], writes=[("y", n)])

            def final(i):
                ht = hts[i % 2]
                for n in range(4):
                    P.op("dve", lambda e, n=n: e.tensor_tensor(out=ot_[:, n * 512:(n + 1) * 512],
                                                                in0=ht[:, n * 512:(n + 1) * 512], in1=ybk[n][:], op=ALU.add),
                         reads=[("ht", i % 2), ("y", n)], writes=["work"])
                rstd_ap, rkey = rms_rstd(ot_[:], 1, "work")
                P.op("dve", lambda e, rstd_ap=rstd_ap: e.scalar_tensor_tensor(
                    out=ot_[:], in0=ot_[:], scalar=rstd_ap, in1=gfin[:], op0=ALU.mult, op1=ALU.mult),
                    reads=["work", rkey, "gfin"], writes=["work"])
                P.op("sp", lambda e, i=i: e.dma_start(out=out[i * 128:(i + 1) * 128, :], in_=ot_[:]),
                     reads=["work"], writes=[("out", i)], dma="p5o")

            def cap4b(i):
                if i >= NOB:
                    return []
                P.capture()
                tile_4b(i)
                return P.end_capture()

            tile_4b(0)
            for i in range(NOB):
                inter = cap4b(i + 1)
                per = (len(inter) + NS - 1) // NS
                LAG = 2
                for s_ in range(NS + LAG):
                    if s_ < NS:
                        slot_a(i, s_)
                    if s_ >= LAG:
                        slot_b(i, s_ - LAG)
                    if s_ < NS:
                        P.replay(inter[s_ * per:(s_ + 1) * per])
                final(i)
            P.barrier()
            P.emit()
            P.finish()
    return nc


def _const_tables(cfg, p):
    NCB, NPB = cfg.NCB, cfg.NPB
    TP = NPB * 128
    idx = np.arange(cfg.TC)
    pos = idx - (0 if p == 1 else TP)
    valid = (pos >= 0).astype(np.float32)
    kvalid = np.ascontiguousarray(valid.reshape(NCB, 128).T)
    inv = (500000.0 ** (-np.arange(0, 16, 2, dtype=np.float32) / 16.0)).astype(np.float32)
    ang = np.maximum(pos, 0).astype(np.float32)[:, None] * inv[None, :]
    cos = np.cos(ang).astype(np.float32).reshape(NCB, 128, 8).transpose(1, 0, 2).reshape(128, NCB * 8)
    sin = np.sin(ang).astype(np.float32).reshape(NCB, 128, 8).transpose(1, 0, 2).reshape(128, NCB * 8)
    kl = np.arange(128)[:, None, None]
    ko = np.arange(20)[None, :, None]
    ql = np.arange(512)[None, None, :]
    delta = (16 - ko) * 128 + ql - kl
    w = ((delta >= 0) & (delta <= 128)).astype(np.float32)
    w += ((delta >= 0) & (delta <= 512) & (delta % 4 == 0)).astype(np.float32)
    w += ((delta >= 0) & (delta <= 2048) & (delta % 16 == 0)).astype(np.float32)
    wcat = np.ascontiguousarray(w.reshape(128, 20 * 512))
    tri = (np.arange(128)[:, None] <= np.arange(128)[None, :]).astype(np.float32)
    return dict(kvalid=kvalid, rope_cos=np.ascontiguousarray(cos), rope_sin=np.ascontiguousarray(sin),
                wcat=wcat, tri=tri, ident=np.eye(128, dtype=np.float32))


def _pc(v, nch):
    return np.ascontiguousarray(np.asarray(v, np.float32).reshape(nch, 128).T)


def shared_inputs(cfg, attn_norm_gain, w_in, forget_bias, fox_out_gain, dil_out_gain, w_out,
                  ffn_norm_gain, peer_query, peer_sub_keys, peer_down, peer_up, final_norm_gain):
    f = lambda a: np.ascontiguousarray(np.asarray(a, np.float32))
    NCH = cfg.NCH
    skT = np.asarray(peer_sub_keys[0], np.float32).reshape(16, 128, 128).transpose(2, 0, 1)
    return dict(
        w_in=f(w_in[0]), g_attn=_pc(attn_norm_gain[0], NCH), fbias=f(forget_bias[0]).reshape(16, 1),
        g_mix=_pc(np.concatenate([np.asarray(fox_out_gain[0]), np.asarray(dil_out_gain[0])]), NCH),
        w_out=f(w_out[0]), g_ffn=np.ascontiguousarray(np.broadcast_to(f(ffn_norm_gain[0])[None, :], (128, cfg.D))),
        w_pq=f(peer_query[0]), skT=np.ascontiguousarray(skT.reshape(128, 16 * 128)),
        p_down=f(peer_down[0]), p_up=f(peer_up[0]),
        g_fin=np.ascontiguousarray(np.broadcast_to(f(final_norm_gain)[None, :], (128, cfg.D))))


def core_inputs(cfg, xb, p, shared, tables):
    TP, TO = cfg.NPB * 128, cfg.TO
    if p == 1:
        x_ctx = np.ascontiguousarray(xb[0:TP + TO])
    else:
        x_ctx = np.concatenate([np.zeros((TP, cfg.D), np.float32), xb[0:TO]], axis=0)
    d = dict(shared)
    d.update(tables[p])
    d["x_ctx"] = x_ctx
    return d


def kernel(**inputs):
    cfg = Cfg(32, 32)
    x = np.asarray(inputs["x"], np.float32)
    B, S, D = x.shape
    sh = shared_inputs(cfg, **{k: np.asarray(v) for k, v in inputs.items() if k != "x"})
    tables = [_const_tables(cfg, 0), _const_tables(cfg, 1)]
    in_maps = [core_inputs(cfg, x[c // 2], c % 2, sh, tables) for c in range(2 * B)]
    nc = build(cfg)
    res = run_bass_kernel_spmd(nc, in_maps, core_ids=list(range(2 * B)))
    outp = np.empty((B, S, D), np.float32)
    for c in range(2 * B):
        outp[c // 2, (c % 2) * cfg.TO:(c % 2 + 1) * cfg.TO] = np.asarray(res.results[c]["out"], np.float32)
    return outp
```

```python
import numpy as np
from contextlib import ExitStack

import concourse.bass as bass
import concourse.mybir as mybir
from concourse.bass_utils import run_bass_kernel_spmd

F32 = mybir.dt.float32
BF16 = mybir.dt.bfloat16
I32 = mybir.dt.int32
U32 = mybir.dt.uint32
AF = mybir.ActivationFunctionType
ALU = mybir.AluOpType
AX = mybir.AxisListType

HD = 64
NH = 16
RMS_EPS = 1e-6
NEXP = 16384


class Cfg:
    def __init__(self, npb=32, nob=32, D=2048):
        self.NPB = npb
        self.NOB = nob
        self.NCB = npb + nob
        self.D = D
        self.NCH = D // 128
        self.TC = self.NCB * 128
        self.TO = self.NOB * 128
        self.IN_COLS = 3 * 1024 + 16 + 3 * 1024


class Prog:
    ENGS = ("pe", "act", "dve", "pool", "sp")

    def __init__(self, nc, stack):
        self.nc = nc
        self.stack = stack
        self.all_sems = []
        self.dpool = []
        self.nphase = 0
        self.streams = {e: [] for e in self.ENGS}
        self.new_phase()

    def _alloc(self, name):
        h = self.stack.enter_context(self.nc.semaphore(name))
        self.all_sems.append(h)
        return h

    def new_phase(self):
        self.nphase += 1
        self.sem = {e: self._alloc("s%d_%s" % (self.nphase, e)) for e in self.ENGS}
        self.cnt = {e: 0 for e in self.ENGS}
        self.dsem = {}
        self.dnext = 0
        self.known = {e: {} for e in self.ENGS}
        self.last_w = {}
        self.readers = {}

    def finish(self):
        sems = list(self.all_sems)
        with self.nc.Block() as block:
            @block.gpsimd
            def _(e):
                for h in sems:
                    e.sem_clear(h)

    def _semh(self, key):
        if isinstance(key, str):
            return self.sem[key]
        return self.dsem[key[1]][0]

    def capture(self):
        self._cap = []

    def end_capture(self):
        c, self._cap = self._cap, None
        return c

    def replay(self, items):
        for it in items:
            self.op(*it)

    def op(self, eng, fn, reads=(), writes=(), dma=None):
        if getattr(self, "_cap", None) is not None:
            self._cap.append((eng, fn, tuple(reads), tuple(writes), dma))
            return
        deps = {}

        def add(ev):
            if ev is None:
                return
            k, v = ev
            if deps.get(k, 0) < v:
                deps[k] = v

        writes = list(writes)
        if dma is not None:
            writes.append(("dmakey", dma))
        for r in reads:
            add(self.last_w.get(r))
        for w in writes:
            add(self.last_w.get(w))
            for k, v in self.readers.get(w, {}).items():
                add((k, v))
        is_dma = dma is not None
        if is_dma:
            if dma not in self.dsem:
                if self.dnext >= len(self.dpool):
                    self.dpool.append([self._alloc("dsem%d" % len(self.dpool)), 0])
                self.dsem[dma] = self.dpool[self.dnext]
                self.dnext += 1
            self.dsem[dma][1] += 1
            ev = (("dma", dma), 16 * self.dsem[dma][1])
            inc = (self.dsem[dma][0], 16)
        else:
            self.cnt[eng] += 1
            ev = (eng, self.cnt[eng])
            inc = (self.sem[eng], 1)
        waits = []
        kn = self.known[eng]
        for k, v in deps.items():
            if k == eng and eng == "pe" and not is_dma:
                continue
            if kn.get(k, 0) >= v:
                continue
            kn[k] = v
            waits.append((self._semh(k), v))
        for w in writes:
            self.last_w[w] = ev
            self.readers[w] = {}
        for r in reads:
            d = self.readers.setdefault(r, {})
            if d.get(ev[0], 0) < ev[1]:
                d[ev[0]] = ev[1]
        self.streams[eng].append((waits, fn, inc))

    def barrier(self):
        allev = [(e, self.cnt[e]) for e in self.ENGS if self.cnt[e] > 0]
        allev += [(("dma", k), 16 * v[1]) for k, v in self.dsem.items()]
        for eng in self.ENGS:
            waits = []
            kn = self.known[eng]
            for k, v in allev:
                if kn.get(k, 0) >= v:
                    continue
                kn[k] = v
                waits.append((self._semh(k), v))
            if waits:
                self.streams[eng].append((waits, None, None))

    def emit(self):
        nc = self.nc
        streams = self.streams
        with nc.Block() as block:
            def run(name, e):
                for waits, fn, inc in streams[name]:
                    for s, v in waits:
                        e.wait_ge(s, v)
                    if fn is not None:
                        fn(e).then_inc(inc[0], inc[1])

            @block.tensor
            def _(e):
                run("pe", e)

            @block.scalar
            def _(e):
                run("act", e)

            @block.vector
            def _(e):
                run("dve", e)

            @block.gpsimd
            def _(e):
                run("pool", e)

            @block.sync
            def _(e):
                run("sp", e)
        self.streams = {e: [] for e in self.ENGS}


def AP(t, off, pat):
    return bass.AP(t, off, [list(x) for x in pat])


class Ring:
    def __init__(self, nc, stack, name, n, shape, dtype, psum=False):
        mk = nc.psum_tensor if psum else nc.sbuf_tensor
        self.t = [stack.enter_context(mk("%s%d" % (name, i), shape, dtype)) for i in range(n)]
        self.name = name
        self.n = n
        self.i = -1

    def next(self):
        self.i += 1
        s = self.i % self.n
        return self.t[s], (self.name, s)


def build(cfg, upto=99, dbg_out=()):
    nc = bass.Bass("TRN2", target_bir_lowering=False)
    D, NCH, NCB, NOB, NPB, TC, TO = cfg.D, cfg.NCH, cfg.NCB, cfg.NOB, cfg.NPB, cfg.TC, cfg.TO
    NTG = NCB // 4
    OTG0 = NPB // 4

    def din(name, shape, dt=F32):
        return nc.dram_tensor(name, list(shape), dt, kind="ExternalInput")

    def dscr(name, shape, dt):
        return nc.dram_tensor(name, list(shape), dt, kind="ExternalOutput" if name in dbg_out else "Internal")

    x_ctx = din("x_ctx", [TC, D])
    kvalid = din("kvalid", [128, NCB])
    rope_cos = din("rope_cos", [128, NCB * 8])
    rope_sin = din("rope_sin", [128, NCB * 8])
    wcat_in = din("wcat", [128, 20 * 512])
    tri_in = din("tri", [128, 128])
    ident_in = din("ident", [128, 128])
    w_in = din("w_in", [D, cfg.IN_COLS])
    g_attn = din("g_attn", [128, NCH])
    fbias = din("fbias", [16, 1])
    g_mix = din("g_mix", [128, NCH])
    w_out = din("w_out", [D, D])
    g_ffn = din("g_ffn", [128, D])
    g_ffn_pc = din("g_ffn_pc", [128, NCH])
    w_pq = din("w_pq", [D, 2048])
    skT_in = din("skT", [128, 16 * 128])
    p_down = din("p_down", [NEXP, D])
    p_up = din("p_up", [NEXP, D])
    g_fin = din("g_fin", [128, D])
    out = nc.dram_tensor("out", [TO, D], F32, kind="ExternalOutput")

    xnT = dscr("xnT", [NCH, 128, TC], BF16)
    qfT = dscr("qfT", [NH, 65, TO], BF16)
    kfT = dscr("kfT", [NH, 65, TC], BF16)
    vf = dscr("vf", [128, NCB, NH, 65], BF16)
    qdT = dscr("qdT", [NH * 64, TO], BF16)
    kdT = dscr("kdT", [NH * 64, TC], BF16)
    vd = dscr("vd", [128, NCB, NH, 65], BF16)
    attn_o = dscr("attn_o", [TO, D], F32)
    hbuf = dscr("hbuf", [TO, D], F32)
    ctab = dscr("ctab", [NEXP, 2 * D], BF16)
    dbg = {}

    with ExitStack() as gstack:
        P = Prog(nc, gstack)
        sb = lambda st, name, shape, dt: st.enter_context(nc.sbuf_tensor(name, list(shape), dt))

        ident_f = sb(gstack, "ident_f", [128, 128], F32)
        ident_b = sb(gstack, "ident_b", [128, 128], BF16)
        negc_tm = sb(gstack, "negc_tm", [128, NCB, 16], F32)
        P.op("sp", lambda e: e.dma_start(out=ident_f[:], in_=ident_in.ap()), writes=["ident_f"], dma="c0")
        P.op("dve", lambda e: e.tensor_copy(out=ident_b[:], in_=ident_f[:]), reads=["ident_f"], writes=["ident_b"])

        with ExitStack() as st:
            xr = Ring(nc, st, "p0x", 2, [128, D], F32)
            jr = Ring(nc, st, "p0j", 1, [128, D], BF16)
            xbr = Ring(nc, st, "p0xb", 2, [128, D], BF16)
            tpr = Ring(nc, st, "p0tp", 4, [128, 1024], BF16, psum=True)
            xtr = Ring(nc, st, "p0xt", 2, [128, NCH, 512], BF16)
            ss = sb(st, "p0ss", [128, NCB], F32)
            rs = sb(st, "p0rs", [128, NCB], F32)
            rstd = sb(st, "p0rstd", [128, NCB], F32)
            for i in range(NCB):
                xt, xres = xr.next()
                P.op("sp", lambda e, xt=xt, i=i: e.dma_start(out=xt[:], in_=x_ctx[i * 128:(i + 1) * 128, :]),
                     writes=[xres], dma=("p0x", xres[1]))
                jt, jres = jr.next()
                P.op("act", lambda e, xt=xt, jt=jt, i=i: e.activation(out=jt[:], in_=xt[:], func=AF.Square,
                                                                    accum_out=ss[:, i:i + 1]),
                     reads=[xres], writes=[jres, ("ss", i)])
                P.op("dve", lambda e, i=i: e.tensor_scalar(out=rs[:, i:i + 1], in0=ss[:, i:i + 1], scalar1=1.0 / D,
                                                           scalar2=RMS_EPS, op0=ALU.mult, op1=ALU.add),
                     reads=[("ss", i)], writes=[("rs", i)])
                P.op("act", lambda e, i=i: e.activation(out=rs[:, i:i + 1], in_=rs[:, i:i + 1], func=AF.Sqrt),
                     reads=[("rs", i)], writes=[("rs", i)])
                P.op("dve", lambda e, i=i: e.reciprocal(out=rstd[:, i:i + 1], in_=rs[:, i:i + 1]),
                     reads=[("rs", i)], writes=[("rstd", i)])
                xb, xbres = xbr.next()
                P.op("dve", lambda e, xt=xt, xb=xb, i=i: e.tensor_scalar(out=xb[:], in0=xt[:], scalar1=rstd[:, i:i + 1],
                                                                        scalar2=None, op0=ALU.mult),
                     reads=[xres, ("rstd", i)], writes=[xbres])
                if i % 4 == 0:
                    xtt, xtres = xtr.next()
                for half in range(NCH // 8):
                    tp, tpres = tpr.next()
                    for j in range(8):
                        c = half * 8 + j
                        P.op("pe", lambda e, tp=tp, xb=xb, j=j, c=c: e.transpose(
                            out=tp[:, j * 128:(j + 1) * 128], in_=xb[:, c * 128:(c + 1) * 128], identity=ident_b[:]),
                            reads=[xbres, "ident_b"], writes=[tpres])
                    eng = "act" if half % 2 == 0 else "dve"
                    o_ap = xtt[:, half * 8:(half + 1) * 8, (i % 4) * 128:(i % 4 + 1) * 128]
                    i_ap = tp[:].rearrange("p (c t) -> p c t", c=8)
                    if eng == "act":
                        P.op("act", lambda e, o_ap=o_ap, i_ap=i_ap: e.activation(out=o_ap, in_=i_ap, func=AF.Copy),
                             reads=[tpres], writes=[xtres])
                    else:
                        P.op("dve", lambda e, o_ap=o_ap, i_ap=i_ap: e.tensor_copy(out=o_ap, in_=i_ap),
                             reads=[tpres], writes=[xtres])
                if i % 4 == 3:
                    g = i // 4
                    dst = AP(xnT, g * 512, [[TC, 128], [128 * TC, NCH], [1, 512]])
                    P.op("sp", lambda e, dst=dst, xtt=xtt: e.dma_start(out=dst, in_=xtt[:]),
                         reads=[xtres], writes=[("xnT", g)], dma=("p0st", xtres[1]))
            P.barrier()
            P.emit()
            P.new_phase()

        if upto <= 0:
            return nc

        with ExitStack() as st:
            wst = Ring(nc, st, "p1ws", 3, [128, 1024], F32)
            wgr = Ring(nc, st, "p1wg", 1, [128, NCH, 1024], BF16)
            xgr = Ring(nc, st, "p1xg", 2, [128, NCH, 512], BF16)
            psr = Ring(nc, st, "p1ps", 4, [128, 512], F32, psum=True)
            tpr = Ring(nc, st, "p1tp", 2, [128, 1024], BF16, psum=True)
            obr = Ring(nc, st, "p1ob", 3, [128, 512], BF16)
            vtr = Ring(nc, st, "p1vt", 2, [128, NH, 65], BF16)
            tmr = Ring(nc, st, "p1tm", 2, [128, 1024], F32)
            rbr = Ring(nc, st, "p1rb", 2, [128, 1024], BF16)
            dtr = Ring(nc, st, "p1dt", 2, [128, 8, 512], BF16)
            gat = sb(st, "p1gat", [128, NCH], F32)
            kv_sb = sb(st, "p1kv", [128, NCB], F32)
            cos_sb = sb(st, "p1cos", [128, NCB * 8], F32)
            sin_sb = sb(st, "p1sin", [128, NCB * 8], F32)
            ra = sb(st, "p1ra", [128, NH, 8], F32)
            rb_ = sb(st, "p1rb_", [128, NH, 8], F32)
            fb_sb = sb(st, "p1fb", [16, 1], F32)
            nfb = sb(st, "p1nfb", [16, 1], F32)
            negc_fm = sb(st, "p1negc", [16, TC], F32)
            ones16 = sb(st, "p1ones", [16, 512], F32)
            onesb = sb(st, "p1onesb", [16, 512], BF16)
            spt = sb(st, "p1spt", [16, 512], F32)
            rqr = Ring(nc, st, "p1rq", 2, [16, 512], BF16)
            for (t_, src, key) in ((gat, g_attn, "gat"), (kv_sb, kvalid, "kv"), (cos_sb, rope_cos, "cos"),
                                   (sin_sb, rope_sin, "sin"), (fb_sb, fbias, "fb")):
                P.op("sp", lambda e, t_=t_, src=src: e.dma_start(out=t_[:], in_=src.ap()), writes=[key], dma="c0")
            P.op("dve", lambda e: e.tensor_scalar(out=nfb[:], in0=fb_sb[:], scalar1=-1.0, scalar2=None, op0=ALU.mult),
                 reads=["fb"], writes=["nfb"])
            P.op("pool", lambda e: e.memset(ones16[:], 1.0), writes=["ones16"])
            P.op("pool", lambda e: e.memset(onesb[:], 1.0), writes=["onesb"])

            groups = [("fq", 0, 1024, "fm", OTG0), ("fk", 1024, 1024, "fm", 0), ("fv", 2048, 1024, "tmv", 0),
                      ("fg", 3072, 16, "fg", 0), ("dq", 3088, 1024, "tmr", OTG0), ("dk", 4112, 1024, "tmr", 0),
                      ("dv", 5136, 1024, "tmv", 0)]
            for (gname, col0, ncols, mode, tg0) in groups:
                wg, wgres = wgr.next()
                for c in range(NCH):
                    ws, wsres = wst.next()
                    P.op("sp" if c % 2 == 0 else "act",
                         lambda e, ws=ws, c=c, col0=col0, ncols=ncols: e.dma_start(
                             out=ws[:, 0:ncols], in_=w_in[c * 128:(c + 1) * 128, col0:col0 + ncols]),
                         writes=[wsres], dma=("p1w", wsres[1]))
                    if c % 2 == 0:
                        P.op("act", lambda e, ws=ws, wg=wg, c=c, ncols=ncols: e.activation(
                            out=wg[:, c, 0:ncols], in_=ws[:, 0:ncols], func=AF.Copy, scale=gat[:, c:c + 1]),
                            reads=[wsres, "gat"], writes=[wgres])
                    else:
                        P.op("dve", lambda e, ws=ws, wg=wg, c=c, ncols=ncols: e.tensor_scalar(
                            out=wg[:, c, 0:ncols], in0=ws[:, 0:ncols], scalar1=gat[:, c:c + 1], scalar2=None,
                            op0=ALU.mult),
                            reads=[wsres, "gat"], writes=[wgres])
                for tg in range(tg0, NTG):
                    xg, xgres = xgr.next()
                    src = AP(xnT, tg * 512, [[TC, 128], [128 * TC, NCH], [1, 512]])
                    P.op("sp", lambda e, xg=xg, src=src: e.dma_start(out=xg[:], in_=src),
                         reads=[("xnT", tg)], writes=[xgres], dma=("p1x", xgres[1]))
                    otg = tg - OTG0
                    if gname == "fk":
                        P.op("act", lambda e, tg=tg: e.dma_start(
                            out=AP(kfT, 64 * TC + tg * 512, [[65 * TC, 16], [1, 512]]), in_=onesb[:]),
                            reads=["onesb"], writes=[("kaug", tg)], dma="kaug")
                    if mode == "fm":
                        dstT, TT, toff, scl = (qfT, TO, otg * 512, 0.125) if gname == "fq" else (kfT, TC, tg * 512, 1.0)
                        for sub in range(8):
                            ps, psres = psr.next()
                            for c in range(NCH):
                                P.op("pe", lambda e, ps=ps, wg=wg, xg=xg, c=c, sub=sub: e.matmul(
                                    ps[:], lhsT=wg[:, c, sub * 128:(sub + 1) * 128], rhs=xg[:, c, :],
                                    start=(c == 0), stop=(c == NCH - 1)),
                                    reads=[wgres, xgres], writes=[psres])
                            ob, obres = obr.next()
                            P.op("act", lambda e, ob=ob, ps=ps, scl=scl: e.activation(out=ob[:], in_=ps[:], func=AF.Copy,
                                                                                      scale=scl),
                                 reads=[psres], writes=[obres])
                            for hh in range(2):
                                h = 2 * sub + hh
                                dst = AP(dstT, h * 65 * TT + toff, [[TT, 64], [1, 512]])
                                P.op("sp" if hh == 0 else "act", lambda e, dst=dst, ob=ob, hh=hh: e.dma_start(
                                    out=dst, in_=ob[hh * 64:(hh + 1) * 64, :]),
                                    reads=[obres], writes=[(gname, h, tg)], dma=("p1o", obres[1], hh))
                    elif mode == "fg":
                        ps, psres = psr.next()
                        for c in range(NCH):
                            P.op("pe", lambda e, ps=ps, wg=wg, xg=xg, c=c: e.matmul(
                                ps[0:16, :], lhsT=wg[:, c, 0:16], rhs=xg[:, c, :], start=(c == 0), stop=(c == NCH - 1)),
                                reads=[wgres, xgres], writes=[psres])
                        P.op("act", lambda e, ps=ps: e.activation(out=spt[:], in_=ps[0:16, :], func=AF.Exp, scale=-1.0,
                                                                  bias=nfb[:]),
                             reads=[psres, "nfb"], writes=["spt"])
                        P.op("act", lambda e: e.activation(out=spt[:], in_=spt[:], func=AF.Ln, bias=1.0),
                             reads=["spt"], writes=["spt"])
                        init = 0.0 if tg == 0 else negc_fm[:, tg * 512 - 1:tg * 512]
                        P.op("dve", lambda e, tg=tg, init=init: e.tensor_tensor_scan(
                            out=negc_fm[:, tg * 512:(tg + 1) * 512], data0=ones16[:], data1=spt[:], initial=init,
                            op0=ALU.mult, op1=ALU.add),
                            reads=["spt", "ones16", "negc_fm"], writes=["negc_fm"])
                        if tg >= OTG0:
                            rq, rqres = rqr.next()
                            P.op("dve", lambda e, rq=rq, tg=tg: e.tensor_scalar(
                                out=rq[:], in0=negc_fm[:, tg * 512:(tg + 1) * 512], scalar1=-1.0, scalar2=None,
                                op0=ALU.mult), reads=["negc_fm"], writes=[rqres])
                            P.op("act", lambda e, rq=rq, otg=otg: e.dma_start(
                                out=AP(qfT, 64 * TO + otg * 512, [[65 * TO, 16], [1, 512]]), in_=rq[:]),
                                reads=[rqres], writes=[("qaug", otg)], dma=("qaug", rqres[1]))
                    else:
                        for ti in range(4):
                            blk = tg * 4 + ti
                            if mode == "tmv":
                                vt, vtres = vtr.next()
                            else:
                                tm, tmres = tmr.next()
                            for half in range(2):
                                ps, psres = psr.next()
                                for c in range(NCH):
                                    P.op("pe", lambda e, ps=ps, wg=wg, xg=xg, c=c, ti=ti, half=half: e.matmul(
                                        ps[:], lhsT=xg[:, c, ti * 128:(ti + 1) * 128],
                                        rhs=wg[:, c, half * 512:(half + 1) * 512], start=(c == 0), stop=(c == NCH - 1)),
                                        reads=[wgres, xgres], writes=[psres])
                                if mode == "tmv":
                                    P.op("act", lambda e, vt=vt, ps=ps, half=half: e.activation(
                                        out=vt[:, half * 8:(half + 1) * 8, 0:64],
                                        in_=ps[:].rearrange("p (h d) -> p h d", h=8), func=AF.Copy),
                                        reads=[psres], writes=[vtres])
                                else:
                                    scl = 0.125 if gname == "dq" else 1.0
                                    P.op("act", lambda e, tm=tm, ps=ps, half=half, scl=scl: e.activation(
                                        out=tm[:, half * 512:(half + 1) * 512], in_=ps[:], func=AF.Copy, scale=scl),
                                        reads=[psres], writes=[tmres])
                            if mode == "tmv":
                                P.op("pool", lambda e, vt=vt, blk=blk: e.tensor_copy(
                                    out=vt[:, :, 64:65], in_=AP(kv_sb, blk, [[NCB, 128], [0, NH], [1, 1]])),
                                    reads=["kv"], writes=[vtres])
                                dstv = vf if gname == "fv" else vd
                                dst = AP(dstv, blk * NH * 65, [[NCB * NH * 65, 128], [1, NH * 65]])
                                P.op("sp", lambda e, dst=dst, vt=vt: e.dma_start(
                                    out=dst, in_=vt[:].rearrange("p h d -> p (h d)")),
                                    reads=[vtres], writes=[(gname, blk)], dma=("p1v", vtres[1]))
                            else:
                                rbt, rbres = rbr.next()
                                P.op("act", lambda e, rbt=rbt, tm=tm: e.activation(out=rbt[:], in_=tm[:], func=AF.Copy),
                                     reads=[tmres], writes=[rbres])
                                tm3 = tm[:].rearrange("p (h d) -> p h d", h=NH)
                                rb3 = rbt[:].rearrange("p (h d) -> p h d", h=NH)
                                x1, x2 = tm3[:, :, 0:8], tm3[:, :, 8:16]
                                cosb = AP(cos_sb, blk * 8, [[NCB * 8, 128], [0, NH], [1, 8]])
                                sinb = AP(sin_sb, blk * 8, [[NCB * 8, 128], [0, NH], [1, 8]])
                                for (o_, a_, ca, b_, cb, opx) in ((rb3[:, :, 0:8], x1, cosb, x2, sinb, ALU.subtract),
                                                                 (rb3[:, :, 8:16], x2, cosb, x1, sinb, ALU.add)):
                                    P.op("dve", lambda e, a_=a_, ca=ca: e.tensor_tensor(out=ra[:], in0=a_, in1=ca, op=ALU.mult),
                                         reads=[tmres, "cos"], writes=["ra"])
                                    P.op("dve", lambda e, b_=b_, cb=cb: e.tensor_tensor(out=rb_[:], in0=b_, in1=cb, op=ALU.mult),
                                         reads=[tmres, "sin"], writes=["rb_"])
                                    P.op("dve", lambda e, o_=o_, opx=opx: e.tensor_tensor(out=o_, in0=ra[:], in1=rb_[:], op=opx),
                                         reads=["ra", "rb_"], writes=[rbres])
                                if ti == 0:
                                    dt_, dtres = dtr.next()
                                tp, tpres = tpr.next()
                                for j in range(8):
                                    P.op("pe", lambda e, tp=tp, rbt=rbt, j=j: e.transpose(
                                        out=tp[:, j * 128:(j + 1) * 128], in_=rbt[:, j * 128:(j + 1) * 128],
                                        identity=ident_b[:]),
                                        reads=[rbres, "ident_b"], writes=[tpres])
                                P.op("dve", lambda e, dt_=dt_, tp=tp, ti=ti: e.tensor_copy(
                                    out=dt_[:, :, ti * 128:(ti + 1) * 128], in_=tp[:].rearrange("p (j t) -> p j t", j=8)),
                                    reads=[tpres], writes=[dtres])
                                if ti == 3:
                                    dstT, TT, toff = (qdT, TO, otg * 512) if gname == "dq" else (kdT, TC, tg * 512)
                                    dst = AP(dstT, toff, [[TT, 128], [128 * TT, 8], [1, 512]])
                                    P.op("sp", lambda e, dst=dst, dt_=dt_: e.dma_start(out=dst, in_=dt_[:]),
                                         reads=[dtres], writes=[(gname, tg)], dma=("p1d", dtres[1]))
                if mode == "fg":
                    for b0 in range(0, NCB, 32):
                        ps, psres = psr.next()
                        nb = min(32, NCB - b0)
                        for bb in range(nb):
                            P.op("pe", lambda e, ps=ps, b0=b0, bb=bb: e.transpose(
                                out=ps[:, bb * 16:(bb + 1) * 16], in_=negc_fm[:, (b0 + bb) * 128:(b0 + bb + 1) * 128],
                                identity=ident_f[0:16, 0:16]),
                                reads=["negc_fm", "ident_f"], writes=[psres])
                        P.op("dve", lambda e, ps=ps, b0=b0, nb=nb: e.tensor_copy(
                            out=negc_tm[:, b0:b0 + nb, :], in_=ps[:, 0:nb * 16].rearrange("p (b h) -> p b h", h=16)),
                            reads=[psres], writes=["negc_tm"])
            P.barrier()
            P.emit()
            P.new_phase()
        if upto <= 1:
            return nc

        for kind in ("fox", "dil"):
            with ExitStack() as st:
                KR = 65 if kind == "fox" else 64
                qr = Ring(nc, st, kind + "q", 2, [65, TO], BF16)
                kr = Ring(nc, st, kind + "k", 2, [65, TC], BF16)
                if kind == "dil":
                    for r_ in range(2):
                        P.op("pool", lambda e, r_=r_: e.memset(qr.t[r_][64:65, :], 0.0), writes=[(qr.name, r_)])
                        P.op("pool", lambda e, r_=r_: e.memset(kr.t[r_][64:65, :], 0.0), writes=[(kr.name, r_)])
                vr = Ring(nc, st, kind + "v", 2, [128, NCB, 4 * 65], BF16)
                sr = Ring(nc, st, kind + "s", 4, [128, 512], F32, psum=True)
                otr = Ring(nc, st, kind + "ot", 2, [128, 512], F32, psum=True)
                tor = Ring(nc, st, kind + "to", 2, [128, 4, 128], F32, psum=True)
                ptr_ = Ring(nc, st, kind + "pt", 4, [128, 512], BF16)
                osr = Ring(nc, st, kind + "os", 2, [65, 512], F32)
                rcr = Ring(nc, st, kind + "rc", 2, [128, 4], F32)
                ofr = Ring(nc, st, kind + "of", 2, [128, 4, 64], F32)
                cst = sb(st, kind + "cst", [128, 5 * 512], F32)
                if kind == "fox":
                    tri_b = sb(st, "tri_b", [128, 128], BF16)
                    P.op("sp", lambda e: e.dma_start(out=cst[:, 0:128], in_=tri_in.ap()), writes=["cst"], dma="c0")
                    P.op("dve", lambda e: e.tensor_copy(out=tri_b[:], in_=cst[:, 0:128]), reads=["cst"], writes=["tri_b"])
                else:
                    wcat_b = sb(st, "wcat_b", [128, 20, 512], BF16)
                    for pc in range(4):
                        P.op("sp", lambda e, pc=pc: e.dma_start(out=cst[:], in_=wcat_in[:, pc * 2560:(pc + 1) * 2560]),
                             writes=["cst"], dma="c0")
                        P.op("dve", lambda e, pc=pc: e.tensor_copy(
                            out=wcat_b[:, pc * 5:(pc + 1) * 5, :], in_=cst[:].rearrange("p (a b) -> p a b", a=5)),
                            reads=["cst"], writes=["wcat_b"])
                qsrc, ksrc, vsrc = (qfT, kfT, vf) if kind == "fox" else (qdT, kdT, vd)
                coff = 0 if kind == "fox" else 1024
                LA = 3
                units = []
                loads = {}
                head_starts = []
                for hg in range(4):
                    for hh in range(4):
                        h = hg * 4 + hh
                        head_starts.append((len(units), hg, hh))
                        for g in range(NOB // 4):
                            qb0 = NPB + 4 * g
                            kb_lo = 0 if kind == "fox" else max(0, qb0 - 16)
                            kbs = list(range(kb_lo, qb0 + 4))
                            for kb in kbs:
                                j = kb - qb0
                                q0 = (max(0, j) * 128) if kind == "fox" else 0
                                units.append((h, hh, g, kb, q0, kb == kbs[0], kb == kbs[-1], kb - (qb0 - 16), j))
                for n_, (u0, hg_, hh_) in enumerate(head_starts):
                    loads.setdefault(0 if n_ == 0 else head_starts[n_ - 1][0], []).append((hg_, hh_))
                cur = {}
                hstate = {}
                sinfo = {}
                deferred = []
                state = {"vg": None}

                def issue_loads(hg, hh):
                    h = hg * 4 + hh
                    if hh == 0:
                        vg, vgres = vr.next()
                        nvp = max(1, NCB // 16)
                        for vp in range(nvp):
                            nb_ = NCB // nvp
                            P.op("act", lambda e, vg=vg, hg=hg, vp=vp, nb_=nb_: e.dma_start(
                                out=vg[:, vp * nb_:(vp + 1) * nb_, :],
                                in_=AP(vsrc, hg * 4 * 65 + vp * nb_ * NH * 65,
                                       [[NCB * NH * 65, 128], [NH * 65, nb_], [1, 4 * 65]])),
                                writes=[vgres], dma=(kind + "v", vgres[1], vp))
                        state["vg"] = (vg, vgres)
                    qT, qres = qr.next()
                    kT, kres = kr.next()
                    P.op("sp", lambda e, qT=qT, h=h: e.dma_start(
                        out=qT[0:KR, :], in_=AP(qsrc, h * KR * TO, [[TO, KR], [1, TO]])),
                        writes=[qres], dma=(kind + "q", qres[1]))
                    P.op("sp", lambda e, kT=kT, h=h: e.dma_start(
                        out=kT[0:KR, :], in_=AP(ksrc, h * KR * TC, [[TC, KR], [1, TC]])),
                        writes=[kres], dma=(kind + "k", kres[1]))
                    hstate[h] = (qT, qres, kT, kres, state["vg"][0], state["vg"][1])

                def emit_S(ui):
                    (h, hh, g, kb, q0, first, last, ko, j) = units[ui]
                    for (hg_, hh_) in loads.get(ui, []):
                        issue_loads(hg_, hh_)
                    qT, qres, kT, kres, vg, vgres = hstate[h]
                    S, sres = sr.next()
                    sinfo[ui] = (S, sres)
                    P.op("pe", lambda e, S=S, kT=kT, qT=qT, kb=kb, g=g, q0=q0: e.matmul(
                        S[:, q0:512], lhsT=kT[:, kb * 128:(kb + 1) * 128],
                        rhs=qT[:, g * 512 + q0:(g + 1) * 512], start=True, stop=True),
                        reads=[kres, qres], writes=[sres])

                def finalize2(osb, osres, g, h):
                    to, tores = tor.next()
                    for jj in range(4):
                        P.op("pe", lambda e, to=to, osb=osb, jj=jj: e.transpose(
                            out=to[:, jj, 0:65], in_=osb[:, jj * 128:(jj + 1) * 128], identity=ident_f[0:65, 0:65]),
                            reads=[osres, "ident_f"], writes=[tores])
                    rc, rcres = rcr.next()
                    of, ofres = ofr.next()
                    P.op("dve", lambda e, rc=rc, to=to: e.reciprocal(out=rc[:], in_=to[:, :, 64]),
                         reads=[tores], writes=[rcres])
                    P.op("dve", lambda e, rc=rc, to=to, of=of: e.tensor_tensor(
                        out=of[:], in0=to[:, :, 0:64], in1=AP(rc, 0, [[4, 128], [1, 4], [0, 64]]), op=ALU.mult),
                        reads=[tores, rcres], writes=[ofres])
                    P.op("sp", lambda e, of=of, g=g, h=h: e.dma_start(
                        out=AP(attn_o, g * 512 * D + coff + h * 64, [[D, 128], [128 * D, 4], [1, 64]]), in_=of[:]),
                        reads=[ofres], writes=[("attn_o", g)], dma=(kind + "o", ofres[1]))

                def emit_rest(ui):
                    (h, hh, g, kb, q0, first, last, ko, j) = units[ui]
                    qT, qres, kT, kres, vg, vgres = hstate[h]
                    S, sres = sinfo.pop(ui)
                    if first:
                        cur["ot"] = otr.next()
                    ot, otres = cur["ot"]
                    Pt, ptres = ptr_.next()
                    if kind == "fox":
                        P.op("act", lambda e, Pt=Pt, S=S, q0=q0, kb=kb, h=h: e.activation(
                            out=Pt[:, q0:512], in_=S[:, q0:512], func=AF.Exp, bias=negc_tm[:, kb, h:h + 1]),
                            reads=[sres, "negc_tm"], writes=[ptres])
                        if j >= 0:
                            P.op("dve", lambda e, Pt=Pt, q0=q0: e.tensor_tensor(
                                out=Pt[:, q0:q0 + 128], in0=Pt[:, q0:q0 + 128], in1=tri_b[:], op=ALU.mult),
                                reads=[ptres, "tri_b"], writes=[ptres])
                    else:
                        P.op("act", lambda e, Pt=Pt, S=S: e.activation(out=Pt[:], in_=S[:], func=AF.Exp),
                             reads=[sres], writes=[ptres])
                        P.op("dve", lambda e, Pt=Pt, ko=ko: e.tensor_tensor(
                            out=Pt[:], in0=Pt[:], in1=wcat_b[:, ko, :], op=ALU.mult),
                            reads=[ptres, "wcat_b"], writes=[ptres])
                    P.op("pe", lambda e, ot=ot, vg=vg, Pt=Pt, kb=kb, hh=hh, q0=q0, first=first, last=last: e.matmul(
                        ot[0:65, q0:512], lhsT=vg[:, kb, hh * 65:(hh + 1) * 65], rhs=Pt[:, q0:512],
                        start=first, stop=last),
                        reads=[vgres, ptres], writes=[otres])
                    if last:
                        osb, osres = osr.next()
                        P.op("act", lambda e, osb=osb, ot=ot: e.activation(out=osb[:], in_=ot[0:65, :], func=AF.Copy),
                             reads=[otres], writes=[osres])
                        deferred.append((ui + 3, lambda osb=osb, osres=osres, g=g, h=h: finalize2(osb, osres, g, h)))

                NU = len(units)
                conv = []
                if kind == "fox":
                    cvi = sb(st, "cvi", [128, 2 * D], F32)
                    cvo = sb(st, "cvo", [128, 2 * D], BF16)
                    gfc = sb(st, "gfc", [128, D], F32)
                    P.op("pool", lambda e: e.dma_start(out=gfc[:], in_=g_ffn.ap()), writes=["gfc"], dma="gfc")
                    for (src_t, tbl) in ((p_down, 0), (p_up, 1)):
                        for ch in range(NEXP // 256):
                            conv.append((src_t, tbl, ch))

                def conv_chunk(src_t, tbl, ch):
                    P.op("pool", lambda e: e.dma_start(
                        out=cvi[:], in_=AP(src_t, ch * 256 * D, [[2 * D, 128], [1, 2 * D]])),
                        writes=["cvi"], dma="cvl")
                    if tbl == 0:
                        P.op("pool", lambda e: e.tensor_tensor(
                            out=cvo[:].rearrange("p (j d) -> p j d", j=2), in0=cvi[:].rearrange("p (j d) -> p j d", j=2),
                            in1=AP(gfc, 0, [[D, 128], [0, 2], [1, D]]), op=ALU.mult),
                            reads=["cvi", "gfc"], writes=["cvo"])
                    else:
                        P.op("pool", lambda e: e.tensor_copy(out=cvo[:], in_=cvi[:]), reads=["cvi"], writes=["cvo"])
                    P.op("pool", lambda e: e.dma_start(
                        out=AP(ctab, ch * 256 * 2 * D + tbl * D, [[4 * D, 128], [2 * D, 2], [1, D]]),
                        in_=cvo[:].rearrange("p (j d) -> p j d", j=2)),
                        reads=["cvo"], writes=[("cv", tbl, ch)], dma="cvs")

                cstep = max(1, (NU - 8) // max(1, len(conv)))
                for idx in range(NU + LA):
                    if conv and idx % cstep == 0:
                        conv_chunk(*conv.pop(0))
                    if idx < NU:
                        emit_S(idx)
                    jdx = idx - LA
                    if jdx >= 0:
                        emit_rest(jdx)
                        while deferred and deferred[0][0] <= jdx:
                            deferred.pop(0)[1]()
                while deferred:
                    deferred.pop(0)[1]()
                while conv:
                    conv_chunk(*conv.pop(0))
                P.barrier()
                P.emit()
                P.new_phase()
        if upto <= 2:
            return nc

        with ExitStack() as st:
            wst = Ring(nc, st, "p4ws", 2, [128, D], F32)
            wo_b = sb(st, "p4wo", [128, NCH, D], BF16)
            gmx = sb(st, "p4gm", [128, NCH], F32)
            atr = Ring(nc, st, "p4at", 2, [128, D], F32)
            xr = Ring(nc, st, "p4x", 2, [128, D], F32)
            jr = Ring(nc, st, "p4j", 1, [128, 1024], BF16)
            mxr = Ring(nc, st, "p4mx", 2, [128, D], BF16)
            mtr = Ring(nc, st, "p4mt", 2, [128, NCH, 128], BF16)
            htr = Ring(nc, st, "p4ht", 2, [128, D], F32)
            tpr = Ring(nc, st, "p4tp", 2, [128, 1024], BF16, psum=True)
            psr = Ring(nc, st, "p4ps", 4, [128, 512], F32, psum=True)
            ss = sb(st, "p4ss", [128, 2 * NOB], F32)
            rs = sb(st, "p4rs", [128, 2 * NOB], F32)
            rstd = sb(st, "p4rstd", [128, 2 * NOB], F32)
            P.op("sp", lambda e: e.dma_start(out=gmx[:], in_=g_mix.ap()), writes=["gmx"], dma="c0")
            for c in range(NCH):
                ws, wsres = wst.next()
                P.op("sp" if c % 2 == 0 else "act", lambda e, ws=ws, c=c: e.dma_start(
                    out=ws[:], in_=w_out[c * 128:(c + 1) * 128, :]), writes=[wsres], dma=("p4w", wsres[1]))
                if c % 2 == 0:
                    P.op("act", lambda e, ws=ws, c=c: e.activation(out=wo_b[:, c, :], in_=ws[:], func=AF.Copy,
                                                                  scale=gmx[:, c:c + 1]),
                         reads=[wsres, "gmx"], writes=["wo_b"])
                else:
                    P.op("dve", lambda e, ws=ws, c=c: e.tensor_scalar(
                        out=wo_b[:, c, :], in0=ws[:], scalar1=gmx[:, c:c + 1], scalar2=None, op0=ALU.mult),
                        reads=[wsres, "gmx"], writes=["wo_b"])
            for i in range(NOB):
                at, atres = atr.next()
                xt, xres = xr.next()
                P.op("sp", lambda e, at=at, i=i: e.dma_start(out=at[:], in_=attn_o[i * 128:(i + 1) * 128, :]),
                     reads=[("attn_o", i // 4)], writes=[atres], dma=("p4a", atres[1]))
                P.op("act", lambda e, xt=xt, i=i: e.dma_start(out=xt[:], in_=x_ctx[(NPB + i) * 128:(NPB + i + 1) * 128, :]),
                     writes=[xres], dma=("p4x", xres[1]))
                mx, mxres = mxr.next()
                for k in range(2):
                    col = 2 * i + k
                    jt, jres = jr.next()
                    P.op("act", lambda e, jt=jt, at=at, k=k, col=col: e.activation(
                        out=jt[:], in_=at[:, k * 1024:(k + 1) * 1024], func=AF.Square, accum_out=ss[:, col:col + 1]),
                        reads=[atres], writes=[jres, ("p4ss", col)])
                    P.op("dve", lambda e, col=col: e.tensor_scalar(out=rs[:, col:col + 1], in0=ss[:, col:col + 1],
                                                                 scalar1=1.0 / 1024, scalar2=RMS_EPS, op0=ALU.mult, op1=ALU.add),
                         reads=[("p4ss", col)], writes=[("p4rs", col)])
                    P.op("act", lambda e, col=col: e.activation(out=rs[:, col:col + 1], in_=rs[:, col:col + 1], func=AF.Sqrt),
                         reads=[("p4rs", col)], writes=[("p4rs", col)])
                    P.op("dve", lambda e, col=col: e.reciprocal(out=rstd[:, col:col + 1], in_=rs[:, col:col + 1]),
                         reads=[("p4rs", col)], writes=[("p4rstd", col)])
                    P.op("dve", lambda e, mx=mx, at=at, k=k, col=col: e.tensor_scalar(
                        out=mx[:, k * 1024:(k + 1) * 1024], in0=at[:, k * 1024:(k + 1) * 1024],
                        scalar1=rstd[:, col:col + 1], scalar2=None, op0=ALU.mult),
                        reads=[atres, ("p4rstd", col)], writes=[mxres])
                mt, mtres = mtr.next()
                for half in range(NCH // 8):
                    tp, tpres = tpr.next()
                    for j in range(8):
                        c = half * 8 + j
                        P.op("pe", lambda e, tp=tp, mx=mx, j=j, c=c: e.transpose(
                            out=tp[:, j * 128:(j + 1) * 128], in_=mx[:, c * 128:(c + 1) * 128], identity=ident_b[:]),
                            reads=[mxres, "ident_b"], writes=[tpres])
                    P.op("act", lambda e, mt=mt, tp=tp, half=half: e.activation(
                        out=mt[:, half * 8:(half + 1) * 8, :], in_=tp[:].rearrange("p (c t) -> p c t", c=8), func=AF.Copy),
                        reads=[tpres], writes=[mtres])
                ht, htres = htr.next()
                for n in range(D // 512):
                    ps, psres = psr.next()
                    for c in range(NCH):
                        P.op("pe", lambda e, ps=ps, mt=mt, c=c, n=n: e.matmul(
                            ps[:], lhsT=mt[:, c, :], rhs=wo_b[:, c, n * 512:(n + 1) * 512], start=(c == 0),
                            stop=(c == NCH - 1)), reads=[mtres, "wo_b"], writes=[psres])
                    P.op("dve", lambda e, ht=ht, xt=xt, ps=ps, n=n: e.tensor_tensor(
                        out=ht[:, n * 512:(n + 1) * 512], in0=xt[:, n * 512:(n + 1) * 512], in1=ps[:], op=ALU.add),
                        reads=[xres, psres], writes=[htres])
                P.op("sp", lambda e, ht=ht, i=i: e.dma_start(out=hbuf[i * 128:(i + 1) * 128, :], in_=ht[:]),
                     reads=[htres], writes=[("hbuf", i)], dma=("p4h", htres[1]))
            P.barrier()
            P.emit()
            P.new_phase()
        if upto <= 3:
            return nc

        with ExitStack() as st:
            NS = 128
            wq_b = sb(st, "p5wq", [128, NCH, 2048], BF16)
            sk_b = sb(st, "p5sk", [128, 16, 128], BF16)
            gfp = sb(st, "p5gfp", [128, NCH], F32)
            gfin = sb(st, "p5gn", [128, D], F32)
            io_i = sb(st, "p5ioi", [128, 16], I32)
            io_f = sb(st, "p5iof", [128, 16], F32)

            for (t_, src, key) in ((gfp, g_ffn_pc, "gfp"), (gfin, g_fin, "gfin")):
                P.op("sp", lambda e, t_=t_, src=src: e.dma_start(out=t_[:], in_=src.ap()), writes=[key], dma="c0")
            P.op("pool", lambda e: e.iota(io_i[:], [[1, 16]], base=0, channel_multiplier=0), writes=["io_i"])
            P.op("dve", lambda e: e.tensor_copy(out=io_f[:], in_=io_i[:]), reads=["io_i"], writes=["io_f"])
            st2 = ExitStack()
            wst = Ring(nc, st2, "p5ws", 2, [128, D], F32)
            for c in range(NCH):
                ws, wsres = wst.next()
                P.op("sp" if c % 2 == 0 else "act", lambda e, ws=ws, c=c: e.dma_start(
                    out=ws[:], in_=w_pq[c * 128:(c + 1) * 128, :]), writes=[wsres], dma=("p5w", wsres[1]))
                if c % 2 == 0:
                    P.op("act", lambda e, ws=ws, c=c: e.activation(out=wq_b[:, c, :], in_=ws[:], func=AF.Copy,
                                                                  scale=gfp[:, c:c + 1]),
                         reads=[wsres, "gfp"], writes=["wq_b"])
                else:
                    P.op("dve", lambda e, ws=ws, c=c: e.tensor_scalar(out=wq_b[:, c, :], in0=ws[:], scalar1=gfp[:, c:c + 1],
                                                                     scalar2=None, op0=ALU.mult),
                         reads=[wsres, "gfp"], writes=["wq_b"])
            ws, wsres = wst.next()
            P.op("sp", lambda e, ws=ws: e.dma_start(out=ws[:], in_=skT_in.ap()), writes=[wsres], dma=("p5w", wsres[1]))
            P.op("dve", lambda e, ws=ws: e.tensor_copy(out=sk_b[:], in_=ws[:].rearrange("p (g n) -> p g n", g=16)),
                 reads=[wsres], writes=["sk_b"])
            P.barrier()
            P.emit()
            P.new_phase()
            st2.close()
            hts = [sb(st, "p5ht0", [128, D], F32)] * 2
            hnbs = [sb(st, "p5hnb%d" % k_, [128, D], BF16) for k_ in range(2)]
            hnT = sb(st, "p5hnT", [128, NCH, 128], BF16)
            qTs = sb(st, "p5qT", [128, 4, 128], BF16)
            scr_ = Ring(nc, st, "p5sc", 2, [128, 512], F32)
            scx = sb(st, "p5scx", [128, 128], F32)
            v16 = sb(st, "p5v16", [128, 16, 16], F32)
            i16u = sb(st, "p5i16u", [128, 16, 16], U32)
            i16f = sb(st, "p5i16f", [128, 16, 16], F32)
            work = sb(st, "p5work", [128, 2048], F32)
            cand = work[:].rearrange("p (h c) -> p h c", h=8)
            cscr = sb(st, "p5cscr", [128, 256], F32)
            tv = sb(st, "p5tv", [128, 8, 16], F32)
            tcu = sb(st, "p5tcu", [128, 8, 16], U32)
            abu = sb(st, "p5abu", [128, 2, 128], U32)
            abf = sb(st, "p5abf", [128, 2, 128], F32)
            eq = work[:].rearrange("p (h k a) -> p h k a", h=8, k=16)
            i12 = sb(st, "p5i12", [128, 2, 128], F32)
            eidf = sb(st, "p5eidf", [128, NS], F32)
            eidxs = [sb(st, "p5eidx%d" % k_, [128, NS], I32) for k_ in range(2)]
            gtss = [sb(st, "p5gts%d" % k_, [128, 8, 16], F32) for k_ in range(2)]
            gsum = sb(st, "p5gsum", [128, 8], F32)
            hdn = sb(st, "p5hdn", [128, NS], F32)
            zc = sb(st, "p5zc", [128, NS], F32)
            ur = Ring(nc, st, "p5u", 8, [128, 2 * D], BF16)
            junk = sb(st, "p5junk", [128, D], BF16)
            dgr = Ring(nc, st, "p5dg", 4, [128, 128], BF16)
            ot_ = work
            ss = sb(st, "p5ss", [128, 4], F32)
            tpr = Ring(nc, st, "p5tp", 1, [128, 1024], BF16, psum=True)
            mmr = Ring(nc, st, "p5mm", 3, [128, 512], F32, psum=True)
            ybk = [st.enter_context(nc.psum_tensor("p5y%d" % n, [128, 512], F32)) for n in range(4)]

            def rms_rstd(src_ap, col, key):
                jt = junk
                P.op("act", lambda e: e.activation(out=jt[:], in_=src_ap, func=AF.Square, accum_out=ss[:, col:col + 1]),
                     reads=[key], writes=[("p5ss", col)])
                P.op("dve", lambda e: e.tensor_scalar(out=ss[:, col:col + 1], in0=ss[:, col:col + 1], scalar1=1.0 / D,
                                                      scalar2=RMS_EPS, op0=ALU.mult, op1=ALU.add),
                     reads=[("p5ss", col)], writes=[("p5ss", col)])
                P.op("act", lambda e: e.activation(out=ss[:, col:col + 1], in_=ss[:, col:col + 1], func=AF.Sqrt),
                     reads=[("p5ss", col)], writes=[("p5ss", col)])
                P.op("dve", lambda e: e.reciprocal(out=ss[:, col + 2:col + 3], in_=ss[:, col:col + 1]),
                     reads=[("p5ss", col)], writes=[("p5ss", col + 2)])
                return ss[:, col + 2:col + 3], ("p5ss", col + 2)

            def top16(src_ap, scratch_ap, vout, iout, rkeys, skey, wkeys):
                P.op("dve", lambda e: e.max(out=vout[0], in_=src_ap), reads=rkeys, writes=[wkeys[0]])
                P.op("dve", lambda e: e.max_index(out=iout[0], in_max=vout[0], in_values=src_ap),
                     reads=rkeys + [wkeys[0]], writes=[wkeys[1]])
                P.op("dve", lambda e: e.match_replace(out=scratch_ap, in_to_replace=vout[0], in_values=src_ap,
                                                      imm_value=-1e30), reads=rkeys + [wkeys[0]], writes=[skey])
                P.op("dve", lambda e: e.max(out=vout[1], in_=scratch_ap), reads=[skey], writes=[wkeys[0]])
                P.op("dve", lambda e: e.max_index(out=iout[1], in_max=vout[1], in_values=scratch_ap),
                     reads=[skey, wkeys[0]], writes=[wkeys[1]])

            def tile_4b(i):
                ht, eidx, hnb, gts = hts[i % 2], eidxs[i % 2], hnbs[i % 2], gtss[i % 2]
                P.op("sp", lambda e, i=i: e.dma_start(out=ht[:], in_=hbuf[i * 128:(i + 1) * 128, :]),
                     reads=[("hbuf", i)], writes=["ht"], dma="p5h")
                rstd_ap, rkey = rms_rstd(ht[:], 0, "ht")
                P.op("dve", lambda e, rstd_ap=rstd_ap: e.tensor_scalar(
                    out=hnb[:], in0=ht[:], scalar1=rstd_ap, scalar2=None, op0=ALU.mult),
                    reads=["ht", rkey], writes=[("hnb", i % 2)])
                for half in range(NCH // 8):
                    tp, tpres = tpr.next()
                    for j in range(8):
                        c = half * 8 + j
                        P.op("pe", lambda e, tp=tp, j=j, c=c: e.transpose(
                            out=tp[:, j * 128:(j + 1) * 128], in_=hnb[:, c * 128:(c + 1) * 128], identity=ident_b[:]),
                            reads=[("hnb", i % 2), "ident_b"], writes=[tpres])
                    P.op("act", lambda e, tp=tp, half=half: e.activation(
                        out=hnT[:, half * 8:(half + 1) * 8, :], in_=tp[:].rearrange("p (c t) -> p c t", c=8), func=AF.Copy),
                        reads=[tpres], writes=["hnT"])
                for qq in range(4):
                    mm, mmres = mmr.next()
                    for gi in range(4):
                        G = 4 * qq + gi
                        for c in range(NCH):
                            P.op("pe", lambda e, mm=mm, gi=gi, G=G, c=c: e.matmul(
                                mm[:, gi * 128:(gi + 1) * 128], lhsT=wq_b[:, c, G * 128:(G + 1) * 128], rhs=hnT[:, c, :],
                                start=(c == 0), stop=(c == NCH - 1)), reads=["wq_b", "hnT"], writes=[mmres])
                    P.op("act", lambda e, mm=mm: e.activation(out=qTs[:], in_=mm[:].rearrange("p (g t) -> p g t", g=4),
                                                              func=AF.Copy), reads=[mmres], writes=["qTs"])
                    mm2, mm2res = mmr.next()
                    for gi in range(4):
                        G = 4 * qq + gi
                        P.op("pe", lambda e, mm2=mm2, gi=gi, G=G: e.matmul(
                            mm2[:, gi * 128:(gi + 1) * 128], lhsT=qTs[:, gi, :], rhs=sk_b[:, G, :], start=True, stop=True),
                            reads=["qTs", "sk_b"], writes=[mm2res])
                    sc, scres = scr_.next()
                    P.op("act", lambda e, sc=sc, mm2=mm2: e.activation(out=sc[:], in_=mm2[:], func=AF.Copy),
                         reads=[mm2res], writes=[scres])
                    for gi in range(4):
                        G = 4 * qq + gi
                        top16(sc[:, gi * 128:(gi + 1) * 128], scx[:], (v16[:, G, 0:8], v16[:, G, 8:16]),
                              (i16u[:, G, 0:8], i16u[:, G, 8:16]), [scres], "scx", ["v16", "i16u"])
                P.op("dve", lambda e: e.tensor_copy(out=i16f[:], in_=i16u[:]), reads=["i16u"], writes=["i16f"])
                P.op("dve", lambda e: e.tensor_tensor(
                    out=cand.rearrange("p h (a b) -> p h a b", a=16),
                    in0=AP(v16, 0, [[256, 128], [32, 8], [1, 16], [0, 16]]),
                    in1=AP(v16, 16, [[256, 128], [32, 8], [0, 16], [1, 16]]), op=ALU.add),
                    reads=["v16"], writes=["work"])
                for h in range(8):
                    top16(cand[:, h, :], cscr[:], (tv[:, h, 0:8], tv[:, h, 8:16]), (tcu[:, h, 0:8], tcu[:, h, 8:16]),
                          ["work"], "cscr", ["tv", "tcu"])
                P.op("dve", lambda e: e.tensor_tensor(out=gts[:], in0=tv[:], in1=AP(tv, 0, [[128, 128], [16, 8], [0, 16]]),
                                                      op=ALU.subtract), reads=["tv"], writes=[("gts", i % 2)])
                P.op("act", lambda e: e.activation(out=gts[:], in_=gts[:], func=AF.Exp), reads=[("gts", i % 2)], writes=[("gts", i % 2)])
                P.op("dve", lambda e: e.tensor_reduce(out=gsum[:], in_=gts[:], axis=AX.X, op=ALU.add),
                     reads=[("gts", i % 2)], writes=["gsum"])
                P.op("dve", lambda e: e.reciprocal(out=gsum[:], in_=gsum[:]), reads=["gsum"], writes=["gsum"])
                P.op("dve", lambda e: e.tensor_tensor(out=gts[:], in0=gts[:], in1=AP(gsum, 0, [[8, 128], [1, 8], [0, 16]]),
                                                      op=ALU.mult), reads=[("gts", i % 2), "gsum"], writes=[("gts", i % 2)])
                tcu2 = tcu[:].rearrange("p h k -> p (h k)")
                P.op("dve", lambda e: e.tensor_scalar(out=abu[:, 0, :], in0=tcu2, scalar1=4, scalar2=None,
                                                      op0=ALU.logical_shift_right), reads=["tcu"], writes=["abu"])
                P.op("dve", lambda e: e.tensor_scalar(out=abu[:, 1, :], in0=tcu2, scalar1=15, scalar2=None,
                                                      op0=ALU.bitwise_and), reads=["tcu"], writes=["abu"])
                P.op("dve", lambda e: e.tensor_copy(out=abf[:], in_=abu[:]), reads=["abu"], writes=["abf"])
                for w in range(2):
                    P.op("dve", lambda e, w=w: e.tensor_tensor(
                        out=eq, in0=AP(abf, w * 128, [[256, 128], [16, 8], [1, 16], [0, 16]]),
                        in1=AP(io_f, 0, [[16, 128], [0, 8], [0, 16], [1, 16]]), op=ALU.is_equal),
                        reads=["abf", "io_f"], writes=["work"])
                    P.op("dve", lambda e, w=w: e.tensor_tensor(
                        out=eq, in0=eq, in1=AP(i16f, w * 16, [[256, 128], [32, 8], [0, 16], [1, 16]]), op=ALU.mult),
                        reads=["work", "i16f"], writes=["work"])
                    P.op("dve", lambda e, w=w: e.tensor_reduce(
                        out=i12[:, w, :], in_=eq.rearrange("p h k a -> p (h k) a"), axis=AX.X, op=ALU.add),
                        reads=["work"], writes=["i12"])
                P.op("dve", lambda e: e.scalar_tensor_tensor(out=eidf[:], in0=i12[:, 0, :], scalar=128.0, in1=i12[:, 1, :],
                                                             op0=ALU.mult, op1=ALU.add), reads=["i12"], writes=["eidf"])
                P.op("dve", lambda e: e.tensor_copy(out=eidx[:], in_=eidf[:]), reads=["eidf"], writes=[("eidx", i % 2)])
            slot_st = {}

            def slot_a(i, s_):
                eidx, hnb = eidxs[i % 2], hnbs[i % 2]
                u, ures = ur.next()
                slot_st[(i, s_)] = (u, ures)
                P.op("pool", lambda e, u=u: e.indirect_dma_start(
                    out=u[:], out_offset=None, in_=ctab.ap(),
                    in_offset=bass.IndirectOffsetOnAxis(ap=eidx[:, s_:s_ + 1], axis=0)),
                    reads=[("eidx", i % 2)], writes=[ures], dma=("p5g", ures[1], i % 2))
                P.op("dve", lambda e, u=u: e.scalar_tensor_tensor(
                    out=junk[:], in0=u[:, 0:D], scalar=1.0, in1=hnb[:], op0=ALU.mult, op1=ALU.mult,
                    accum_out=hdn[:, s_:s_ + 1]), reads=[ures, ("hnb", i % 2)], writes=[("hdn", s_)])
                P.op("act", lambda e: e.activation(out=zc[:, s_:s_ + 1], in_=hdn[:, s_:s_ + 1], func=AF.Gelu_apprx_tanh),
                     reads=[("hdn", s_)], writes=[("zc", s_)])

            def slot_b(i, s_):
                gts = gtss[i % 2]
                u, ures = slot_st.pop((i, s_))
                dg, dgres = dgr.next()
                P.op("dve", lambda e, dg=dg: e.tensor_scalar(
                    out=dg[:], in0=ident_f[:], scalar1=zc[:, s_:s_ + 1],
                    scalar2=gts[:].rearrange("p h k -> p (h k)")[:, s_:s_ + 1], op0=ALU.mult, op1=ALU.mult),
                    reads=["ident_f", ("zc", s_), ("gts", i % 2)], writes=[dgres])
                for n in range(4):
                    P.op("pe", lambda e, dg=dg, u=u, n=n: e.matmul(
                        ybk[n][:], lhsT=dg[:], rhs=u[:, D + n * 512:D + (n + 1) * 512], start=(s_ == 0),
                        stop=(s_ == NS - 1)), reads=[dgres, ures], writes=[("y", n)])

            def final(i):
                P.op("sp", lambda e, i=i: e.dma_start(out=ot_[:], in_=hbuf[i * 128:(i + 1) * 128, :]),
                     writes=["work"], dma="p5h2")
                for n in range(4):
                    P.op("dve", lambda e, n=n: e.tensor_tensor(out=ot_[:, n * 512:(n + 1) * 512],
                                                                in0=ot_[:, n * 512:(n + 1) * 512], in1=ybk[n][:], op=ALU.add),
                         reads=["work", ("y", n)], writes=["work"])
                rstd_ap, rkey = rms_rstd(ot_[:], 1, "work")
                P.op("dve", lambda e, rstd_ap=rstd_ap: e.scalar_tensor_tensor(
                    out=ot_[:], in0=ot_[:], scalar=rstd_ap, in1=gfin[:], op0=ALU.mult, op1=ALU.mult),
                    reads=["work", rkey, "gfin"], writes=["work"])
                P.op("sp", lambda e, i=i: e.dma_start(out=out[i * 128:(i + 1) * 128, :], in_=ot_[:]),
                     reads=["work"], writes=[("out", i)], dma="p5o")

            def cap4b(i):
                if i >= NOB:
                    return []
                P.capture()
                tile_4b(i)
                return P.end_capture()

            tile_4b(0)
            for i in range(NOB):
                inter = cap4b(i + 1)
                per = (len(inter) + NS - 1) // NS
                LAG = 2
                for s_ in range(NS + LAG):
                    if s_ < NS:
                        slot_a(i, s_)
                    if s_ >= LAG:
                        slot_b(i, s_ - LAG)
                    if s_ < NS:
                        P.replay(inter[s_ * per:(s_ + 1) * per])
                final(i)
            P.barrier()
            P.emit()
            P.finish()
    return nc


def _const_tables(cfg, p):
    NCB, NPB = cfg.NCB, cfg.NPB
    TP = NPB * 128
    idx = np.arange(cfg.TC)
    pos = idx - (0 if p == 1 else TP)
    valid = (pos >= 0).astype(np.float32)
    kvalid = np.ascontiguousarray(valid.reshape(NCB, 128).T)
    inv = (500000.0 ** (-np.arange(0, 16, 2, dtype=np.float32) / 16.0)).astype(np.float32)
    ang = np.maximum(pos, 0).astype(np.float32)[:, None] * inv[None, :]
    cos = np.cos(ang).astype(np.float32).reshape(NCB, 128, 8).transpose(1, 0, 2).reshape(128, NCB * 8)
    sin = np.sin(ang).astype(np.float32).reshape(NCB, 128, 8).transpose(1, 0, 2).reshape(128, NCB * 8)
    kl = np.arange(128)[:, None, None]
    ko = np.arange(20)[None, :, None]
    ql = np.arange(512)[None, None, :]
    delta = (16 - ko) * 128 + ql - kl
    w = ((delta >= 0) & (delta <= 128)).astype(np.float32)
    w += ((delta >= 0) & (delta <= 512) & (delta % 4 == 0)).astype(np.float32)
    w += ((delta >= 0) & (delta <= 2048) & (delta % 16 == 0)).astype(np.float32)
    wcat = np.ascontiguousarray(w.reshape(128, 20 * 512))
    tri = (np.arange(128)[:, None] <= np.arange(128)[None, :]).astype(np.float32)
    return dict(kvalid=kvalid, rope_cos=np.ascontiguousarray(cos), rope_sin=np.ascontiguousarray(sin),
                wcat=wcat, tri=tri, ident=np.eye(128, dtype=np.float32))


def _pc(v, nch):
    return np.ascontiguousarray(np.asarray(v, np.float32).reshape(nch, 128).T)


def shared_inputs(cfg, attn_norm_gain, w_in, forget_bias, fox_out_gain, dil_out_gain, w_out,
                  ffn_norm_gain, peer_query, peer_sub_keys, peer_down, peer_up, final_norm_gain):
    f = lambda a: np.ascontiguousarray(np.asarray(a, np.float32))
    NCH = cfg.NCH
    skT = np.asarray(peer_sub_keys[0], np.float32).reshape(16, 128, 128).transpose(2, 0, 1)
    return dict(
        w_in=f(w_in[0]), g_attn=_pc(attn_norm_gain[0], NCH), fbias=f(forget_bias[0]).reshape(16, 1),
        g_mix=_pc(np.concatenate([np.asarray(fox_out_gain[0]), np.asarray(dil_out_gain[0])]), NCH),
        w_out=f(w_out[0]), g_ffn_pc=_pc(ffn_norm_gain[0], NCH), g_ffn=np.ascontiguousarray(np.broadcast_to(f(ffn_norm_gain[0])[None, :], (128, cfg.D))),
        w_pq=f(peer_query[0]), skT=np.ascontiguousarray(skT.reshape(128, 16 * 128)),
        p_down=f(peer_down[0]), p_up=f(peer_up[0]),
        g_fin=np.ascontiguousarray(np.broadcast_to(f(final_norm_gain)[None, :], (128, cfg.D))))


def core_inputs(cfg, xb, p, shared, tables):
    TP, TO = cfg.NPB * 128, cfg.TO
    if p == 1:
        x_ctx = np.ascontiguousarray(xb[0:TP + TO])
    else:
        x_ctx = np.concatenate([np.zeros((TP, cfg.D), np.float32), xb[0:TO]], axis=0)
    d = dict(shared)
    d.update(tables[p])
    d["x_ctx"] = x_ctx
    return d


def kernel(**inputs):
    cfg = Cfg(32, 32)
    x = np.asarray(inputs["x"], np.float32)
    B, S, D = x.shape
    sh = shared_inputs(cfg, **{k: np.asarray(v) for k, v in inputs.items() if k != "x"})
    tables = [_const_tables(cfg, 0), _const_tables(cfg, 1)]
    in_maps = [core_inputs(cfg, x[c // 2], c % 2, sh, tables) for c in range(2 * B)]
    nc = build(cfg)
    res = run_bass_kernel_spmd(nc, in_maps, core_ids=list(range(2 * B)))
    outp = np.empty((B, S, D), np.float32)
    for c in range(2 * B):
        outp[c // 2, (c % 2) * cfg.TO:(c % 2 + 1) * cfg.TO] = np.asarray(res.results[c]["out"], np.float32)
    return outp
```

```python
import numpy as np
from contextlib import ExitStack

import concourse.bass as bass
import concourse.mybir as mybir
from concourse.bass_utils import run_bass_kernel_spmd

F32 = mybir.dt.float32
BF16 = mybir.dt.bfloat16
I32 = mybir.dt.int32
U32 = mybir.dt.uint32
AF = mybir.ActivationFunctionType
ALU = mybir.AluOpType
AX = mybir.AxisListType

HD = 64
NH = 16
RMS_EPS = 1e-6
NEXP = 16384


class Cfg:
    def __init__(self, npb=32, nob=32, D=2048):
        self.NPB = npb
        self.NOB = nob
        self.NCB = npb + nob
        self.D = D
        self.NCH = D // 128
        self.TC = self.NCB * 128
        self.TO = self.NOB * 128
        self.IN_COLS = 3 * 1024 + 16 + 3 * 1024


class Prog:
    ENGS = ("pe", "act", "dve", "pool", "sp")

    def __init__(self, nc, stack):
        self.nc = nc
        self.stack = stack
        self.all_sems = []
        self.dpool = []
        self.nphase = 0
        self.streams = {e: [] for e in self.ENGS}
        self.new_phase()

    def _alloc(self, name):
        h = self.stack.enter_context(self.nc.semaphore(name))
        self.all_sems.append(h)
        return h

    def new_phase(self):
        self.nphase += 1
        self.sem = {e: self._alloc("s%d_%s" % (self.nphase, e)) for e in self.ENGS}
        self.cnt = {e: 0 for e in self.ENGS}
        self.dsem = {}
        self.dnext = 0
        self.known = {e: {} for e in self.ENGS}
        self.last_w = {}
        self.readers = {}

    def finish(self):
        sems = list(self.all_sems)
        with self.nc.Block() as block:
            @block.gpsimd
            def _(e):
                for h in sems:
                    e.sem_clear(h)

    def _semh(self, key):
        if isinstance(key, str):
            return self.sem[key]
        return self.dsem[key[1]][0]

    def capture(self):
        self._cap = []

    def end_capture(self):
        c, self._cap = self._cap, None
        return c

    def replay(self, items):
        for it in items:
            self.op(*it)

    def op(self, eng, fn, reads=(), writes=(), dma=None):
        if getattr(self, "_cap", None) is not None:
            self._cap.append((eng, fn, tuple(reads), tuple(writes), dma))
            return
        deps = {}

        def add(ev):
            if ev is None:
                return
            k, v = ev
            if deps.get(k, 0) < v:
                deps[k] = v

        writes = list(writes)
        if dma is not None:
            writes.append(("dmakey", dma))
        for r in reads:
            add(self.last_w.get(r))
        for w in writes:
            add(self.last_w.get(w))
            for k, v in self.readers.get(w, {}).items():
                add((k, v))
        is_dma = dma is not None
        if is_dma:
            if dma not in self.dsem:
                if self.dnext >= len(self.dpool):
                    self.dpool.append([self._alloc("dsem%d" % len(self.dpool)), 0])
                self.dsem[dma] = self.dpool[self.dnext]
                self.dnext += 1
            self.dsem[dma][1] += 1
            ev = (("dma", dma), 16 * self.dsem[dma][1])
            inc = (self.dsem[dma][0], 16)
        else:
            self.cnt[eng] += 1
            ev = (eng, self.cnt[eng])
            inc = (self.sem[eng], 1)
        waits = []
        kn = self.known[eng]
        for k, v in deps.items():
            if k == eng and eng == "pe" and not is_dma:
                continue
            if kn.get(k, 0) >= v:
                continue
            kn[k] = v
            waits.append((self._semh(k), v))
        for w in writes:
            self.last_w[w] = ev
            self.readers[w] = {}
        for r in reads:
            d = self.readers.setdefault(r, {})
            if d.get(ev[0], 0) < ev[1]:
                d[ev[0]] = ev[1]
        self.streams[eng].append((waits, fn, inc))

    def barrier(self):
        allev = [(e, self.cnt[e]) for e in self.ENGS if self.cnt[e] > 0]
        allev += [(("dma", k), 16 * v[1]) for k, v in self.dsem.items()]
        for eng in self.ENGS:
            waits = []
            kn = self.known[eng]
            for k, v in allev:
                if kn.get(k, 0) >= v:
                    continue
                kn[k] = v
                waits.append((self._semh(k), v))
            if waits:
                self.streams[eng].append((waits, None, None))

    def emit(self):
        nc = self.nc
        streams = self.streams
        with nc.Block() as block:
            def run(name, e):
                for waits, fn, inc in streams[name]:
                    for s, v in waits:
                        e.wait_ge(s, v)
                    if fn is not None:
                        fn(e).then_inc(inc[0], inc[1])

            @block.tensor
            def _(e):
                run("pe", e)

            @block.scalar
            def _(e):
                run("act", e)

            @block.vector
            def _(e):
                run("dve", e)

            @block.gpsimd
            def _(e):
                run("pool", e)

            @block.sync
            def _(e):
                run("sp", e)
        self.streams = {e: [] for e in self.ENGS}


def AP(t, off, pat):
    return bass.AP(t, off, [list(x) for x in pat])


class Ring:
    def __init__(self, nc, stack, name, n, shape, dtype, psum=False):
        mk = nc.psum_tensor if psum else nc.sbuf_tensor
        self.t = [stack.enter_context(mk("%s%d" % (name, i), shape, dtype)) for i in range(n)]
        self.name = name
        self.n = n
        self.i = -1

    def next(self):
        self.i += 1
        s = self.i % self.n
        return self.t[s], (self.name, s)


def build(cfg, upto=99, dbg_out=()):
    nc = bass.Bass("TRN2", target_bir_lowering=False)
    D, NCH, NCB, NOB, NPB, TC, TO = cfg.D, cfg.NCH, cfg.NCB, cfg.NOB, cfg.NPB, cfg.TC, cfg.TO
    NTG = NCB // 4
    OTG0 = NPB // 4

    def din(name, shape, dt=F32):
        return nc.dram_tensor(name, list(shape), dt, kind="ExternalInput")

    def dscr(name, shape, dt):
        return nc.dram_tensor(name, list(shape), dt, kind="ExternalOutput" if name in dbg_out else "Internal")

    x_ctx = din("x_ctx", [TC, D])
    kvalid = din("kvalid", [128, NCB])
    rope_cos = din("rope_cos", [128, NCB * 8])
    rope_sin = din("rope_sin", [128, NCB * 8])
    wcat_in = din("wcat", [128, 20 * 512])
    tri_in = din("tri", [128, 128])
    ident_in = din("ident", [128, 128])
    w_in = din("w_in", [D, cfg.IN_COLS])
    g_attn = din("g_attn", [128, NCH])
    fbias = din("fbias", [16, 1])
    g_mix = din("g_mix", [128, NCH])
    w_out = din("w_out", [D, D])
    g_ffn = din("g_ffn", [128, D])
    g_ffn_pc = din("g_ffn_pc", [128, NCH])
    w_pq = din("w_pq", [D, 2048])
    skT_in = din("skT", [128, 16 * 128])
    p_down = din("p_down", [NEXP, D])
    p_up = din("p_up", [NEXP, D])
    g_fin = din("g_fin", [128, D])
    out = nc.dram_tensor("out", [TO, D], F32, kind="ExternalOutput")

    xnT = dscr("xnT", [NCH, 128, TC], BF16)
    qfT = dscr("qfT", [NH, 65, TO], BF16)
    kfT = dscr("kfT", [NH, 65, TC], BF16)
    vf = dscr("vf", [128, NCB, NH, 65], BF16)
    qdT = dscr("qdT", [NH * 64, TO], BF16)
    kdT = dscr("kdT", [NH * 64, TC], BF16)
    vd = dscr("vd", [128, NCB, NH, 65], BF16)
    attn_o = dscr("attn_o", [TO, D], F32)
    hbuf = dscr("hbuf", [TO, D], F32)
    ctab = dscr("ctab", [NEXP, 2 * D], BF16)
    dbg = {}

    with ExitStack() as gstack:
        P = Prog(nc, gstack)
        sb = lambda st, name, shape, dt: st.enter_context(nc.sbuf_tensor(name, list(shape), dt))

        ident_f = sb(gstack, "ident_f", [128, 128], F32)
        ident_b = sb(gstack, "ident_b", [128, 128], BF16)
        negc_tm = sb(gstack, "negc_tm", [128, NCB, 16], F32)
        P.op("sp", lambda e: e.dma_start(out=ident_f[:], in_=ident_in.ap()), writes=["ident_f"], dma="c0")
        P.op("dve", lambda e: e.tensor_copy(out=ident_b[:], in_=ident_f[:]), reads=["ident_f"], writes=["ident_b"])

        with ExitStack() as st:
            xr = Ring(nc, st, "p0x", 2, [128, D], F32)
            jr = Ring(nc, st, "p0j", 1, [128, D], BF16)
            xbr = Ring(nc, st, "p0xb", 2, [128, D], BF16)
            tpr = Ring(nc, st, "p0tp", 4, [128, 1024], BF16, psum=True)
            xtr = Ring(nc, st, "p0xt", 2, [128, NCH, 512], BF16)
            ss = sb(st, "p0ss", [128, NCB], F32)
            rs = sb(st, "p0rs", [128, NCB], F32)
            rstd = sb(st, "p0rstd", [128, NCB], F32)
            for i in range(NCB):
                xt, xres = xr.next()
                P.op("sp", lambda e, xt=xt, i=i: e.dma_start(out=xt[:], in_=x_ctx[i * 128:(i + 1) * 128, :]),
                     writes=[xres], dma=("p0x", xres[1]))
                jt, jres = jr.next()
                P.op("act", lambda e, xt=xt, jt=jt, i=i: e.activation(out=jt[:], in_=xt[:], func=AF.Square,
                                                                    accum_out=ss[:, i:i + 1]),
                     reads=[xres], writes=[jres, ("ss", i)])
                P.op("dve", lambda e, i=i: e.tensor_scalar(out=rs[:, i:i + 1], in0=ss[:, i:i + 1], scalar1=1.0 / D,
                                                           scalar2=RMS_EPS, op0=ALU.mult, op1=ALU.add),
                     reads=[("ss", i)], writes=[("rs", i)])
                P.op("act", lambda e, i=i: e.activation(out=rs[:, i:i + 1], in_=rs[:, i:i + 1], func=AF.Sqrt),
                     reads=[("rs", i)], writes=[("rs", i)])
                P.op("dve", lambda e, i=i: e.reciprocal(out=rstd[:, i:i + 1], in_=rs[:, i:i + 1]),
                     reads=[("rs", i)], writes=[("rstd", i)])
                xb, xbres = xbr.next()
                P.op("dve", lambda e, xt=xt, xb=xb, i=i: e.tensor_scalar(out=xb[:], in0=xt[:], scalar1=rstd[:, i:i + 1],
                                                                        scalar2=None, op0=ALU.mult),
                     reads=[xres, ("rstd", i)], writes=[xbres])
                if i % 4 == 0:
                    xtt, xtres = xtr.next()
                for half in range(NCH // 8):
                    tp, tpres = tpr.next()
                    for j in range(8):
                        c = half * 8 + j
                        P.op("pe", lambda e, tp=tp, xb=xb, j=j, c=c: e.transpose(
                            out=tp[:, j * 128:(j + 1) * 128], in_=xb[:, c * 128:(c + 1) * 128], identity=ident_b[:]),
                            reads=[xbres, "ident_b"], writes=[tpres])
                    eng = "act" if half % 2 == 0 else "dve"
                    o_ap = xtt[:, half * 8:(half + 1) * 8, (i % 4) * 128:(i % 4 + 1) * 128]
                    i_ap = tp[:].rearrange("p (c t) -> p c t", c=8)
                    if eng == "act":
                        P.op("act", lambda e, o_ap=o_ap, i_ap=i_ap: e.activation(out=o_ap, in_=i_ap, func=AF.Copy),
                             reads=[tpres], writes=[xtres])
                    else:
                        P.op("dve", lambda e, o_ap=o_ap, i_ap=i_ap: e.tensor_copy(out=o_ap, in_=i_ap),
                             reads=[tpres], writes=[xtres])
                if i % 4 == 3:
                    g = i // 4
                    dst = AP(xnT, g * 512, [[TC, 128], [128 * TC, NCH], [1, 512]])
                    P.op("sp", lambda e, dst=dst, xtt=xtt: e.dma_start(out=dst, in_=xtt[:]),
                         reads=[xtres], writes=[("xnT", g)], dma=("p0st", xtres[1]))
            P.barrier()
            P.emit()
            P.new_phase()

        if upto <= 0:
            return nc

        with ExitStack() as st:
            wst = Ring(nc, st, "p1ws", 3, [128, 1024], F32)
            wgr = Ring(nc, st, "p1wg", 2, [128, NCH, 1024], BF16)
            xgr = Ring(nc, st, "p1xg", 2, [128, NCH, 512], BF16)
            psr = Ring(nc, st, "p1ps", 4, [128, 512], F32, psum=True)
            tpr = Ring(nc, st, "p1tp", 2, [128, 1024], BF16, psum=True)
            obr = Ring(nc, st, "p1ob", 3, [128, 512], BF16)
            vtr = Ring(nc, st, "p1vt", 2, [128, NH, 65], BF16)
            tmr = Ring(nc, st, "p1tm", 2, [128, 1024], F32)
            rbr = Ring(nc, st, "p1rb", 2, [128, 1024], BF16)
            dtr = Ring(nc, st, "p1dt", 2, [128, 8, 512], BF16)
            gat = sb(st, "p1gat", [128, NCH], F32)
            kv_sb = sb(st, "p1kv", [128, NCB], F32)
            cos_sb = sb(st, "p1cos", [128, NCB * 8], F32)
            sin_sb = sb(st, "p1sin", [128, NCB * 8], F32)
            ra = sb(st, "p1ra", [128, NH, 8], F32)
            rb_ = sb(st, "p1rb_", [128, NH, 8], F32)
            fb_sb = sb(st, "p1fb", [16, 1], F32)
            nfb = sb(st, "p1nfb", [16, 1], F32)
            ncr = Ring(nc, st, "p1ncg", 2, [16, 512], F32)
            nc_prev = [None]
            ones16 = sb(st, "p1ones", [16, 512], F32)
            onesb = sb(st, "p1onesb", [16, 512], BF16)
            spt = sb(st, "p1spt", [16, 512], F32)
            rqr = Ring(nc, st, "p1rq", 2, [16, 512], BF16)
            for (t_, src, key) in ((gat, g_attn, "gat"), (kv_sb, kvalid, "kv"), (cos_sb, rope_cos, "cos"),
                                   (sin_sb, rope_sin, "sin"), (fb_sb, fbias, "fb")):
                P.op("sp", lambda e, t_=t_, src=src: e.dma_start(out=t_[:], in_=src.ap()), writes=[key], dma="c0")
            P.op("dve", lambda e: e.tensor_scalar(out=nfb[:], in0=fb_sb[:], scalar1=-1.0, scalar2=None, op0=ALU.mult),
                 reads=["fb"], writes=["nfb"])
            P.op("pool", lambda e: e.memset(ones16[:], 1.0), writes=["ones16"])
            P.op("pool", lambda e: e.memset(onesb[:], 1.0), writes=["onesb"])

            groups = [("fq", 0, 1024, "fm", OTG0), ("fk", 1024, 1024, "fm", 0), ("fv", 2048, 1024, "tmv", 0),
                      ("fg", 3072, 16, "fg", 0), ("dq", 3088, 1024, "tmr", OTG0), ("dk", 4112, 1024, "tmr", 0),
                      ("dv", 5136, 1024, "tmv", 0)]
            wgs = {}

            def load_group(k):
                (gname, col0, ncols, mode, tg0) = groups[k]
                wg, wgres = wgr.next()
                wgs[k] = (wg, wgres)
                for c in range(NCH):
                    ws, wsres = wst.next()
                    P.op("sp" if c % 2 == 0 else "act",
                         lambda e, ws=ws, c=c, col0=col0, ncols=ncols: e.dma_start(
                             out=ws[:, 0:ncols], in_=w_in[c * 128:(c + 1) * 128, col0:col0 + ncols]),
                         writes=[wsres], dma=("p1w", wsres[1]))
                    if c % 2 == 0:
                        P.op("act", lambda e, ws=ws, wg=wg, c=c, ncols=ncols: e.activation(
                            out=wg[:, c, 0:ncols], in_=ws[:, 0:ncols], func=AF.Copy, scale=gat[:, c:c + 1]),
                            reads=[wsres, "gat"], writes=[wgres])
                    else:
                        P.op("dve", lambda e, ws=ws, wg=wg, c=c, ncols=ncols: e.tensor_scalar(
                            out=wg[:, c, 0:ncols], in0=ws[:, 0:ncols], scalar1=gat[:, c:c + 1], scalar2=None,
                            op0=ALU.mult),
                            reads=[wsres, "gat"], writes=[wgres])

            load_group(0)
            for gk, (gname, col0, ncols, mode, tg0) in enumerate(groups):
                wg, wgres = wgs[gk]
                for tg in range(tg0, NTG):
                    if tg == min(tg0 + 1, NTG - 1) and gk + 1 < len(groups):
                        load_group(gk + 1)
                    xg, xgres = xgr.next()
                    src = AP(xnT, tg * 512, [[TC, 128], [128 * TC, NCH], [1, 512]])
                    P.op("sp", lambda e, xg=xg, src=src: e.dma_start(out=xg[:], in_=src),
                         reads=[("xnT", tg)], writes=[xgres], dma=("p1x", xgres[1]))
                    otg = tg - OTG0
                    if gname == "fk":
                        P.op("act", lambda e, tg=tg: e.dma_start(
                            out=AP(kfT, 64 * TC + tg * 512, [[65 * TC, 16], [1, 512]]), in_=onesb[:]),
                            reads=["onesb"], writes=[("kaug", tg)], dma="kaug")
                    if mode == "fm":
                        dstT, TT, toff, scl = (qfT, TO, otg * 512, 0.125) if gname == "fq" else (kfT, TC, tg * 512, 1.0)
                        for sub in range(8):
                            ps, psres = psr.next()
                            for c in range(NCH):
                                P.op("pe", lambda e, ps=ps, wg=wg, xg=xg, c=c, sub=sub: e.matmul(
                                    ps[:], lhsT=wg[:, c, sub * 128:(sub + 1) * 128], rhs=xg[:, c, :],
                                    start=(c == 0), stop=(c == NCH - 1)),
                                    reads=[wgres, xgres], writes=[psres])
                            ob, obres = obr.next()
                            P.op("act", lambda e, ob=ob, ps=ps, scl=scl: e.activation(out=ob[:], in_=ps[:], func=AF.Copy,
                                                                                      scale=scl),
                                 reads=[psres], writes=[obres])
                            for hh in range(2):
                                h = 2 * sub + hh
                                dst = AP(dstT, h * 65 * TT + toff, [[TT, 64], [1, 512]])
                                P.op("sp" if hh == 0 else "act", lambda e, dst=dst, ob=ob, hh=hh: e.dma_start(
                                    out=dst, in_=ob[hh * 64:(hh + 1) * 64, :]),
                                    reads=[obres], writes=[(gname, h, tg)], dma=("p1o", obres[1], hh))
                    elif mode == "fg":
                        ps, psres = psr.next()
                        for c in range(NCH):
                            P.op("pe", lambda e, ps=ps, wg=wg, xg=xg, c=c: e.matmul(
                                ps[0:16, :], lhsT=wg[:, c, 0:16], rhs=xg[:, c, :], start=(c == 0), stop=(c == NCH - 1)),
                                reads=[wgres, xgres], writes=[psres])
                        P.op("act", lambda e, ps=ps: e.activation(out=spt[:], in_=ps[0:16, :], func=AF.Exp, scale=-1.0,
                                                                  bias=nfb[:]),
                             reads=[psres, "nfb"], writes=["spt"])
                        P.op("act", lambda e: e.activation(out=spt[:], in_=spt[:], func=AF.Ln, bias=1.0),
                             reads=["spt"], writes=["spt"])
                        ncg, ncres = ncr.next()
                        init = 0.0 if tg == 0 else nc_prev[0][0][:, 511:512]
                        prev_res = [] if tg == 0 else [nc_prev[0][1]]
                        P.op("dve", lambda e, ncg=ncg, init=init: e.tensor_tensor_scan(
                            out=ncg[:], data0=ones16[:], data1=spt[:], initial=init, op0=ALU.mult, op1=ALU.add),
                            reads=["spt", "ones16"] + prev_res, writes=[ncres])
                        nc_prev[0] = (ncg, ncres)
                        ps2, ps2res = psr.next()
                        for bb in range(4):
                            P.op("pe", lambda e, ps2=ps2, ncg=ncg, bb=bb: e.transpose(
                                out=ps2[:, bb * 16:(bb + 1) * 16], in_=ncg[:, bb * 128:(bb + 1) * 128],
                                identity=ident_f[0:16, 0:16]),
                                reads=[ncres, "ident_f"], writes=[ps2res])
                        P.op("dve", lambda e, ps2=ps2, tg=tg: e.tensor_copy(
                            out=negc_tm[:, tg * 4:(tg + 1) * 4, :], in_=ps2[:, 0:64].rearrange("p (b h) -> p b h", h=16)),
                            reads=[ps2res], writes=["negc_tm"])
                        if tg >= OTG0:
                            rq, rqres = rqr.next()
                            P.op("dve", lambda e, rq=rq, ncg=ncg: e.tensor_scalar(
                                out=rq[:], in0=ncg[:], scalar1=-1.0, scalar2=None,
                                op0=ALU.mult), reads=[ncres], writes=[rqres])
                            P.op("act", lambda e, rq=rq, otg=otg: e.dma_start(
                                out=AP(qfT, 64 * TO + otg * 512, [[65 * TO, 16], [1, 512]]), in_=rq[:]),
                                reads=[rqres], writes=[("qaug", otg)], dma=("qaug", rqres[1]))
                    else:
                        for ti in range(4):
                            blk = tg * 4 + ti
                            if mode == "tmv":
                                vt, vtres = vtr.next()
                            else:
                                tm, tmres = tmr.next()
                            for half in range(2):
                                ps, psres = psr.next()
                                for c in range(NCH):
                                    P.op("pe", lambda e, ps=ps, wg=wg, xg=xg, c=c, ti=ti, half=half: e.matmul(
                                        ps[:], lhsT=xg[:, c, ti * 128:(ti + 1) * 128],
                                        rhs=wg[:, c, half * 512:(half + 1) * 512], start=(c == 0), stop=(c == NCH - 1)),
                                        reads=[wgres, xgres], writes=[psres])
                                if mode == "tmv":
                                    P.op("act", lambda e, vt=vt, ps=ps, half=half: e.activation(
                                        out=vt[:, half * 8:(half + 1) * 8, 0:64],
                                        in_=ps[:].rearrange("p (h d) -> p h d", h=8), func=AF.Copy),
                                        reads=[psres], writes=[vtres])
                                else:
                                    scl = 0.125 if gname == "dq" else 1.0
                                    P.op("act", lambda e, tm=tm, ps=ps, half=half, scl=scl: e.activation(
                                        out=tm[:, half * 512:(half + 1) * 512], in_=ps[:], func=AF.Copy, scale=scl),
                                        reads=[psres], writes=[tmres])
                            if mode == "tmv":
                                P.op("pool", lambda e, vt=vt, blk=blk: e.tensor_copy(
                                    out=vt[:, :, 64:65], in_=AP(kv_sb, blk, [[NCB, 128], [0, NH], [1, 1]])),
                                    reads=["kv"], writes=[vtres])
                                dstv = vf if gname == "fv" else vd
                                dst = AP(dstv, blk * NH * 65, [[NCB * NH * 65, 128], [1, NH * 65]])
                                P.op("sp", lambda e, dst=dst, vt=vt: e.dma_start(
                                    out=dst, in_=vt[:].rearrange("p h d -> p (h d)")),
                                    reads=[vtres], writes=[(gname, blk)], dma=("p1v", vtres[1]))
                            else:
                                rbt, rbres = rbr.next()
                                P.op("act", lambda e, rbt=rbt, tm=tm: e.activation(out=rbt[:], in_=tm[:], func=AF.Copy),
                                     reads=[tmres], writes=[rbres])
                                tm3 = tm[:].rearrange("p (h d) -> p h d", h=NH)
                                rb3 = rbt[:].rearrange("p (h d) -> p h d", h=NH)
                                x1, x2 = tm3[:, :, 0:8], tm3[:, :, 8:16]
                                cosb = AP(cos_sb, blk * 8, [[NCB * 8, 128], [0, NH], [1, 8]])
                                sinb = AP(sin_sb, blk * 8, [[NCB * 8, 128], [0, NH], [1, 8]])
                                for (o_, a_, ca, b_, cb, opx) in ((rb3[:, :, 0:8], x1, cosb, x2, sinb, ALU.subtract),
                                                                 (rb3[:, :, 8:16], x2, cosb, x1, sinb, ALU.add)):
                                    P.op("dve", lambda e, a_=a_, ca=ca: e.tensor_tensor(out=ra[:], in0=a_, in1=ca, op=ALU.mult),
                                         reads=[tmres, "cos"], writes=["ra"])
                                    P.op("dve", lambda e, b_=b_, cb=cb: e.tensor_tensor(out=rb_[:], in0=b_, in1=cb, op=ALU.mult),
                                         reads=[tmres, "sin"], writes=["rb_"])
                                    P.op("dve", lambda e, o_=o_, opx=opx: e.tensor_tensor(out=o_, in0=ra[:], in1=rb_[:], op=opx),
                                         reads=["ra", "rb_"], writes=[rbres])
                                if ti == 0:
                                    dt_, dtres = dtr.next()
                                tp, tpres = tpr.next()
                                for j in range(8):
                                    P.op("pe", lambda e, tp=tp, rbt=rbt, j=j: e.transpose(
                                        out=tp[:, j * 128:(j + 1) * 128], in_=rbt[:, j * 128:(j + 1) * 128],
                                        identity=ident_b[:]),
                                        reads=[rbres, "ident_b"], writes=[tpres])
                                P.op("dve", lambda e, dt_=dt_, tp=tp, ti=ti: e.tensor_copy(
                                    out=dt_[:, :, ti * 128:(ti + 1) * 128], in_=tp[:].rearrange("p (j t) -> p j t", j=8)),
                                    reads=[tpres], writes=[dtres])
                                if ti == 3:
                                    dstT, TT, toff = (qdT, TO, otg * 512) if gname == "dq" else (kdT, TC, tg * 512)
                                    dst = AP(dstT, toff, [[TT, 128], [128 * TT, 8], [1, 512]])
                                    P.op("sp", lambda e, dst=dst, dt_=dt_: e.dma_start(out=dst, in_=dt_[:]),
                                         reads=[dtres], writes=[(gname, tg)], dma=("p1d", dtres[1]))
            P.barrier()
            P.emit()
            P.new_phase()
        if upto <= 1:
            return nc

        for kind in ("fox", "dil"):
            with ExitStack() as st:
                KR = 65 if kind == "fox" else 64
                qr = Ring(nc, st, kind + "q", 2, [65, TO], BF16)
                kr = Ring(nc, st, kind + "k", 2, [65, TC], BF16)
                if kind == "dil":
                    for r_ in range(2):
                        P.op("pool", lambda e, r_=r_: e.memset(qr.t[r_][64:65, :], 0.0), writes=[(qr.name, r_)])
                        P.op("pool", lambda e, r_=r_: e.memset(kr.t[r_][64:65, :], 0.0), writes=[(kr.name, r_)])
                vr = Ring(nc, st, kind + "v", 2, [128, NCB, 4 * 65], BF16)
                sr = Ring(nc, st, kind + "s", 4, [128, 512], F32, psum=True)
                otr = Ring(nc, st, kind + "ot", 2, [128, 512], F32, psum=True)
                tor = Ring(nc, st, kind + "to", 2, [128, 4, 128], F32, psum=True)
                ptr_ = Ring(nc, st, kind + "pt", 4, [128, 512], BF16)
                osr = Ring(nc, st, kind + "os", 2, [65, 512], F32)
                rcr = Ring(nc, st, kind + "rc", 2, [128, 4], F32)
                ofr = Ring(nc, st, kind + "of", 2, [128, 4, 64], F32)
                cst = sb(st, kind + "cst", [128, 5 * 512], F32)
                if kind == "fox":
                    tri_b = sb(st, "tri_b", [128, 128], BF16)
                    P.op("sp", lambda e: e.dma_start(out=cst[:, 0:128], in_=tri_in.ap()), writes=["cst"], dma="c0")
                    P.op("dve", lambda e: e.tensor_copy(out=tri_b[:], in_=cst[:, 0:128]), reads=["cst"], writes=["tri_b"])
                else:
                    wcat_b = sb(st, "wcat_b", [128, 20, 512], BF16)
                    for pc in range(4):
                        P.op("sp", lambda e, pc=pc: e.dma_start(out=cst[:], in_=wcat_in[:, pc * 2560:(pc + 1) * 2560]),
                             writes=["cst"], dma="c0")
                        P.op("dve", lambda e, pc=pc: e.tensor_copy(
                            out=wcat_b[:, pc * 5:(pc + 1) * 5, :], in_=cst[:].rearrange("p (a b) -> p a b", a=5)),
                            reads=["cst"], writes=["wcat_b"])
                qsrc, ksrc, vsrc = (qfT, kfT, vf) if kind == "fox" else (qdT, kdT, vd)
                coff = 0 if kind == "fox" else 1024
                LA = 3
                units = []
                loads = {}
                head_starts = []
                for hg in range(4):
                    for hh in range(4):
                        h = hg * 4 + hh
                        head_starts.append((len(units), hg, hh))
                        for g in range(NOB // 4):
                            qb0 = NPB + 4 * g
                            kb_lo = 0 if kind == "fox" else max(0, qb0 - 16)
                            kbs = list(range(kb_lo, qb0 + 4))
                            for kb in kbs:
                                j = kb - qb0
                                q0 = (max(0, j) * 128) if kind == "fox" else 0
                                units.append((h, hh, g, kb, q0, kb == kbs[0], kb == kbs[-1], kb - (qb0 - 16), j))
                for n_, (u0, hg_, hh_) in enumerate(head_starts):
                    loads.setdefault(0 if n_ == 0 else head_starts[n_ - 1][0], []).append((hg_, hh_))
                cur = {}
                hstate = {}
                sinfo = {}
                deferred = []
                state = {"vg": None}

                def issue_loads(hg, hh):
                    h = hg * 4 + hh
                    if hh == 0:
                        vg, vgres = vr.next()
                        nvp = max(1, NCB // 16)
                        for vp in range(nvp):
                            nb_ = NCB // nvp
                            P.op("act", lambda e, vg=vg, hg=hg, vp=vp, nb_=nb_: e.dma_start(
                                out=vg[:, vp * nb_:(vp + 1) * nb_, :],
                                in_=AP(vsrc, hg * 4 * 65 + vp * nb_ * NH * 65,
                                       [[NCB * NH * 65, 128], [NH * 65, nb_], [1, 4 * 65]])),
                                writes=[vgres], dma=(kind + "v", vgres[1], vp))
                        state["vg"] = (vg, vgres)
                    qT, qres = qr.next()
                    kT, kres = kr.next()
                    P.op("sp", lambda e, qT=qT, h=h: e.dma_start(
                        out=qT[0:KR, :], in_=AP(qsrc, h * KR * TO, [[TO, KR], [1, TO]])),
                        writes=[qres], dma=(kind + "q", qres[1]))
                    P.op("sp", lambda e, kT=kT, h=h: e.dma_start(
                        out=kT[0:KR, :], in_=AP(ksrc, h * KR * TC, [[TC, KR], [1, TC]])),
                        writes=[kres], dma=(kind + "k", kres[1]))
                    hstate[h] = (qT, qres, kT, kres, state["vg"][0], state["vg"][1])

                def emit_S(ui):
                    (h, hh, g, kb, q0, first, last, ko, j) = units[ui]
                    for (hg_, hh_) in loads.get(ui, []):
                        issue_loads(hg_, hh_)
                    qT, qres, kT, kres, vg, vgres = hstate[h]
                    S, sres = sr.next()
                    sinfo[ui] = (S, sres)
                    P.op("pe", lambda e, S=S, kT=kT, qT=qT, kb=kb, g=g, q0=q0: e.matmul(
                        S[:, q0:512], lhsT=kT[:, kb * 128:(kb + 1) * 128],
                        rhs=qT[:, g * 512 + q0:(g + 1) * 512], start=True, stop=True),
                        reads=[kres, qres], writes=[sres])

                def finalize2(osb, osres, g, h):
                    to, tores = tor.next()
                    for jj in range(4):
                        P.op("pe", lambda e, to=to, osb=osb, jj=jj: e.transpose(
                            out=to[:, jj, 0:65], in_=osb[:, jj * 128:(jj + 1) * 128], identity=ident_f[0:65, 0:65]),
                            reads=[osres, "ident_f"], writes=[tores])
                    rc, rcres = rcr.next()
                    of, ofres = ofr.next()
                    P.op("dve", lambda e, rc=rc, to=to: e.reciprocal(out=rc[:], in_=to[:, :, 64]),
                         reads=[tores], writes=[rcres])
                    P.op("dve", lambda e, rc=rc, to=to, of=of: e.tensor_tensor(
                        out=of[:], in0=to[:, :, 0:64], in1=AP(rc, 0, [[4, 128], [1, 4], [0, 64]]), op=ALU.mult),
                        reads=[tores, rcres], writes=[ofres])
                    P.op("sp", lambda e, of=of, g=g, h=h: e.dma_start(
                        out=AP(attn_o, g * 512 * D + coff + h * 64, [[D, 128], [128 * D, 4], [1, 64]]), in_=of[:]),
                        reads=[ofres], writes=[("attn_o", g)], dma=(kind + "o", ofres[1]))

                def emit_rest(ui):
                    (h, hh, g, kb, q0, first, last, ko, j) = units[ui]
                    qT, qres, kT, kres, vg, vgres = hstate[h]
                    S, sres = sinfo.pop(ui)
                    if first:
                        cur["ot"] = otr.next()
                    ot, otres = cur["ot"]
                    Pt, ptres = ptr_.next()
                    if kind == "fox":
                        P.op("act", lambda e, Pt=Pt, S=S, q0=q0, kb=kb, h=h: e.activation(
                            out=Pt[:, q0:512], in_=S[:, q0:512], func=AF.Exp, bias=negc_tm[:, kb, h:h + 1]),
                            reads=[sres, "negc_tm"], writes=[ptres])
                        if j >= 0:
                            P.op("dve", lambda e, Pt=Pt, q0=q0: e.tensor_tensor(
                                out=Pt[:, q0:q0 + 128], in0=Pt[:, q0:q0 + 128], in1=tri_b[:], op=ALU.mult),
                                reads=[ptres, "tri_b"], writes=[ptres])
                    else:
                        P.op("act", lambda e, Pt=Pt, S=S: e.activation(out=Pt[:], in_=S[:], func=AF.Exp),
                             reads=[sres], writes=[ptres])
                        P.op("dve", lambda e, Pt=Pt, ko=ko: e.tensor_tensor(
                            out=Pt[:], in0=Pt[:], in1=wcat_b[:, ko, :], op=ALU.mult),
                            reads=[ptres, "wcat_b"], writes=[ptres])
                    P.op("pe", lambda e, ot=ot, vg=vg, Pt=Pt, kb=kb, hh=hh, q0=q0, first=first, last=last: e.matmul(
                        ot[0:65, q0:512], lhsT=vg[:, kb, hh * 65:(hh + 1) * 65], rhs=Pt[:, q0:512],
                        start=first, stop=last),
                        reads=[vgres, ptres], writes=[otres])
                    if last:
                        osb, osres = osr.next()
                        P.op("act", lambda e, osb=osb, ot=ot: e.activation(out=osb[:], in_=ot[0:65, :], func=AF.Copy),
                             reads=[otres], writes=[osres])
                        deferred.append((ui + 3, lambda osb=osb, osres=osres, g=g, h=h: finalize2(osb, osres, g, h)))

                NU = len(units)
                conv = []
                if kind == "fox":
                    cvi = sb(st, "cvi", [128, 2 * D], F32)
                    cvo = sb(st, "cvo", [128, 2 * D], BF16)
                    gfc = sb(st, "gfc", [128, D], F32)
                    P.op("pool", lambda e: e.dma_start(out=gfc[:], in_=g_ffn.ap()), writes=["gfc"], dma="gfc")
                    for (src_t, tbl) in ((p_down, 0), (p_up, 1)):
                        for ch in range(NEXP // 256):
                            conv.append((src_t, tbl, ch))

                def conv_chunk(src_t, tbl, ch):
                    P.op("pool", lambda e: e.dma_start(
                        out=cvi[:], in_=AP(src_t, ch * 256 * D, [[2 * D, 128], [1, 2 * D]])),
                        writes=["cvi"], dma="cvl")
                    if tbl == 0:
                        P.op("pool", lambda e: e.tensor_tensor(
                            out=cvo[:].rearrange("p (j d) -> p j d", j=2), in0=cvi[:].rearrange("p (j d) -> p j d", j=2),
                            in1=AP(gfc, 0, [[D, 128], [0, 2], [1, D]]), op=ALU.mult),
                            reads=["cvi", "gfc"], writes=["cvo"])
                    else:
                        P.op("pool", lambda e: e.tensor_copy(out=cvo[:], in_=cvi[:]), reads=["cvi"], writes=["cvo"])
                    P.op("pool", lambda e: e.dma_start(
                        out=AP(ctab, ch * 256 * 2 * D + tbl * D, [[4 * D, 128], [2 * D, 2], [1, D]]),
                        in_=cvo[:].rearrange("p (j d) -> p j d", j=2)),
                        reads=["cvo"], writes=[("cv", tbl, ch)], dma="cvs")

                cstep = max(1, (NU - 8) // max(1, len(conv)))
                for idx in range(NU + LA):
                    if conv and idx % cstep == 0:
                        conv_chunk(*conv.pop(0))
                    if idx < NU:
                        emit_S(idx)
                    jdx = idx - LA
                    if jdx >= 0:
                        emit_rest(jdx)
                        while deferred and deferred[0][0] <= jdx:
                            deferred.pop(0)[1]()
                while deferred:
                    deferred.pop(0)[1]()
                while conv:
                    conv_chunk(*conv.pop(0))
                P.barrier()
                P.emit()
                P.new_phase()
        if upto <= 2:
            return nc

        with ExitStack() as st:
            wst = Ring(nc, st, "p4ws", 2, [128, D], F32)
            wo_b = sb(st, "p4wo", [128, NCH, D], BF16)
            gmx = sb(st, "p4gm", [128, NCH], F32)
            atr = Ring(nc, st, "p4at", 2, [128, D], F32)
            xr = Ring(nc, st, "p4x", 2, [128, D], F32)
            jr = Ring(nc, st, "p4j", 1, [128, 1024], BF16)
            mxr = Ring(nc, st, "p4mx", 2, [128, D], BF16)
            mtr = Ring(nc, st, "p4mt", 2, [128, NCH, 128], BF16)
            htr = Ring(nc, st, "p4ht", 2, [128, D], F32)
            tpr = Ring(nc, st, "p4tp", 2, [128, 1024], BF16, psum=True)
            psr = Ring(nc, st, "p4ps", 4, [128, 512], F32, psum=True)
            ss = sb(st, "p4ss", [128, 2 * NOB], F32)
            rs = sb(st, "p4rs", [128, 2 * NOB], F32)
            rstd = sb(st, "p4rstd", [128, 2 * NOB], F32)
            P.op("sp", lambda e: e.dma_start(out=gmx[:], in_=g_mix.ap()), writes=["gmx"], dma="c0")
            for c in range(NCH):
                ws, wsres = wst.next()
                P.op("sp" if c % 2 == 0 else "act", lambda e, ws=ws, c=c: e.dma_start(
                    out=ws[:], in_=w_out[c * 128:(c + 1) * 128, :]), writes=[wsres], dma=("p4w", wsres[1]))
                if c % 2 == 0:
                    P.op("act", lambda e, ws=ws, c=c: e.activation(out=wo_b[:, c, :], in_=ws[:], func=AF.Copy,
                                                                  scale=gmx[:, c:c + 1]),
                         reads=[wsres, "gmx"], writes=["wo_b"])
                else:
                    P.op("dve", lambda e, ws=ws, c=c: e.tensor_scalar(
                        out=wo_b[:, c, :], in0=ws[:], scalar1=gmx[:, c:c + 1], scalar2=None, op0=ALU.mult),
                        reads=[wsres, "gmx"], writes=["wo_b"])
            for i in range(NOB):
                at, atres = atr.next()
                xt, xres = xr.next()
                P.op("sp", lambda e, at=at, i=i: e.dma_start(out=at[:], in_=attn_o[i * 128:(i + 1) * 128, :]),
                     reads=[("attn_o", i // 4)], writes=[atres], dma=("p4a", atres[1]))
                P.op("act", lambda e, xt=xt, i=i: e.dma_start(out=xt[:], in_=x_ctx[(NPB + i) * 128:(NPB + i + 1) * 128, :]),
                     writes=[xres], dma=("p4x", xres[1]))
                mx, mxres = mxr.next()
                for k in range(2):
                    col = 2 * i + k
                    jt, jres = jr.next()
                    P.op("act", lambda e, jt=jt, at=at, k=k, col=col: e.activation(
                        out=jt[:], in_=at[:, k * 1024:(k + 1) * 1024], func=AF.Square, accum_out=ss[:, col:col + 1]),
                        reads=[atres], writes=[jres, ("p4ss", col)])
                    P.op("dve", lambda e, col=col: e.tensor_scalar(out=rs[:, col:col + 1], in0=ss[:, col:col + 1],
                                                                 scalar1=1.0 / 1024, scalar2=RMS_EPS, op0=ALU.mult, op1=ALU.add),
                         reads=[("p4ss", col)], writes=[("p4rs", col)])
                    P.op("act", lambda e, col=col: e.activation(out=rs[:, col:col + 1], in_=rs[:, col:col + 1], func=AF.Sqrt),
                         reads=[("p4rs", col)], writes=[("p4rs", col)])
                    P.op("dve", lambda e, col=col: e.reciprocal(out=rstd[:, col:col + 1], in_=rs[:, col:col + 1]),
                         reads=[("p4rs", col)], writes=[("p4rstd", col)])
                    P.op("dve", lambda e, mx=mx, at=at, k=k, col=col: e.tensor_scalar(
                        out=mx[:, k * 1024:(k + 1) * 1024], in0=at[:, k * 1024:(k + 1) * 1024],
                        scalar1=rstd[:, col:col + 1], scalar2=None, op0=ALU.mult),
                        reads=[atres, ("p4rstd", col)], writes=[mxres])
                mt, mtres = mtr.next()
                for half in range(NCH // 8):
                    tp, tpres = tpr.next()
                    for j in range(8):
                        c = half * 8 + j
                        P.op("pe", lambda e, tp=tp, mx=mx, j=j, c=c: e.transpose(
                            out=tp[:, j * 128:(j + 1) * 128], in_=mx[:, c * 128:(c + 1) * 128], identity=ident_b[:]),
                            reads=[mxres, "ident_b"], writes=[tpres])
                    P.op("act", lambda e, mt=mt, tp=tp, half=half: e.activation(
                        out=mt[:, half * 8:(half + 1) * 8, :], in_=tp[:].rearrange("p (c t) -> p c t", c=8), func=AF.Copy),
                        reads=[tpres], writes=[mtres])
                ht, htres = htr.next()
                for n in range(D // 512):
                    ps, psres = psr.next()
                    for c in range(NCH):
                        P.op("pe", lambda e, ps=ps, mt=mt, c=c, n=n: e.matmul(
                            ps[:], lhsT=mt[:, c, :], rhs=wo_b[:, c, n * 512:(n + 1) * 512], start=(c == 0),
                            stop=(c == NCH - 1)), reads=[mtres, "wo_b"], writes=[psres])
                    P.op("dve", lambda e, ht=ht, xt=xt, ps=ps, n=n: e.tensor_tensor(
                        out=ht[:, n * 512:(n + 1) * 512], in0=xt[:, n * 512:(n + 1) * 512], in1=ps[:], op=ALU.add),
                        reads=[xres, psres], writes=[htres])
                P.op("sp", lambda e, ht=ht, i=i: e.dma_start(out=hbuf[i * 128:(i + 1) * 128, :], in_=ht[:]),
                     reads=[htres], writes=[("hbuf", i)], dma=("p4h", htres[1]))
            P.barrier()
            P.emit()
            P.new_phase()
        if upto <= 3:
            return nc

        with ExitStack() as st:
            NS = 128
            wq_b = sb(st, "p5wq", [128, NCH, 2048], BF16)
            sk_b = sb(st, "p5sk", [128, 16, 128], BF16)
            gfp = sb(st, "p5gfp", [128, NCH], F32)
            gfin = sb(st, "p5gn", [128, D], F32)
            io_i = sb(st, "p5ioi", [128, 16], I32)
            io_f = sb(st, "p5iof", [128, 16], F32)

            for (t_, src, key) in ((gfp, g_ffn_pc, "gfp"), (gfin, g_fin, "gfin")):
                P.op("sp", lambda e, t_=t_, src=src: e.dma_start(out=t_[:], in_=src.ap()), writes=[key], dma="c0")
            P.op("pool", lambda e: e.iota(io_i[:], [[1, 16]], base=0, channel_multiplier=0), writes=["io_i"])
            P.op("dve", lambda e: e.tensor_copy(out=io_f[:], in_=io_i[:]), reads=["io_i"], writes=["io_f"])
            st2 = ExitStack()
            wst = Ring(nc, st2, "p5ws", 2, [128, D], F32)
            for c in range(NCH):
                ws, wsres = wst.next()
                P.op("sp" if c % 2 == 0 else "act", lambda e, ws=ws, c=c: e.dma_start(
                    out=ws[:], in_=w_pq[c * 128:(c + 1) * 128, :]), writes=[wsres], dma=("p5w", wsres[1]))
                if c % 2 == 0:
                    P.op("act", lambda e, ws=ws, c=c: e.activation(out=wq_b[:, c, :], in_=ws[:], func=AF.Copy,
                                                                  scale=gfp[:, c:c + 1]),
                         reads=[wsres, "gfp"], writes=["wq_b"])
                else:
                    P.op("dve", lambda e, ws=ws, c=c: e.tensor_scalar(out=wq_b[:, c, :], in0=ws[:], scalar1=gfp[:, c:c + 1],
                                                                     scalar2=None, op0=ALU.mult),
                         reads=[wsres, "gfp"], writes=["wq_b"])
            ws, wsres = wst.next()
            P.op("sp", lambda e, ws=ws: e.dma_start(out=ws[:], in_=skT_in.ap()), writes=[wsres], dma=("p5w", wsres[1]))
            P.op("dve", lambda e, ws=ws: e.tensor_copy(out=sk_b[:], in_=ws[:].rearrange("p (g n) -> p g n", g=16)),
                 reads=[wsres], writes=["sk_b"])
            P.barrier()
            P.emit()
            P.new_phase()
            st2.close()
            hts = [sb(st, "p5ht0", [128, D], F32)] * 2
            hnbs = [sb(st, "p5hnb%d" % k_, [128, D], BF16) for k_ in range(2)]
            hnT = sb(st, "p5hnT", [128, NCH, 128], BF16)
            qTs = sb(st, "p5qT", [128, 4, 128], BF16)
            scr_ = Ring(nc, st, "p5sc", 2, [128, 512], F32)
            scx = sb(st, "p5scx", [128, 128], F32)
            v16 = sb(st, "p5v16", [128, 16, 16], F32)
            i16u = sb(st, "p5i16u", [128, 16, 16], U32)
            i16f = sb(st, "p5i16f", [128, 16, 16], F32)
            work = sb(st, "p5work", [128, 2048], F32)
            cand = work[:].rearrange("p (h c) -> p h c", h=8)
            cscr = sb(st, "p5cscr", [128, 256], F32)
            tv = sb(st, "p5tv", [128, 8, 16], F32)
            tcu = sb(st, "p5tcu", [128, 8, 16], U32)
            abu = sb(st, "p5abu", [128, 2, 128], U32)
            abf = sb(st, "p5abf", [128, 2, 128], F32)
            eq = work[:].rearrange("p (h k a) -> p h k a", h=8, k=16)
            i12 = sb(st, "p5i12", [128, 2, 128], F32)
            eidf = sb(st, "p5eidf", [128, NS], F32)
            eidxs = [sb(st, "p5eidx%d" % k_, [128, NS], I32) for k_ in range(2)]
            gtss = [sb(st, "p5gts%d" % k_, [128, 8, 16], F32) for k_ in range(2)]
            gsum = sb(st, "p5gsum", [128, 8], F32)
            hdn = sb(st, "p5hdn", [128, NS], F32)
            zc = sb(st, "p5zc", [128, NS], F32)
            ur = Ring(nc, st, "p5u", 8, [128, 2 * D], BF16)
            junk = sb(st, "p5junk", [128, D], BF16)
            dgr = Ring(nc, st, "p5dg", 4, [128, 128], BF16)
            ot_ = work
            ss = sb(st, "p5ss", [128, 4], F32)
            tpr = Ring(nc, st, "p5tp", 1, [128, 1024], BF16, psum=True)
            mmr = Ring(nc, st, "p5mm", 3, [128, 512], F32, psum=True)
            ybk = [st.enter_context(nc.psum_tensor("p5y%d" % n, [128, 512], F32)) for n in range(4)]

            def rms_rstd(src_ap, col, key):
                jt = junk
                P.op("act", lambda e: e.activation(out=jt[:], in_=src_ap, func=AF.Square, accum_out=ss[:, col:col + 1]),
                     reads=[key], writes=[("p5ss", col)])
                P.op("dve", lambda e: e.tensor_scalar(out=ss[:, col:col + 1], in0=ss[:, col:col + 1], scalar1=1.0 / D,
                                                      scalar2=RMS_EPS, op0=ALU.mult, op1=ALU.add),
                     reads=[("p5ss", col)], writes=[("p5ss", col)])
                P.op("act", lambda e: e.activation(out=ss[:, col:col + 1], in_=ss[:, col:col + 1], func=AF.Sqrt),
                     reads=[("p5ss", col)], writes=[("p5ss", col)])
                P.op("dve", lambda e: e.reciprocal(out=ss[:, col + 2:col + 3], in_=ss[:, col:col + 1]),
                     reads=[("p5ss", col)], writes=[("p5ss", col + 2)])
                return ss[:, col + 2:col + 3], ("p5ss", col + 2)

            def top16(src_ap, scratch_ap, vout, iout, rkeys, skey, wkeys):
                P.op("dve", lambda e: e.max(out=vout[0], in_=src_ap), reads=rkeys, writes=[wkeys[0]])
                P.op("dve", lambda e: e.max_index(out=iout[0], in_max=vout[0], in_values=src_ap),
                     reads=rkeys + [wkeys[0]], writes=[wkeys[1]])
                P.op("dve", lambda e: e.match_replace(out=scratch_ap, in_to_replace=vout[0], in_values=src_ap,
                                                      imm_value=-1e30), reads=rkeys + [wkeys[0]], writes=[skey])
                P.op("dve", lambda e: e.max(out=vout[1], in_=scratch_ap), reads=[skey], writes=[wkeys[0]])
                P.op("dve", lambda e: e.max_index(out=iout[1], in_max=vout[1], in_values=scratch_ap),
                     reads=[skey, wkeys[0]], writes=[wkeys[1]])

            def tile_4b(i):
                ht, eidx, hnb, gts = hts[i % 2], eidxs[i % 2], hnbs[i % 2], gtss[i % 2]
                P.op("sp", lambda e, i=i: e.dma_start(out=ht[:], in_=hbuf[i * 128:(i + 1) * 128, :]),
                     reads=[("hbuf", i)], writes=["ht"], dma="p5h")
                rstd_ap, rkey = rms_rstd(ht[:], 0, "ht")
                P.op("dve", lambda e, rstd_ap=rstd_ap: e.tensor_scalar(
                    out=hnb[:], in0=ht[:], scalar1=rstd_ap, scalar2=None, op0=ALU.mult),
                    reads=["ht", rkey], writes=[("hnb", i % 2)])
                for half in range(NCH // 8):
                    tp, tpres = tpr.next()
                    for j in range(8):
                        c = half * 8 + j
                        P.op("pe", lambda e, tp=tp, j=j, c=c: e.transpose(
                            out=tp[:, j * 128:(j + 1) * 128], in_=hnb[:, c * 128:(c + 1) * 128], identity=ident_b[:]),
                            reads=[("hnb", i % 2), "ident_b"], writes=[tpres])
                    P.op("act", lambda e, tp=tp, half=half: e.activation(
                        out=hnT[:, half * 8:(half + 1) * 8, :], in_=tp[:].rearrange("p (c t) -> p c t", c=8), func=AF.Copy),
                        reads=[tpres], writes=["hnT"])
                for qq in range(4):
                    mm, mmres = mmr.next()
                    for gi in range(4):
                        G = 4 * qq + gi
                        for c in range(NCH):
                            P.op("pe", lambda e, mm=mm, gi=gi, G=G, c=c: e.matmul(
                                mm[:, gi * 128:(gi + 1) * 128], lhsT=wq_b[:, c, G * 128:(G + 1) * 128], rhs=hnT[:, c, :],
                                start=(c == 0), stop=(c == NCH - 1)), reads=["wq_b", "hnT"], writes=[mmres])
                    P.op("act", lambda e, mm=mm: e.activation(out=qTs[:], in_=mm[:].rearrange("p (g t) -> p g t", g=4),
                                                              func=AF.Copy), reads=[mmres], writes=["qTs"])
                    mm2, mm2res = mmr.next()
                    for gi in range(4):
                        G = 4 * qq + gi
                        P.op("pe", lambda e, mm2=mm2, gi=gi, G=G: e.matmul(
                            mm2[:, gi * 128:(gi + 1) * 128], lhsT=qTs[:, gi, :], rhs=sk_b[:, G, :], start=True, stop=True),
                            reads=["qTs", "sk_b"], writes=[mm2res])
                    sc, scres = scr_.next()
                    P.op("act", lambda e, sc=sc, mm2=mm2: e.activation(out=sc[:], in_=mm2[:], func=AF.Copy),
                         reads=[mm2res], writes=[scres])
                    for gi in range(4):
                        G = 4 * qq + gi
                        top16(sc[:, gi * 128:(gi + 1) * 128], scx[:], (v16[:, G, 0:8], v16[:, G, 8:16]),
                              (i16u[:, G, 0:8], i16u[:, G, 8:16]), [scres], "scx", ["v16", "i16u"])
                P.op("dve", lambda e: e.tensor_copy(out=i16f[:], in_=i16u[:]), reads=["i16u"], writes=["i16f"])
                P.op("dve", lambda e: e.tensor_tensor(
                    out=cand.rearrange("p h (a b) -> p h a b", a=16),
                    in0=AP(v16, 0, [[256, 128], [32, 8], [1, 16], [0, 16]]),
                    in1=AP(v16, 16, [[256, 128], [32, 8], [0, 16], [1, 16]]), op=ALU.add),
                    reads=["v16"], writes=["work"])
                for h in range(8):
                    top16(cand[:, h, :], cscr[:], (tv[:, h, 0:8], tv[:, h, 8:16]), (tcu[:, h, 0:8], tcu[:, h, 8:16]),
                          ["work"], "cscr", ["tv", "tcu"])
                P.op("dve", lambda e: e.tensor_tensor(out=gts[:], in0=tv[:], in1=AP(tv, 0, [[128, 128], [16, 8], [0, 16]]),
                                                      op=ALU.subtract), reads=["tv"], writes=[("gts", i % 2)])
                P.op("act", lambda e: e.activation(out=gts[:], in_=gts[:], func=AF.Exp), reads=[("gts", i % 2)], writes=[("gts", i % 2)])
                P.op("dve", lambda e: e.tensor_reduce(out=gsum[:], in_=gts[:], axis=AX.X, op=ALU.add),
                     reads=[("gts", i % 2)], writes=["gsum"])
                P.op("dve", lambda e: e.reciprocal(out=gsum[:], in_=gsum[:]), reads=["gsum"], writes=["gsum"])
                P.op("dve", lambda e: e.tensor_tensor(out=gts[:], in0=gts[:], in1=AP(gsum, 0, [[8, 128], [1, 8], [0, 16]]),
                                                      op=ALU.mult), reads=[("gts", i % 2), "gsum"], writes=[("gts", i % 2)])
                tcu2 = tcu[:].rearrange("p h k -> p (h k)")
                P.op("dve", lambda e: e.tensor_scalar(out=abu[:, 0, :], in0=tcu2, scalar1=4, scalar2=None,
                                                      op0=ALU.logical_shift_right), reads=["tcu"], writes=["abu"])
                P.op("dve", lambda e: e.tensor_scalar(out=abu[:, 1, :], in0=tcu2, scalar1=15, scalar2=None,
                                                      op0=ALU.bitwise_and), reads=["tcu"], writes=["abu"])
                P.op("dve", lambda e: e.tensor_copy(out=abf[:], in_=abu[:]), reads=["abu"], writes=["abf"])
                for w in range(2):
                    P.op("dve", lambda e, w=w: e.tensor_tensor(
                        out=eq, in0=AP(abf, w * 128, [[256, 128], [16, 8], [1, 16], [0, 16]]),
                        in1=AP(io_f, 0, [[16, 128], [0, 8], [0, 16], [1, 16]]), op=ALU.is_equal),
                        reads=["abf", "io_f"], writes=["work"])
                    P.op("dve", lambda e, w=w: e.tensor_tensor(
                        out=eq, in0=eq, in1=AP(i16f, w * 16, [[256, 128], [32, 8], [0, 16], [1, 16]]), op=ALU.mult),
                        reads=["work", "i16f"], writes=["work"])
                    P.op("dve", lambda e, w=w: e.tensor_reduce(
                        out=i12[:, w, :], in_=eq.rearrange("p h k a -> p (h k) a"), axis=AX.X, op=ALU.add),
                        reads=["work"], writes=["i12"])
                P.op("dve", lambda e: e.scalar_tensor_tensor(out=eidf[:], in0=i12[:, 0, :], scalar=128.0, in1=i12[:, 1, :],
                                                             op0=ALU.mult, op1=ALU.add), reads=["i12"], writes=["eidf"])
                P.op("dve", lambda e: e.tensor_copy(out=eidx[:], in_=eidf[:]), reads=["eidf"], writes=[("eidx", i % 2)])
            slot_st = {}

            def slot_a(i, s_):
                eidx, hnb = eidxs[i % 2], hnbs[i % 2]
                u, ures = ur.next()
                slot_st[(i, s_)] = (u, ures)
                P.op("pool", lambda e, u=u: e.indirect_dma_start(
                    out=u[:], out_offset=None, in_=ctab.ap(),
                    in_offset=bass.IndirectOffsetOnAxis(ap=eidx[:, s_:s_ + 1], axis=0)),
                    reads=[("eidx", i % 2)], writes=[ures], dma=("p5g", ures[1], i % 2))
                P.op("dve", lambda e, u=u: e.scalar_tensor_tensor(
                    out=junk[:], in0=u[:, 0:D], scalar=1.0, in1=hnb[:], op0=ALU.mult, op1=ALU.mult,
                    accum_out=hdn[:, s_:s_ + 1]), reads=[ures, ("hnb", i % 2)], writes=[("hdn", s_)])
                P.op("act", lambda e: e.activation(out=zc[:, s_:s_ + 1], in_=hdn[:, s_:s_ + 1], func=AF.Gelu_apprx_tanh),
                     reads=[("hdn", s_)], writes=[("zc", s_)])

            def slot_b(i, s_):
                gts = gtss[i % 2]
                u, ures = slot_st.pop((i, s_))
                dg, dgres = dgr.next()
                P.op("dve", lambda e, dg=dg: e.tensor_scalar(
                    out=dg[:], in0=ident_f[:], scalar1=zc[:, s_:s_ + 1],
                    scalar2=gts[:].rearrange("p h k -> p (h k)")[:, s_:s_ + 1], op0=ALU.mult, op1=ALU.mult),
                    reads=["ident_f", ("zc", s_), ("gts", i % 2)], writes=[dgres])
                for n in range(4):
                    P.op("pe", lambda e, dg=dg, u=u, n=n: e.matmul(
                        ybk[n][:], lhsT=dg[:], rhs=u[:, D + n * 512:D + (n + 1) * 512], start=(s_ == 0),
                        stop=(s_ == NS - 1)), reads=[dgres, ures], writes=[("y", n)])

            def final(i):
                P.op("sp", lambda e, i=i: e.dma_start(out=ot_[:], in_=hbuf[i * 128:(i + 1) * 128, :]),
                     writes=["work"], dma="p5h2")
                for n in range(4):
                    P.op("dve", lambda e, n=n: e.tensor_tensor(out=ot_[:, n * 512:(n + 1) * 512],
                                                                in0=ot_[:, n * 512:(n + 1) * 512], in1=ybk[n][:], op=ALU.add),
                         reads=["work", ("y", n)], writes=["work"])
                rstd_ap, rkey = rms_rstd(ot_[:], 1, "work")
                P.op("dve", lambda e, rstd_ap=rstd_ap: e.scalar_tensor_tensor(
                    out=ot_[:], in0=ot_[:], scalar=rstd_ap, in1=gfin[:], op0=ALU.mult, op1=ALU.mult),
                    reads=["work", rkey, "gfin"], writes=["work"])
                P.op("sp", lambda e, i=i: e.dma_start(out=out[i * 128:(i + 1) * 128, :], in_=ot_[:]),
                     reads=["work"], writes=[("out", i)], dma="p5o")

            def cap4b(i):
                if i >= NOB:
                    return []
                P.capture()
                tile_4b(i)
                return P.end_capture()

            tile_4b(0)
            for i in range(NOB):
                inter = cap4b(i + 1)
                per = (len(inter) + NS - 1) // NS
                LAG = 2
                for s_ in range(NS + LAG):
                    if s_ < NS:
                        slot_a(i, s_)
                    if s_ >= LAG:
                        slot_b(i, s_ - LAG)
                    if s_ < NS:
                        P.replay(inter[s_ * per:(s_ + 1) * per])
                final(i)
            P.barrier()
            P.emit()
            P.finish()
    return nc


def _const_tables(cfg, p):
    NCB, NPB = cfg.NCB, cfg.NPB
    TP = NPB * 128
    idx = np.arange(cfg.TC)
    pos = idx - (0 if p == 1 else TP)
    valid = (pos >= 0).astype(np.float32)
    kvalid = np.ascontiguousarray(valid.reshape(NCB, 128).T)
    inv = (500000.0 ** (-np.arange(0, 16, 2, dtype=np.float32) / 16.0)).astype(np.float32)
    ang = np.maximum(pos, 0).astype(np.float32)[:, None] * inv[None, :]
    cos = np.cos(ang).astype(np.float32).reshape(NCB, 128, 8).transpose(1, 0, 2).reshape(128, NCB * 8)
    sin = np.sin(ang).astype(np.float32).reshape(NCB, 128, 8).transpose(1, 0, 2).reshape(128, NCB * 8)
    kl = np.arange(128)[:, None, None]
    ko = np.arange(20)[None, :, None]
    ql = np.arange(512)[None, None, :]
    delta = (16 - ko) * 128 + ql - kl
    w = ((delta >= 0) & (delta <= 128)).astype(np.float32)
    w += ((delta >= 0) & (delta <= 512) & (delta % 4 == 0)).astype(np.float32)
    w += ((delta >= 0) & (delta <= 2048) & (delta % 16 == 0)).astype(np.float32)
    wcat = np.ascontiguousarray(w.reshape(128, 20 * 512))
    tri = (np.arange(128)[:, None] <= np.arange(128)[None, :]).astype(np.float32)
    return dict(kvalid=kvalid, rope_cos=np.ascontiguousarray(cos), rope_sin=np.ascontiguousarray(sin),
                wcat=wcat, tri=tri, ident=np.eye(128, dtype=np.float32))


def _pc(v, nch):
    return np.ascontiguousarray(np.asarray(v, np.float32).reshape(nch, 128).T)


def shared_inputs(cfg, attn_norm_gain, w_in, forget_bias, fox_out_gain, dil_out_gain, w_out,
                  ffn_norm_gain, peer_query, peer_sub_keys, peer_down, peer_up, final_norm_gain):
    f = lambda a: np.ascontiguousarray(np.asarray(a, np.float32))
    NCH = cfg.NCH
    skT = np.asarray(peer_sub_keys[0], np.float32).reshape(16, 128, 128).transpose(2, 0, 1)
    return dict(
        w_in=f(w_in[0]), g_attn=_pc(attn_norm_gain[0], NCH), fbias=f(forget_bias[0]).reshape(16, 1),
        g_mix=_pc(np.concatenate([np.asarray(fox_out_gain[0]), np.asarray(dil_out_gain[0])]), NCH),
        w_out=f(w_out[0]), g_ffn_pc=_pc(ffn_norm_gain[0], NCH), g_ffn=np.ascontiguousarray(np.broadcast_to(f(ffn_norm_gain[0])[None, :], (128, cfg.D))),
        w_pq=f(peer_query[0]), skT=np.ascontiguousarray(skT.reshape(128, 16 * 128)),
        p_down=f(peer_down[0]), p_up=f(peer_up[0]),
        g_fin=np.ascontiguousarray(np.broadcast_to(f(final_norm_gain)[None, :], (128, cfg.D))))


def core_inputs(cfg, xb, p, shared, tables):
    TP, TO = cfg.NPB * 128, cfg.TO
    if p == 1:
        x_ctx = np.ascontiguousarray(xb[0:TP + TO])
    else:
        x_ctx = np.concatenate([np.zeros((TP, cfg.D), np.float32), xb[0:TO]], axis=0)
    d = dict(shared)
    d.update(tables[p])
    d["x_ctx"] = x_ctx
    return d


def kernel(**inputs):
    cfg = Cfg(32, 32)
    x = np.asarray(inputs["x"], np.float32)
    B, S, D = x.shape
    sh = shared_inputs(cfg, **{k: np.asarray(v) for k, v in inputs.items() if k != "x"})
    tables = [_const_tables(cfg, 0), _const_tables(cfg, 1)]
    in_maps = [core_inputs(cfg, x[c // 2], c % 2, sh, tables) for c in range(2 * B)]
    nc = build(cfg)
    res = run_bass_kernel_spmd(nc, in_maps, core_ids=list(range(2 * B)))
    outp = np.empty((B, S, D), np.float32)
    for c in range(2 * B):
        outp[c // 2, (c % 2) * cfg.TO:(c % 2 + 1) * cfg.TO] = np.asarray(res.results[c]["out"], np.float32)
    return outp
```
